# Optimizing a Trainium2 kernel written in Bass

```python
import jax, jax.numpy as jnp
from jax import lax
import numpy as np


D_MODEL = 2048
BATCH = 2
SEQ = 8192
DEPTH = 4

CTX_LEN = 256
GRID_W = 64
N_MIXERS = 3
MIXER_NA = 0
MIXER_GMLP = 1
MIXER_CONV = 2
NA_HEADS = 16
HEAD_DIM = D_MODEL // NA_HEADS
WIN_H = 8
WIN_W = 16
QROWS = 2
GM_WIDTH = 2 * D_MODEL
GM_GROUPS = 16
GM_GROUP_DIM = GM_WIDTH // GM_GROUPS
GM_CHUNK = 128
FFN_DIM = 5504
N_MOD = 6
EPS = 1e-6

kernel_name = 'hybrid_na_gmlp_shortconv_dit_block'


def rms_norm(x, g):
    x32 = x.astype(jnp.float32)
    y = x32 * lax.rsqrt(jnp.mean(x32 * x32, axis=-1, keepdims=True) + EPS)
    return (y * g.astype(jnp.float32)).astype(x.dtype)


def adaln(cond, w, b):
    return jnp.split(jax.nn.silu(cond) @ w + b, N_MOD, axis=-1)


def modulate(x, shift, scale):
    return x * (1.0 + scale) + shift


def dwconv3(z, w, b=None):
    zp = jnp.pad(z, ((0, 0), (1, 1), (0, 0)))
    y = zp[:, :-2] * w[0] + zp[:, 1:-1] * w[1] + zp[:, 2:] * w[2]
    return y if b is None else y + b


def neighbourhood_attention(h, hc, w_qkv, q_g, k_g, rpb, w_o, ctx_out):
    bn, n, d = h.shape
    rows = n // GRID_W
    kh = min(WIN_H, rows)
    nb = min(kh + QROWS - 1, rows)
    n_blk = rows // QROWS
    qw = QROWS * GRID_W
    nk = nb * GRID_W
    scale = HEAD_DIM ** -0.5

    def project(t):
        q, k, v = jnp.split(t @ w_qkv, 3, axis=-1)
        sh = (t.shape[0], t.shape[1], NA_HEADS, HEAD_DIM)
        return rms_norm(q.reshape(sh), q_g), rms_norm(k.reshape(sh), k_g), v.reshape(sh)

    q, k, v = project(h)
    qc, kc, vc = project(hc)
    kg = k.reshape(bn, rows, GRID_W, NA_HEADS, HEAD_DIM)
    vg = v.reshape(bn, rows, GRID_W, NA_HEADS, HEAD_DIM)

    q_row_off = jnp.repeat(jnp.arange(QROWS), GRID_W)
    q_col = jnp.tile(jnp.arange(GRID_W), QROWS)
    k_row_off = jnp.repeat(jnp.arange(nb), GRID_W)
    k_col = jnp.tile(jnp.arange(GRID_W), nb)
    c_start = jnp.clip(q_col - WIN_W // 2, 0, GRID_W - WIN_W)
    col_ok = (k_col[None, :] >= c_start[:, None]) & (k_col[None, :] < c_start[:, None] + WIN_W)
    dc_idx = jnp.clip(k_col[None, :] - q_col[:, None], -(WIN_W - 1), WIN_W - 1) + WIN_W - 1

    def block(args):
        blk, qb = args
        r0 = blk * QROWS
        q_row = r0 + q_row_off
        r_start = jnp.clip(q_row - kh // 2, 0, rows - kh)
        band0 = jnp.minimum(jnp.clip(r0 - kh // 2, 0, rows - kh), rows - nb)
        k_row = band0 + k_row_off
        kb = lax.dynamic_slice_in_dim(kg, band0, nb, axis=1).reshape(bn, nk, NA_HEADS, HEAD_DIM)
        vb = lax.dynamic_slice_in_dim(vg, band0, nb, axis=1).reshape(bn, nk, NA_HEADS, HEAD_DIM)
        ok = col_ok & (k_row[None, :] >= r_start[:, None]) & (k_row[None, :] < r_start[:, None] + kh)
        dr_idx = jnp.clip(k_row[None, :] - q_row[:, None], -(WIN_H - 1), WIN_H - 1) + WIN_H - 1
        bias = rpb[:, dr_idx, dc_idx].astype(jnp.float32)
        s_lat = jnp.einsum('bqhd,bkhd->bhqk', qb, kb).astype(jnp.float32) * scale + bias
        s_lat = jnp.where(ok, s_lat, -jnp.inf)
        s_ctx = jnp.einsum('bqhd,bchd->bhqc', qb, kc).astype(jnp.float32) * scale
        p = jax.nn.softmax(jnp.concatenate([s_lat, s_ctx], axis=-1), axis=-1).astype(vb.dtype)
        return (jnp.einsum('bhqk,bkhd->bqhd', p[..., :nk], vb)
                + jnp.einsum('bhqc,bchd->bqhd', p[..., nk:], vc))

    q_blocks = q.reshape(bn, n_blk, qw, NA_HEADS, HEAD_DIM).transpose(1, 0, 2, 3, 4)
    o = lax.map(block, (jnp.arange(n_blk), q_blocks))
    y = o.transpose(1, 0, 2, 3, 4).reshape(bn, n, d) @ w_o
    yc = None
    if ctx_out:
        s = jnp.einsum('bqhd,bkhd->bhqk', qc, kc).astype(jnp.float32) * scale
        p = jax.nn.softmax(s, axis=-1).astype(vc.dtype)
        yc = jnp.einsum('bhqk,bkhd->bqhd', p, vc).reshape(bn, hc.shape[1], d) @ w_o
    return y, yc


def chunk_gmlp(h, w_in, v_g, w_s, b_s, w_out):
    bn, n, _ = h.shape
    u, v = jnp.split(jax.nn.gelu(h @ w_in), 2, axis=-1)
    v = rms_norm(v, v_g)
    vch = v.reshape(bn, n // GM_CHUNK, GM_CHUNK, GM_GROUPS, GM_GROUP_DIM)
    sv = jnp.einsum('gij,bnjgd->bnigd', w_s, vch) + b_s.T[None, None, :, :, None]
    return (u * sv.reshape(bn, n, GM_WIDTH)) @ w_out


def short_conv(h, w_in, conv_w, w_out):
    b_gate, c_gate, xv = jnp.split(h @ w_in, 3, axis=-1)
    return (b_gate * dwconv3(c_gate * xv, conv_w)) @ w_out


def conv_ffn(h, w_up, conv_w, conv_b, w_down):
    gate, up = jnp.split(dwconv3(h @ w_up, conv_w, conv_b), 2, axis=-1)
    return (jax.nn.silu(gate) * up) @ w_down


def setup_inputs(seed: int = 0) -> dict:
    key = jax.random.key(seed)
    keys = iter(jax.random.split(key, 40))

    def nrm(shape, s):
        return jax.random.normal(next(keys), shape, jnp.float32) * s

    d = D_MODEL
    n_na = len(range(MIXER_NA, DEPTH, N_MIXERS))
    n_gm = len(range(MIXER_GMLP, DEPTH, N_MIXERS))
    n_sc = len(range(MIXER_CONV, DEPTH, N_MIXERS))
    return {
        'x': nrm((BATCH, SEQ, d), 1.0),
        'c': nrm((BATCH, d), 1.0),
        'ctx': nrm((BATCH, CTX_LEN, d), 1.0),
        'c_ctx': nrm((d,), 1.0),
        'norm_mix_g': 1.0 + nrm((DEPTH, d), 0.1),
        'norm_ffn_g': 1.0 + nrm((DEPTH, d), 0.1),
        'w_ada': nrm((DEPTH, d, N_MOD * d), 0.5 * d ** -0.5),
        'b_ada': nrm((DEPTH, N_MOD * d), 0.02),
        'na_w_qkv': nrm((n_na, d, 3 * d), d ** -0.5),
        'na_q_g': 1.0 + nrm((n_na, HEAD_DIM), 0.1),
        'na_k_g': 1.0 + nrm((n_na, HEAD_DIM), 0.1),
        'na_rpb': nrm((n_na, NA_HEADS, 2 * WIN_H - 1, 2 * WIN_W - 1), 0.2),
        'na_w_o': nrm((n_na, d, d), d ** -0.5),
        'gm_w_in': nrm((n_gm, d, 2 * GM_WIDTH), d ** -0.5),
        'gm_v_g': 1.0 + nrm((n_gm, GM_WIDTH), 0.1),
        'gm_w_s': nrm((n_gm, GM_GROUPS, GM_CHUNK, GM_CHUNK), GM_CHUNK ** -0.5),
        'gm_b_s': 1.0 + nrm((n_gm, GM_GROUPS, GM_CHUNK), 0.1),
        'gm_w_out': nrm((n_gm, GM_WIDTH, d), GM_WIDTH ** -0.5),
        'sc_w_in': nrm((n_sc, d, 3 * d), d ** -0.5),
        'sc_conv_w': nrm((n_sc, 3, d), 3 ** -0.5),
        'sc_w_out': nrm((n_sc, d, d), d ** -0.5),
        'ffn_w_up': nrm((DEPTH, d, 2 * FFN_DIM), d ** -0.5),
        'ffn_conv_w': nrm((DEPTH, 3, 2 * FFN_DIM), 3 ** -0.5),
        'ffn_conv_b': nrm((DEPTH, 2 * FFN_DIM), 0.02),
        'ffn_w_down': nrm((DEPTH, FFN_DIM, d), FFN_DIM ** -0.5),
    }


def reference(x, c, ctx, c_ctx, norm_mix_g, norm_ffn_g, w_ada, b_ada,
              na_w_qkv, na_q_g, na_k_g, na_rpb, na_w_o,
              gm_w_in, gm_v_g, gm_w_s, gm_b_s, gm_w_out,
              sc_w_in, sc_conv_w, sc_w_out,
              ffn_w_up, ffn_conv_w, ffn_conv_b, ffn_w_down):
    cond_lat = c[:, None, :]
    cond_ctx = c_ctx[None, None, :]
    for i in range(DEPTH):
        last = i == DEPTH - 1
        m, j = i % N_MIXERS, i // N_MIXERS
        sh1, sc1, g1, sh2, sc2, g2 = adaln(cond_lat, w_ada[i], b_ada[i])
        h = modulate(rms_norm(x, norm_mix_g[i]), sh1, sc1)
        hc = None
        if (not last) or m == MIXER_NA:
            csh1, csc1, cg1, csh2, csc2, cg2 = adaln(cond_ctx, w_ada[i], b_ada[i])
            hc = modulate(rms_norm(ctx, norm_mix_g[i]), csh1, csc1)
        if m == MIXER_NA:
            y, yc = neighbourhood_attention(h, hc, na_w_qkv[j], na_q_g[j], na_k_g[j],
                                            na_rpb[j], na_w_o[j], not last)
        elif m == MIXER_GMLP:
            y = chunk_gmlp(h, gm_w_in[j], gm_v_g[j], gm_w_s[j], gm_b_s[j], gm_w_out[j])
            yc = None if last else chunk_gmlp(hc, gm_w_in[j], gm_v_g[j], gm_w_s[j], gm_b_s[j], gm_w_out[j])
        else:
            y = short_conv(h, sc_w_in[j], sc_conv_w[j], sc_w_out[j])
            yc = None if last else short_conv(hc, sc_w_in[j], sc_conv_w[j], sc_w_out[j])
        x = x + g1 * y
        hf = modulate(rms_norm(x, norm_ffn_g[i]), sh2, sc2)
        x = x + g2 * conv_ffn(hf, ffn_w_up[i], ffn_conv_w[i], ffn_conv_b[i], ffn_w_down[i])
        if not last:
            ctx = ctx + cg1 * yc
            hcf = modulate(rms_norm(ctx, norm_ffn_g[i]), csh2, csc2)
            ctx = ctx + cg2 * conv_ffn(hcf, ffn_w_up[i], ffn_conv_w[i], ffn_conv_b[i], ffn_w_down[i])
    return x
```

```python
import numpy as np
from contextlib import ExitStack
import concourse.bass as bass
import concourse.mybir as mybir
from concourse.bass_utils import run_bass_kernel_spmd

F32 = mybir.dt.float32
BF16 = mybir.dt.bfloat16
AF = mybir.ActivationFunctionType
ALU = mybir.AluOpType

D = 2048
KD = 16
FF = 5504
KF = 43
NLAT = 2048
NCTX = 256
EPS = 1e-6
NCORES = 8

SEM_ROT = 30000


class _DmaSem:
    def __init__(self, kb, name):
        self.kb = kb
        self.name = name
        self.sem = kb.newsem(name)
        self.n = 0

    def next(self):
        if (self.n + 1) * 16 > 60000:
            self.sem = self.kb.newsem(self.name)
            self.n = 0
        self.n += 1
        return self.sem, self.n * 16


def DmaSem(kb, name):
    d = kb.dsem_pool.get(name)
    if d is None:
        d = kb.dsem_pool[name] = _DmaSem(kb, name)
    return d


class KB:
    def __init__(self, nc, es):
        self.nc = nc
        self.es = es
        self.eng = {"pe": nc.tensor, "act": nc.scalar, "dve": nc.vector, "pool": nc.gpsimd, "sp": nc.sync}
        self.nsem = 0
        self.dsem_pool = {}
        self.cur_es = es
        self.nbuf = 0
        self.csem = {}
        self.cnt = {}
        for e in ("pe", "act", "dve", "pool"):
            self.csem[e] = self.newsem("c_" + e)
            self.cnt[e] = 0
        self.waited = {e: {} for e in self.eng}
        self.res = {}
        self.pending = {e: [] for e in self.eng}
        self.store_toks = []
        self.n_wait = 0
        self.n_ins = 0

    def newsem(self, name):
        self.nsem += 1
        return self.es.enter_context(self.nc.semaphore("%s_%d" % (name, self.nsem)))

    def sbuf(self, name, shape, dt):
        self.nbuf += 1
        return self.cur_es.enter_context(self.nc.sbuf_tensor("%s_%d" % (name, self.nbuf), shape, dt))

    def barrier(self):
        for e in self.pending:
            assert not self.pending[e], e
        toks = [(self.csem[e], self.cnt[e], e) for e in self.csem if self.cnt[e] > 0]
        toks += [(d.sem, d.n * 16, "dma") for d in self.dsem_pool.values() if d.n > 0]
        for e in self.eng:
            for tok in toks:
                self._wait(e, tok, True)
        self.res = {}
        self.store_toks = []

    def psum(self, name, shape, dt):
        return self.es.enter_context(self.nc.psum_tensor(name, shape, dt))

    def _wait(self, e, tok, raw):
        sem, val, src = tok
        if src == e and not raw:
            return
        w = self.waited[e]
        if w.get(id(sem), 0) >= val:
            return
        self.eng[e].wait_ge(sem, val)
        self.n_wait += 1
        w[id(sem)] = val

    def _deps(self, e, reads, writes):
        for k in reads:
            st = self.res.get(k)
            if st:
                for tok in st["w"]:
                    self._wait(e, tok, True)
                if isinstance(k, tuple) and len(k) > 1 and k[1] == "ps":
                    for tok in st["r"]:
                        self._wait(e, tok, False)
        for k in writes:
            st = self.res.get(k)
            if st:
                for tok in st["w"]:
                    self._wait(e, tok, False)
                for tok in st["r"]:
                    self._wait(e, tok, False)

    def _reg(self, tok, reads, writes):
        for k in writes:
            self.res[k] = {"w": [tok], "r": []}
        for k in reads:
            st = self.res.get(k)
            if st is None:
                st = self.res[k] = {"w": [], "r": []}
            st["r"].append(tok)
            if len(st["r"]) > 64:
                st["r"] = st["r"][-32:]

    def op(self, e, fn, reads=(), writes=(), inc=True):
        self._deps(e, reads, writes)
        ins = fn(self.eng[e])
        self.n_ins += 1
        if not inc:
            self.pending[e].append((tuple(reads), tuple(writes)))
            return ins
        if self.cnt[e] >= SEM_ROT:
            self.csem[e] = self.newsem("c_" + e)
            self.cnt[e] = 0
        self.cnt[e] += 1
        ins.then_inc(self.csem[e], 1)
        tok = (self.csem[e], self.cnt[e], e)
        for (r, w) in self.pending[e]:
            self._reg(tok, r, w)
        self.pending[e] = []
        self._reg(tok, reads, writes)
        return ins

    def dma(self, q, dsem, out, in_, reads=(), writes=(), store=False, partial=False):
        self._deps(q, reads, writes)
        ins = self.eng[q].dma_start(out=out, in_=in_)
        self.n_ins += 1
        sem, val = dsem.next()
        ins.then_inc(sem, 16)
        tok = (sem, val, "dma")
        if partial:
            for k in writes:
                st = self.res.get(k)
                if st is None:
                    st = self.res[k] = {"w": [], "r": []}
                st["w"] = [t for t in st["w"] if t[0] is not sem] + [tok]
            self._reg(tok, reads, ())
        else:
            self._reg(tok, reads, writes)
        if store:
            self.store_toks.append(tok)
        return ins

    def finish(self):
        for tok in self.store_toks:
            self._wait("sp", tok, True)
        self.store_toks = []


NE = 3584
OWN0 = 768
OWN1 = OWN0 + NLAT
CTXC = NE
NCA = NE + NCTX
NH = 16
GMW = 4096
KG = 32
ADA_NG = 24


def _split_even(a, b, maxn):
    n = b - a
    k = -(-n // maxn)
    out = []
    c = a
    for i in range(k):
        m = n // k + (1 if i < n % k else 0)
        out.append((c, m))
        c += m
    return out


def _cols_split(c0, c1, step):
    out = []
    c = c0
    while c < c1:
        e = min(c + step, c1)
        out.append((c, e - c))
        c = e
    return out


class Consts:
    pass


def make_consts(kb, name, modt, modkey, ng_ap, which):
    c = Consts()
    c.mod = modt
    c.ng = kb.sbuf(name + "_ng", [128, 16], F32)
    c.A = kb.sbuf(name + "_A", [128, 2, 16], F32)
    kb.dma("sp", DmaSem(kb, "cs_ng"), c.ng[:], ng_ap, writes=[name + "_ng"])
    o = 3 * which
    for s in range(2):
        kb.op("dve", lambda v: v.scalar_tensor_tensor(out=c.A[:, s, :], in0=c.mod[:, s, o + 1, :], scalar=1.0,
                                                     in1=c.ng[:], op0=ALU.add, op1=ALU.mult),
              reads=[modkey, name + "_ng"], writes=[name + "_A%d" % s])
    c.Akey = lambda s: name + "_A%d" % s
    c.Bap = lambda s, k: c.mod[:, s, o + 0, k:k + 1]
    c.Gap = lambda s, k: c.mod[:, s, o + 2, k:k + 1]
    c.Aap = lambda s, k: c.A[:, s, k:k + 1]
    c.modkey = modkey
    return c


class NormBufs:
    def __init__(self, kb, name, w, ones, okey):
        self.w = w
        self.xn = [kb.sbuf("%s_xn%d" % (name, i), [128, KD, w], F32) for i in range(2)]
        self.sq = kb.sbuf(name + "_sq", [128, KD, w], BF16)
        self.rs = kb.sbuf(name + "_rs", [128, w], F32)
        self.cm = kb.sbuf(name + "_cm", [128, w], F32)
        self.ones = ones
        self.okey = okey
        self.ds = [DmaSem(kb, "nb_xs%d" % i) for i in range(2)]
        self.dcm = DmaSem(kb, "nb_cm")
        self.name = name
        self.i = 0


def emit_norm(kb, nb, cst, x_ap, cmask_ap, pieces, hT, hkey, ps_ap, pskey):
    name = nb.name
    for (c0, n, s, hc, masked) in pieces:
        slot = nb.i % 2
        nb.i += 1
        xn = nb.xn[slot]
        xk = "%s_xn%d" % (name, slot)
        kb.dma("sp", nb.ds[slot], xn[:, :, 0:n], x_ap[:, :, c0:c0 + n], writes=[xk])
        if masked:
            kb.dma("sp", nb.dcm, nb.cm[:, 0:n], cmask_ap[:, c0:c0 + n], writes=[name + "_cm"])
        kb.op("act", lambda a: a.activation(out=nb.sq[:, :, 0:n], in_=xn[:, :, 0:n], func=AF.Square),
              reads=[xk], writes=[name + "_sq"])
        for k in range(KD):
            kb.op("pe", lambda t: t.matmul(ps_ap[:, 0:n], lhsT=nb.ones[:], rhs=nb.sq[:, k, 0:n],
                                           start=(k == 0), stop=(k == KD - 1)),
                  reads=[name + "_sq", nb.okey], writes=[pskey], inc=(k == KD - 1))
        kb.op("act", lambda a: a.activation(out=nb.rs[:, 0:n], in_=ps_ap[:, 0:n], func=AF.Sqrt,
                                            scale=1.0 / D, bias=EPS),
              reads=[pskey], writes=[name + "_rs"])
        kb.op("dve", lambda v: v.reciprocal(out=nb.rs[:, 0:n], in_=nb.rs[:, 0:n]),
              reads=[name + "_rs"], writes=[name + "_rs"])
        if masked:
            pass
        kb.op("dve", lambda v: v.tensor_tensor(out=xn[:, :, 0:n], in0=xn[:, :, 0:n],
                                               in1=nb.rs[:, 0:n].unsqueeze(1).broadcast_to([128, KD, n]),
                                               op=ALU.mult),
              reads=[xk, name + "_rs"], writes=[xk])
        for k in range(KD):
            kb.op("act", lambda a: a.activation(out=hT[:, k, hc:hc + n], in_=xn[:, k, 0:n],
                                                func=AF.Identity, scale=cst.Aap(s, k), bias=cst.Bap(s, k)),
                  reads=[xk, cst.Akey(s), cst.modkey], writes=[hkey])
        if masked:
            kb.op("dve", lambda v: v.tensor_tensor(out=hT[:, :, hc:hc + n], in0=hT[:, :, hc:hc + n],
                                                   in1=nb.cm[:, 0:n].unsqueeze(1).broadcast_to([128, KD, n]),
                                                   op=ALU.mult),
                  reads=[hkey, name + "_cm"], writes=[hkey])


class OutProj:
    def __init__(self, kb, name, kmax):
        self.name = name
        self.wd = [kb.sbuf("%s_wd%d" % (name, i), [128, kmax, 128], BF16) for i in range(2)]
        self.wds = [DmaSem(kb, "op_wds%d" % i) for i in range(2)]
        self.xr = [kb.sbuf("%s_xr%d" % (name, i), [128, 512], F32) for i in range(2)]
        self.xrs = [DmaSem(kb, "op_xrs%d" % i) for i in range(2)]
        self.ot = [kb.sbuf("%s_ot%d" % (name, i), [128, 512], F32) for i in range(2)]
        self.ots = [DmaSem(kb, "op_ots%d" % i) for i in range(2)]
        self.wd_i = 0
        self.dn = 0


def emit_outproj(kb, o, cst, g, gkey, nk, wdn, groups, xT, yT, ps, banks):
    name = o.name

    def load_wd(dc, wd_i):
        slot = wd_i % 2
        kb.dma("pool", o.wds[slot], o.wd[slot][:, 0:nk, :], wdn[dc, :, 0:nk, :], writes=[(name, "wd", slot)])

    load_wd(0, o.wd_i)
    for dc in range(KD):
        if dc + 1 < KD:
            load_wd(dc + 1, o.wd_i + 1)
        slot = o.wd_i % 2
        wkey = (name, "wd", slot)
        wdt = o.wd[slot]
        for (g0, xc, oc, n, s) in groups:
            r = o.dn % 2
            o.dn += 1
            b = banks[r]
            xr, ot = o.xr[r], o.ot[r]
            kb.dma("sp", o.xrs[r], xr[:, 0:n], xT[:, dc, xc:xc + n], writes=[(name, "xr", r)])
            for j in range(nk):
                kb.op("pe", lambda t: t.matmul(ps[:, b, 0:n], lhsT=wdt[:, j, :], rhs=g[:, j, g0:g0 + n],
                                               start=(j == 0), stop=(j == nk - 1)),
                      reads=[wkey, gkey], writes=[("P", "ps", b)], inc=(j == nk - 1))
            kb.op("dve", lambda v: v.scalar_tensor_tensor(out=ot[:, 0:n], in0=ps[:, b, 0:n],
                                                          scalar=cst.Gap(s, dc), in1=xr[:, 0:n],
                                                          op0=ALU.mult, op1=ALU.add),
                  reads=[("P", "ps", b), (name, "xr", r), cst.modkey], writes=[(name, "ot", r)])
            kb.dma("sp", o.ots[r], yT[:, dc, oc:oc + n], ot[:, 0:n], reads=[(name, "ot", r)], store=True)
        o.wd_i += 1


def _conv_supers(a, b, with_ctx, cap):
    tot = (b - a) + (NCTX if with_ctx else 0)
    ns = -(-tot // cap)
    per = -(-tot // ns)
    supers = []
    c = a
    for si in range(ns):
        room = per if si < ns - 1 else cap
        tiles = []
        nl = min(room, b - c)
        if nl > 0:
            tiles += [("lat", c0, n) for (c0, n) in _split_even(c, c + nl, 510)]
            c += nl
            room -= nl
        supers.append(tiles)
    if with_ctx:
        used = sum(t[2] for t in supers[-1])
        if used + NCTX <= cap:
            supers[-1].append(("ctx", CTXC, NCTX))
        else:
            supers.append([("ctx", CTXC, NCTX)])
    assert c == b
    return supers


def _conv_plan(tiles):
    descs, spans, zero = [], [], []
    h = 0
    lat = [t for t in tiles if t[0] == "lat"]
    if lat:
        lo = lat[0][1] - 1
        hi = lat[-1][1] + lat[-1][2] + 1
        spans.append((lo, hi - lo, 0, h))
        for (_, c0, n) in lat:
            descs.append((c0, n, h + (c0 - 1 - lo), 0))
        h += hi - lo
    for (kind, c0, n) in tiles:
        if kind == "ctx":
            zero += [h, h + n + 1]
            spans.append((c0, n, 1, h + 1))
            descs.append((c0, n, h, 1))
            h += n + 2
    return descs, spans, zero, h


def _norm_pieces(spans, w):
    pieces = []
    for (c0, n, s, hc) in spans:
        for (p0, pn) in _cols_split(c0, c0 + n, w):
            masked = (s == 0) and (p0 < OWN0 or p0 + pn > OWN1)
            pieces.append((p0, pn, s, hc + (p0 - c0), masked))
    return pieces


def emit_ffn(kb, G, cst, src, dst, dst_off, a, b, with_ctx, wup, cwb, wdn):
    name = "f"
    ps = G.ps
    cw = kb.sbuf(name + "_cw", [128, 2, KF, 4], F32)
    kb.dma("sp", DmaSem(kb, "f_c2"), cw[:], cwb, writes=[name + "_cw"])
    nb = NormBufs(kb, name + "n", 64, G.ones, G.okey)
    CAP = 1024
    WMAX = CAP + 8
    hT = kb.sbuf(name + "_hT", [128, KD, WMAX], BF16)
    g = kb.sbuf(name + "_g", [128, KF, CAP], BF16)
    NWB = 2
    wu = [kb.sbuf("%s_wu%d" % (name, i), [128, 2, KD, 128], BF16) for i in range(NWB)]
    wus = [DmaSem(kb, "f_wus%d" % i) for i in range(NWB)]
    ya = [kb.sbuf("%s_ya%d" % (name, i), [128, 512], F32) for i in range(2)]
    yb = [kb.sbuf("%s_yb%d" % (name, i), [128, 512], F32) for i in range(2)]
    sg = [kb.sbuf("%s_sg%d" % (name, i), [128, 512], F32) for i in range(2)]
    opj = OutProj(kb, name + "o", KF)
    hkey, gkey, cwk = name + "_hT", name + "_g", name + "_cw"
    zb = ev = wu_i = 0
    for tiles in _conv_supers(a, b, with_ctx, CAP):
        descs, spans, zero, width = _conv_plan(tiles)
        assert width <= WMAX, width
        emit_norm(kb, nb, cst, src, G.cmask, _norm_pieces(spans, nb.w), hT, hkey, ps[:, 6, :], ("P", "ps", 6))
        for zc in zero:
            kb.op("dve", lambda v: v.memset(hT[:, :, zc:zc + 1], 0.0), writes=[hkey])
        gcol = {}
        gc = 0
        for (c0, n, hb, s) in descs:
            gcol[c0] = gc
            gc += n
        assert gc <= CAP

        def load_wu(j, i):
            slot = i % NWB
            kb.dma("pool", wus[slot], wu[slot][:], wup[j], writes=[(name, "wu", slot)])

        load_wu(0, wu_i)
        for j in range(KF):
            if j + 1 < KF:
                load_wu(j + 1, wu_i + 1)
            slot = wu_i % NWB
            wkey = (name, "wu", slot)
            for (c0, n, hb, s) in descs:
                banks = []
                for half in range(2):
                    bk = zb % 6
                    zb += 1
                    banks.append(bk)
                    for k in range(KD):
                        kb.op("pe", lambda t: t.matmul(ps[:, bk, 0:n + 2], lhsT=wu[slot][:, half, k, :],
                                                       rhs=hT[:, k, hb:hb + n + 2],
                                                       start=(k == 0), stop=(k == KD - 1)),
                              reads=[wkey, hkey], writes=[("P", "ps", bk)], inc=(k == KD - 1))
                e = ev % 2
                ev += 1
                bg, bu = banks
                kg, ku = ("P", "ps", bg), ("P", "ps", bu)
                kb.op("act", lambda a_: a_.activation(out=ya[e][:, 0:n], in_=ps[:, bg, 1:n + 1], func=AF.Identity,
                                                      scale=cw[:, 0, j, 1:2], bias=cw[:, 0, j, 3:4]),
                      reads=[kg, cwk], writes=[(name, "ya", e)])
                kb.op("act", lambda a_: a_.activation(out=yb[e][:, 0:n], in_=ps[:, bu, 1:n + 1], func=AF.Identity,
                                                      scale=cw[:, 1, j, 1:2], bias=cw[:, 1, j, 3:4]),
                      reads=[ku, cwk], writes=[(name, "yb", e)])
                for (tap, off) in ((0, 0), (2, 2)):
                    kb.op("dve", lambda v: v.scalar_tensor_tensor(
                        out=ya[e][:, 0:n], in0=ps[:, bg, off:off + n], scalar=cw[:, 0, j, tap:tap + 1],
                        in1=ya[e][:, 0:n], op0=ALU.mult, op1=ALU.add),
                        reads=[kg, cwk, (name, "ya", e)], writes=[(name, "ya", e)])
                    kb.op("dve", lambda v: v.scalar_tensor_tensor(
                        out=yb[e][:, 0:n], in0=ps[:, bu, off:off + n], scalar=cw[:, 1, j, tap:tap + 1],
                        in1=yb[e][:, 0:n], op0=ALU.mult, op1=ALU.add),
                        reads=[ku, cwk, (name, "yb", e)], writes=[(name, "yb", e)])
                kb.op("act", lambda a_: a_.activation(out=sg[e][:, 0:n], in_=ya[e][:, 0:n], func=AF.Silu),
                      reads=[(name, "ya", e)], writes=[(name, "sg", e)])
                g0 = gcol[c0]
                kb.op("dve", lambda v: v.tensor_tensor(out=g[:, j, g0:g0 + n], in0=yb[e][:, 0:n], in1=sg[e][:, 0:n],
                                                       op=ALU.mult),
                      reads=[(name, "yb", e), (name, "sg", e)], writes=[gkey])
            wu_i += 1
        groups = [(gcol[c0], c0, c0 - dst_off, n, s) for (c0, n, hb, s) in descs]
        emit_outproj(kb, opj, cst, g, gkey, KF, wdn, groups, src, dst, ps, (6, 7))


def emit_sc(kb, G, cst, src, dst, a, b, with_ctx, win, cwd, wout):
    name = "s"
    ps = G.ps
    cw = kb.sbuf(name + "_cw", [128, KD, 3], F32)
    kb.dma("sp", DmaSem(kb, "s_c2"), cw[:], cwd, writes=[name + "_cw"])
    nb = NormBufs(kb, name + "n", 64, G.ones, G.okey)
    CAP = 1024
    WMAX = CAP + 8
    hT = kb.sbuf(name + "_hT", [128, KD, WMAX], BF16)
    g = kb.sbuf(name + "_g", [128, KD, CAP], BF16)
    wu = [kb.sbuf("%s_wu%d" % (name, i), [128, 3, KD, 128], BF16) for i in range(2)]
    wus = [DmaSem(kb, "s_wus%d" % i) for i in range(2)]
    ux = [kb.sbuf("%s_ux%d" % (name, i), [128, 512], F32) for i in range(2)]
    uu = [kb.sbuf("%s_uu%d" % (name, i), [128, 512], F32) for i in range(2)]
    vv = [kb.sbuf("%s_vv%d" % (name, i), [128, 512], F32) for i in range(2)]
    opj = OutProj(kb, name + "o", KD)
    hkey, gkey, cwk = name + "_hT", name + "_g", name + "_cw"
    zb = ev = wi = 0
    for tiles in _conv_supers(a, b, with_ctx, CAP):
        descs, spans, zero, width = _conv_plan(tiles)
        assert width <= WMAX
        emit_norm(kb, nb, cst, src, G.cmask, _norm_pieces(spans, nb.w), hT, hkey, ps[:, 6, :], ("P", "ps", 6))
        for zc in zero:
            kb.op("dve", lambda v: v.memset(hT[:, :, zc:zc + 1], 0.0), writes=[hkey])
        gcol = {}
        gc = 0
        for (c0, n, hb, s) in descs:
            gcol[c0] = gc
            gc += n

        def load_w(j, i):
            slot = i % 2
            kb.dma("pool", wus[slot], wu[slot][:], win[j], writes=[(name, "wu", slot)])

        load_w(0, wi)
        for j in range(KD):
            if j + 1 < KD:
                load_w(j + 1, wi + 1)
            slot = wi % 2
            wkey = (name, "wu", slot)
            for (c0, n, hb, s) in descs:
                banks = []
                for t3 in range(3):
                    bk = zb % 6
                    zb += 1
                    banks.append(bk)
                    for k in range(KD):
                        kb.op("pe", lambda t: t.matmul(ps[:, bk, 0:n + 2], lhsT=wu[slot][:, t3, k, :],
                                                       rhs=hT[:, k, hb:hb + n + 2],
                                                       start=(k == 0), stop=(k == KD - 1)),
                              reads=[wkey, hkey], writes=[("P", "ps", bk)], inc=(k == KD - 1))
                e = ev % 2
                ev += 1
                bb, bc, bx = banks
                kb.op("act", lambda a_: a_.activation(out=ux[e][:, 0:n + 2], in_=ps[:, bx, 0:n + 2], func=AF.Copy),
                      reads=[("P", "ps", bx)], writes=[(name, "ux", e)])
                kb.op("dve", lambda v: v.tensor_tensor(out=uu[e][:, 0:n + 2], in0=ps[:, bc, 0:n + 2],
                                                       in1=ux[e][:, 0:n + 2], op=ALU.mult),
                      reads=[("P", "ps", bc), (name, "ux", e)], writes=[(name, "uu", e)])
                kb.op("act", lambda a_: a_.activation(out=vv[e][:, 0:n], in_=uu[e][:, 1:n + 1], func=AF.Identity,
                                                      scale=cw[:, j, 1:2]),
                      reads=[(name, "uu", e), cwk], writes=[(name, "vv", e)])
                for (tap, off) in ((0, 0), (2, 2)):
                    kb.op("dve", lambda v: v.scalar_tensor_tensor(
                        out=vv[e][:, 0:n], in0=uu[e][:, off:off + n], scalar=cw[:, j, tap:tap + 1],
                        in1=vv[e][:, 0:n], op0=ALU.mult, op1=ALU.add),
                        reads=[(name, "uu", e), cwk, (name, "vv", e)], writes=[(name, "vv", e)])
                g0 = gcol[c0]
                kb.op("dve", lambda v: v.tensor_tensor(out=g[:, j, g0:g0 + n], in0=ps[:, bb, 1:n + 1],
                                                       in1=vv[e][:, 0:n], op=ALU.mult),
                      reads=[("P", "ps", bb), (name, "vv", e)], writes=[gkey])
            wi += 1
        groups = [(gcol[c0], c0, c0, n, s) for (c0, n, hb, s) in descs]
        emit_outproj(kb, opj, cst, g, gkey, KD, wout, groups, src, dst, ps, (6, 7))


def emit_gm(kb, G, cst, src, dst, a, b, with_ctx, wu_d, wv_d, vg_d, wsT_d, bsb_d, wout):
    name = "m"
    ps = G.ps
    vg = kb.sbuf(name + "_vg", [128, KG], F32)
    wsT = kb.sbuf(name + "_wsT", [128, 16, 128], F32)
    bsb = kb.sbuf(name + "_bsb", [128, 16, 128], F32)
    kb.dma("sp", DmaSem(kb, "m_d1"), vg[:], vg_d, writes=[name + "_vg"])
    kb.dma("sp", DmaSem(kb, "m_d2"), wsT[:], wsT_d, writes=[name + "_wsT"])
    kb.dma("sp", DmaSem(kb, "m_d3"), bsb[:], bsb_d, writes=[name + "_bsb"])
    nb = NormBufs(kb, name + "n", 128, G.ones, G.okey)
    TS = 512
    hT = kb.sbuf(name + "_hT", [128, KD, TS], BF16)
    uT = kb.sbuf(name + "_uT", [128, KG, TS], BF16)
    vraw = [kb.sbuf("%s_vr%d" % (name, i), [128, GMW], BF16) for i in range(4)]
    sqj = kb.sbuf(name + "_sqj", [128, 512], BF16)
    ssp = kb.sbuf(name + "_ssp", [128, 4, 8], F32)
    rst = kb.sbuf(name + "_rst", [128, 4], F32)
    NWU = 3
    wu = [kb.sbuf("%s_wu%d" % (name, i), [128, KD, 128], BF16) for i in range(NWU)]
    wus = [DmaSem(kb, "m_wus%d" % i) for i in range(NWU)]
    wv = [kb.sbuf("%s_wv%d" % (name, i), [128, KD, 512], BF16) for i in range(2)]
    wvs = [DmaSem(kb, "m_wvs%d" % i) for i in range(2)]
    wss = [kb.sbuf("%s_wss%d" % (name, i), [128, 16, 128], BF16) for i in range(2)]
    tmp = [kb.sbuf("%s_tmp%d" % (name, i), [128, 4, 128], F32) for i in range(2)]
    opj = OutProj(kb, name + "o", KG)
    hkey, ukey = name + "_hT", name + "_uT"
    zb = wu_i = wv_i = sp_i = 0
    supers = [(c0, n, 0) for (c0, n) in _cols_split(a, b, TS)]
    if with_ctx:
        supers.append((CTXC, NCTX, 1))
    for (t0, T, sset) in supers:
        ntc = T // 128
        pieces = [(c0, n, sset, c0 - t0, False) for (c0, n) in _cols_split(t0, t0 + T, nb.w)]
        emit_norm(kb, nb, cst, src, G.cmask, pieces, hT, hkey, ps[:, 6, :], ("P", "ps", 6))

        def load_wu(j, i):
            slot = i % NWU
            kb.dma("pool", wus[slot], wu[slot][:], wu_d[j, :, 0], writes=[(name, "wu", slot)])

        load_wu(0, wu_i)
        load_wu(1, wu_i + 1)
        for j in range(KG):
            if j + 2 < KG:
                load_wu(j + 2, wu_i + 2)
            slot = wu_i % NWU
            bk = zb % 4
            zb += 1
            for k in range(KD):
                kb.op("pe", lambda t: t.matmul(ps[:, bk, 0:T], lhsT=wu[slot][:, k, :], rhs=hT[:, k, 0:T],
                                               start=(k == 0), stop=(k == KD - 1)),
                      reads=[(name, "wu", slot), hkey], writes=[("P", "ps", bk)], inc=(k == KD - 1))
            kb.op("act", lambda a_: a_.activation(out=uT[:, j, 0:T], in_=ps[:, bk, 0:T], func=AF.Gelu_apprx_tanh),
                  reads=[("P", "ps", bk)], writes=[ukey])
            wu_i += 1

        def load_wv(cg, i):
            slot = i % 2
            kb.dma("pool", wvs[slot], wv[slot][:], wv_d[cg], writes=[(name, "wv", slot)])

        load_wv(0, wv_i)
        for cg in range(8):
            if cg + 1 < 8:
                load_wv(cg + 1, wv_i + 1)
            slot = wv_i % 2
            for tc in range(ntc):
                bk = zb % 4
                zb += 1
                for k in range(KD):
                    kb.op("pe", lambda t: t.matmul(ps[:, bk, :], lhsT=hT[:, k, tc * 128:(tc + 1) * 128],
                                                   rhs=wv[slot][:, k, :], start=(k == 0), stop=(k == KD - 1)),
                          reads=[(name, "wv", slot), hkey], writes=[("P", "ps", bk)], inc=(k == KD - 1))
                vslice = vraw[tc][:, cg * 512:(cg + 1) * 512]
                kb.op("act", lambda a_: a_.activation(out=vslice, in_=ps[:, bk, :], func=AF.Gelu_apprx_tanh),
                      reads=[("P", "ps", bk)], writes=[(name, "vr", tc)])
                kb.op("act", lambda a_: a_.activation(out=sqj[:], in_=vslice, func=AF.Square,
                                                      accum_out=ssp[:, tc, cg:cg + 1]),
                      reads=[(name, "vr", tc)], writes=[name + "_sqj", (name, "ssp", tc)])
            wv_i += 1

        for tc in range(ntc):
            kb.op("dve", lambda v: v.tensor_reduce(out=rst[:, tc:tc + 1], in_=ssp[:, tc, :], op=ALU.add,
                                                   axis=mybir.AxisListType.X),
                  reads=[(name, "ssp", tc)], writes=[(name, "rst", tc)])
            kb.op("act", lambda a_: a_.activation(out=rst[:, tc:tc + 1], in_=rst[:, tc:tc + 1], func=AF.Sqrt,
                                                  scale=1.0 / GMW, bias=EPS),
                  reads=[(name, "rst", tc)], writes=[(name, "rst", tc)])
            kb.op("dve", lambda v: v.reciprocal(out=rst[:, tc:tc + 1], in_=rst[:, tc:tc + 1]),
                  reads=[(name, "rst", tc)], writes=[(name, "rst", tc)])
            w = sp_i % 2
            sp_i += 1
            kb.op("dve", lambda v: v.tensor_scalar(out=wss[w][:], in0=wsT[:], scalar1=rst[:, tc:tc + 1], scalar2=None,
                                                   op0=ALU.mult),
                  reads=[name + "_wsT", (name, "rst", tc)], writes=[(name, "wss", w)])
            for q4 in range(8):
                bk = 4 + (q4 % 2)
                for cc in range(4):
                    c = q4 * 4 + cc
                    kb.op("pe", lambda t: t.matmul(ps[:, bk, cc * 128:(cc + 1) * 128],
                                                   lhsT=vraw[tc][:, c * 128:(c + 1) * 128],
                                                   rhs=wss[w][:, c // 2, :], start=True, stop=True),
                          reads=[(name, "vr", tc), (name, "wss", w)], writes=[("P", "ps", bk)], inc=(cc == 3))
                tm = tmp[q4 % 2]
                tk = (name, "tmp", q4 % 2)
                for cc in range(4):
                    c = q4 * 4 + cc
                    kb.op("dve", lambda v: v.scalar_tensor_tensor(
                        out=tm[:, cc, :], in0=ps[:, bk, cc * 128:(cc + 1) * 128], scalar=vg[:, c:c + 1],
                        in1=bsb[:, c // 2, :], op0=ALU.mult, op1=ALU.add),
                        reads=[("P", "ps", bk), name + "_vg", name + "_bsb"], writes=[tk])
                c0 = q4 * 4
                kb.op("dve", lambda v: v.tensor_tensor(out=uT[:, c0:c0 + 4, tc * 128:(tc + 1) * 128],
                                                       in0=uT[:, c0:c0 + 4, tc * 128:(tc + 1) * 128],
                                                       in1=tm[:], op=ALU.mult),
                      reads=[tk, ukey], writes=[ukey])
        emit_outproj(kb, opj, cst, uT, ukey, KG, wout, [(0, t0, t0, T, sset)], src, dst, ps, (6, 7))


def emit_na(kb, G, cst, src, dst, qb0, qb1, kv0, kvlen, ctx_q, special, wqkv, qkg_d, biasI, biasS, wo, oscr):
    name = "a"
    ps = G.ps
    psb = ps[:, 7, :].bitcast(BF16)
    ones, okey, idn = G.ones, G.okey, G.idn
    nkv = kvlen + NCTX
    nqb = qb1 - qb0
    nq = nqb * 128 + (NCTX if ctx_q else 0)
    nvb = nkv // 128
    qkg = kb.sbuf(name + "_qkg", [128, 2], F32)
    kb.dma("sp", DmaSem(kb, "a_d1"), qkg[:], qkg_d, writes=[name + "_qkg"])
    kb.op("dve", lambda v: v.tensor_scalar(out=qkg[:, 0:1], in0=qkg[:, 0:1], scalar1=float(128 ** -0.5), scalar2=None,
                                           op0=ALU.mult), reads=[name + "_qkg"], writes=[name + "_qkg"])
    nb = NormBufs(kb, name + "n", 64, ones, okey)
    hT = kb.sbuf(name + "_hT", [128, KD, nkv], BF16)
    hkey = name + "_hT"
    wq = [kb.sbuf("%s_wq%d" % (name, i), [128, 3, KD, 128], BF16) for i in range(2)]
    wqs = [DmaSem(kb, "a_wqs%d" % i) for i in range(2)]
    qT = kb.sbuf(name + "_qT", [128, nq], BF16)
    kT = kb.sbuf(name + "_kT", [128, nkv], BF16)
    vT = kb.sbuf(name + "_vT", [128, nkv], BF16)
    V = kb.sbuf(name + "_V", [128, nvb, 128], BF16)
    qraw = [kb.sbuf("%s_qraw%d" % (name, i), [128, 512], F32) for i in range(2)]
    sq = kb.sbuf(name + "_sqh", [128, 512], BF16)
    rs = kb.sbuf(name + "_rsh", [128, 512], F32)
    bI = [kb.sbuf("%s_bI%d" % (name, i), [128, 5, 128], F32) for i in range(2)]
    bIs = [DmaSem(kb, "a_bIs%d" % i) for i in range(2)]
    bS = [kb.sbuf("%s_bS%d" % (name, i), [128, 6, 128], F32) for i in range(2)]
    bSs = [DmaSem(kb, "a_bSs%d" % i) for i in range(2)]
    sb = [kb.sbuf("%s_sb%d" % (name, i), [128, 6, 128], F32) for i in range(2)]
    PT = [kb.sbuf("%s_PT%d" % (name, i), [128, 8, 128], BF16) for i in range(2)]
    rden = [kb.sbuf("%s_rden%d" % (name, i), [128, 128], F32) for i in range(2)]
    oTh = [kb.sbuf("%s_oTh%d" % (name, i), [128, nq], BF16) for i in range(2)]
    oTs = [DmaSem(kb, "a_oTs%d" % i) for i in range(2)]
    opj = OutProj(kb, name + "o", NH)
    P = ("P", "ps")

    pieces = [(c0, n, 0, c0 - kv0, False) for (c0, n) in _cols_split(kv0, kv0 + kvlen, nb.w)]
    pieces += [(c0, n, 1, kvlen + c0 - CTXC, False) for (c0, n) in _cols_split(CTXC, CTXC + NCTX, nb.w)]
    emit_norm(kb, nb, cst, src, G.cmask, pieces, hT, hkey, ps[:, 4, :], P + (4,))

    q_tiles = [(c0 - kv0, n, c0 - qb0 * 128) for (c0, n) in _cols_split(qb0 * 128, qb1 * 128, 512)]
    if ctx_q:
        q_tiles.append((kvlen, NCTX, nqb * 128))
    kv_tiles = [(c0, n, c0) for (c0, n) in _cols_split(0, nkv, 512)]
    zb = qr_i = sb_i = pv_i = bs_i = 0

    def load_w(h):
        slot = h % 2
        kb.dma("pool", wqs[slot], wq[slot][:], wqkv[h], writes=[(name, "wq", slot)])

    load_w(0)
    for h in range(NH):
        if h + 1 < NH:
            load_w(h + 1)
        hs = h % 2
        wkey = (name, "wq", hs)
        w = wq[hs]
        kb.dma("sp", bIs[hs], bI[hs][:], biasI[h], writes=[(name, "bI", hs)])
        for t3 in range(3):
            for (c0, n, dc0) in (q_tiles if t3 == 0 else kv_tiles):
                bk = zb % 4
                zb += 1
                for k in range(KD):
                    kb.op("pe", lambda t: t.matmul(ps[:, bk, 0:n], lhsT=w[:, t3, k, :], rhs=hT[:, k, c0:c0 + n],
                                                   start=(k == 0), stop=(k == KD - 1)),
                          reads=[wkey, hkey], writes=[P + (bk,)], inc=(k == KD - 1))
                if t3 == 2:
                    kb.op("act", lambda a_: a_.activation(out=vT[:, c0:c0 + n], in_=ps[:, bk, 0:n], func=AF.Copy),
                          reads=[P + (bk,)], writes=[name + "_vT"])
                    continue
                r = qr_i % 2
                qr_i += 1
                kb.op("dve", lambda v: v.tensor_copy(out=qraw[r][:, 0:n], in_=ps[:, bk, 0:n]),
                      reads=[P + (bk,)], writes=[(name, "qraw", r)])
                kb.op("act", lambda a_: a_.activation(out=sq[:, 0:n], in_=qraw[r][:, 0:n], func=AF.Square),
                      reads=[(name, "qraw", r)], writes=[name + "_sqh"])
                kb.op("pe", lambda t: t.matmul(ps[:, 4, 0:n], lhsT=ones[:], rhs=sq[:, 0:n], start=True, stop=True),
                      reads=[name + "_sqh", okey], writes=[P + (4,)])
                kb.op("act", lambda a_: a_.activation(out=rs[:, 0:n], in_=ps[:, 4, 0:n], func=AF.Ln,
                                                      scale=1.0 / 128, bias=EPS),
                      reads=[P + (4,)], writes=[name + "_rsh"])
                kb.op("act", lambda a_: a_.activation(out=rs[:, 0:n], in_=rs[:, 0:n], func=AF.Exp, scale=-0.5),
                      reads=[name + "_rsh"], writes=[name + "_rsh"])
                dstt = qT if t3 == 0 else kT
                dkey = name + ("_qT" if t3 == 0 else "_kT")
                kb.op("dve", lambda v: v.scalar_tensor_tensor(out=dstt[:, dc0:dc0 + n], in0=qraw[r][:, 0:n],
                                                              scalar=qkg[:, t3:t3 + 1], in1=rs[:, 0:n],
                                                              op0=ALU.mult, op1=ALU.mult),
                      reads=[(name, "qraw", r), name + "_qkg", name + "_rsh"], writes=[dkey])
        for b0 in range(0, nvb, 8):
            nbk = min(8, nvb - b0)
            for i in range(nbk):
                kb.op("pe", lambda t: t.transpose(psb[:, i * 128:(i + 1) * 128],
                                                  vT[:, (b0 + i) * 128:(b0 + i + 1) * 128], idn[:]),
                      reads=[name + "_vT", G.ikey], writes=[P + (7,)], inc=(i == nbk - 1))
            kb.op("dve", lambda v: v.tensor_copy(out=V[:, b0:b0 + nbk, :],
                                                 in_=psb[:, 0:nbk * 128].rearrange("p (b c) -> p b c", c=128)),
                  reads=[P + (7,)], writes=[name + "_V"])
        oT = oTh[hs]
        okh = (name, "oTh", hs)
        qblocks = []
        for bq in range(qb0, qb1):
            st, npair, sidx = special.get(bq, (-4, 5, -1))
            qblocks.append(("lat", bq, st, npair, sidx, (bq - qb0) * 128))
        if ctx_q:
            qblocks += [("ctx", 0, 0, 0, -1, nqb * 128), ("ctx", 0, 0, 0, -1, nqb * 128 + 128)]
        for (kind, bq, start, npair, sidx, qc0) in qblocks:
            s2 = sb_i % 2
            sb_i += 1
            chunks = []
            if kind == "lat":
                for p in range(npair):
                    er = 2 * bq + start + 2 * p
                    kc = er * 64 - kv0
                    assert 0 <= kc and kc + 128 <= kvlen, (bq, er, kv0, kvlen)
                    chunks.append((kc, kc // 128, p))
            for cc in range(2):
                chunks.append((kvlen + cc * 128, kvlen // 128 + cc, 6 + cc))
            if sidx >= 0:
                bsl = bs_i % 2
                bs_i += 1
                kb.dma("sp", bSs[bsl], bS[bsl][:], biasS[sidx, h], writes=[(name, "bS", bsl)])
                bias_ap, bias_key = bS[bsl], (name, "bS", bsl)
            else:
                bias_ap, bias_key = bI[hs], (name, "bI", hs)
            bkA, bkB = 2 * s2, 2 * s2 + 1
            for (kc, vb, slot) in chunks:
                bk = bkA if slot < 4 else bkB
                col = (slot % 4) * 128
                kb.op("pe", lambda t: t.matmul(ps[:, bk, col:col + 128], lhsT=kT[:, kc:kc + 128],
                                               rhs=qT[:, qc0:qc0 + 128], start=True, stop=True),
                      reads=[name + "_kT", name + "_qT"], writes=[P + (bk,)], inc=(slot == 7))
            if kind == "lat":
                n1 = min(npair, 4)
                kb.op("dve", lambda v: v.tensor_tensor(out=sb[s2][:, 0:n1, :],
                                                       in0=ps[:, bkA, 0:n1 * 128].rearrange("p (c q) -> p c q", q=128),
                                                       in1=bias_ap[:, 0:n1, :], op=ALU.add),
                      reads=[P + (bkA,), bias_key], writes=[(name, "sb", s2)])
                n2 = npair - 4
                kb.op("dve", lambda v: v.tensor_tensor(out=sb[s2][:, 4:4 + n2, :],
                                                       in0=ps[:, bkB, 0:n2 * 128].rearrange("p (c q) -> p c q", q=128),
                                                       in1=bias_ap[:, 4:4 + n2, :], op=ALU.add),
                      reads=[P + (bkB,), bias_key], writes=[(name, "sb", s2)])
                kb.op("act", lambda a_: a_.activation(out=PT[s2][:, 0:npair, :], in_=sb[s2][:, 0:npair, :],
                                                      func=AF.Exp),
                      reads=[(name, "sb", s2)], writes=[(name, "PT", s2)])
            kb.op("act", lambda a_: a_.activation(out=PT[s2][:, 6:8, :],
                                                  in_=ps[:, bkB, 256:512].rearrange("p (c q) -> p c q", q=128),
                                                  func=AF.Exp),
                  reads=[P + (bkB,)], writes=[(name, "PT", s2)])
            pv = pv_i % 2
            pv_i += 1
            pvb = 5 + pv
            nch = len(chunks)
            for i, (kc, vb, slot) in enumerate(chunks):
                kb.op("pe", lambda t: t.matmul(ps[:, pvb, 0:128], lhsT=V[:, vb, :], rhs=PT[s2][:, slot, :],
                                               start=(i == 0), stop=(i == nch - 1)),
                      reads=[name + "_V", (name, "PT", s2)], writes=[P + (pvb,)], inc=False)
            for i, (kc, vb, slot) in enumerate(chunks):
                kb.op("pe", lambda t: t.matmul(ps[:, pvb, 128:256], lhsT=ones[:], rhs=PT[s2][:, slot, :],
                                               start=(i == 0), stop=(i == nch - 1)),
                      reads=[okey, (name, "PT", s2)], writes=[P + (pvb,)], inc=(i == nch - 1))
            kb.op("act", lambda a_: a_.activation(out=rden[pv][:], in_=ps[:, pvb, 128:256], func=AF.Ln),
                  reads=[P + (pvb,)], writes=[(name, "rden", pv)])
            kb.op("act", lambda a_: a_.activation(out=rden[pv][:], in_=rden[pv][:], func=AF.Exp, scale=-1.0),
                  reads=[(name, "rden", pv)], writes=[(name, "rden", pv)])
            kb.op("dve", lambda v: v.tensor_tensor(out=oT[:, qc0:qc0 + 128], in0=ps[:, pvb, 0:128],
                                                   in1=rden[pv][:], op=ALU.mult),
                  reads=[P + (pvb,), (name, "rden", pv)], writes=[okh])
        kb.dma("sp", oTs[hs], oscr[h, :, 0:nq], oT[:, 0:nq], reads=[okh], writes=[name + "_oscr"], partial=True)
    lds = DmaSem(kb, "a_old")
    for h in range(NH):
        kb.dma("sp", lds, hT[:, h, 0:nq], oscr[h, :, 0:nq], reads=[name + "_oscr"], writes=[hkey], partial=(h > 0))
    groups = [(c0 - qb0 * 128, c0, c0, n, 0) for (c0, n) in _cols_split(qb0 * 128, qb1 * 128, 512)]
    if ctx_q:
        groups.append((nqb * 128, CTXC, CTXC, NCTX, 1))
    emit_outproj(kb, opj, cst, hT, hkey, NH, wo, groups, src, dst, ps, (5, 6))


def emit_ada(kb, G, l, modt, modkey, wa):
    ps = G.ps
    w = [kb.sbuf("ada_w%d" % i, [128, KD, 512], BF16) for i in range(2)]
    ws = [DmaSem(kb, "ada_ws%d" % i) for i in range(2)]
    kb.dma("pool", ws[0], w[0][:], wa[l, 0], writes=["ada_w0"])
    for gi in range(ADA_NG):
        if gi + 1 < ADA_NG:
            kb.dma("pool", ws[(gi + 1) % 2], w[(gi + 1) % 2][:], wa[l, gi + 1], writes=["ada_w%d" % ((gi + 1) % 2)])
        wt = w[gi % 2]
        for c4 in range(4):
            ch = gi * 4 + c4
            bk = ch % 4
            for k in range(KD):
                kb.op("pe", lambda t: t.matmul(ps[:, bk, 0:2], lhsT=wt[:, k, c4 * 128:(c4 + 1) * 128],
                                               rhs=G.sg[:, k, :], start=(k == 0), stop=(k == KD - 1)),
                      reads=["ada_w%d" % (gi % 2), "G_sg"], writes=[("P", "ps", bk)], inc=(k == KD - 1))
            kb.op("act", lambda a_: a_.activation(out=modt[:, :, ch // 16, ch % 16], in_=ps[:, bk, 0:2],
                                                  func=AF.Identity, bias=G.bada[:, l, ch:ch + 1]),
                  reads=[("P", "ps", bk), "G_bada"], writes=[modkey])


class Globals:
    pass


R_FFN0 = (384, 3200)
R_GM1 = (384, 3200)
R_FFN1 = (446, 3138)
R_SC2 = (447, 3137)
R_FFN2 = (448, 3136)
NA_SPECIAL = {6: (-4, 6, 0), 7: (-4, 5, 1), 20: (-4, 5, 2), 21: (-6, 6, 3)}
NA0_PASSES = [(2, 14, 0, 2048, True), (14, 26, 1536, 2048, False)]
NA3_PASSES = [(5, 14, 384, 1664, False), (14, 23, 1536, 1664, False)]


def build_fused(nc, n_phases=99, dbg_out=None):
    def din(name, shape, dt=F32):
        return nc.dram_tensor(name, shape, dt, kind="ExternalInput").ap()
    xin = din("xin", [128, KD, NCA])
    cmask = din("cmask", [128, NCA])
    cond = din("cond", [128, KD, 2])
    wa = din("wa", [4, ADA_NG, 128, KD, 512])
    ba = din("ba", [128, 4, 96])
    ngm = din("ngm", [4, 128, 16])
    ngf = din("ngf", [4, 128, 16])
    na_wqkv = din("na_wqkv", [2, NH, 128, 3, KD, 128])
    na_qkg = din("na_qkg", [2, 128, 2])
    na_bI = din("na_bI", [2, NH, 128, 5, 128])
    na_bS = din("na_bS", [2, 4, NH, 128, 6, 128])
    na_wo = din("na_wo", [2, KD, 128, NH, 128])
    gm_wu = din("gm_wu", [KG, 128, 1, KD, 128])
    gm_wv = din("gm_wv", [8, 128, KD, 512])
    gm_vg = din("gm_vg", [128, KG])
    gm_wsT = din("gm_wsT", [128, 16, 128])
    gm_bsb = din("gm_bsb", [128, 16, 128])
    gm_wout = din("gm_wout", [KD, 128, KG, 128])
    sc_win = din("sc_win", [KD, 128, 3, KD, 128])
    sc_cw = din("sc_cw", [128, KD, 3])
    sc_wout = din("sc_wout", [KD, 128, KD, 128])
    f_wup = din("f_wup", [4, KF, 128, 2, KD, 128])
    f_cwb = din("f_cwb", [4, 128, 2, KF, 4])
    f_wdn = din("f_wdn", [4, KD, 128, KF, 128])
    yT = nc.dram_tensor("yT", [128, KD, NLAT], F32, kind="ExternalOutput").ap()
    kind = "ExternalOutput" if dbg_out else "Internal"
    XA = nc.dram_tensor("XA", [128, KD, NCA], F32, kind=kind).ap()
    XB = nc.dram_tensor("XB", [128, KD, NCA], F32, kind=kind).ap()
    oscr = nc.dram_tensor("oscr", [NH, 128, 12 * 128 + NCTX], BF16, kind="Internal").ap()
    with ExitStack() as es:
        kb = KB(nc, es)
        G = Globals()
        G.ps = kb.psum("ps", [128, 8, 512], F32)
        G.cmask = cmask
        G.ones = kb.sbuf("G_ones", [128, 128], BF16)
        G.okey = "G_ones"
        idf = kb.sbuf("G_idf", [128, 128], F32)
        G.idn = kb.sbuf("G_idn", [128, 128], BF16)
        G.ikey = "G_idn"
        cs = kb.sbuf("G_cs", [128, KD, 2], F32)
        G.sg = kb.sbuf("G_sg", [128, KD, 2], BF16)
        G.bada = kb.sbuf("G_bada", [128, 4, 96], F32)
        modt = [kb.sbuf("G_mod%d" % l, [128, 2, 6, 16], F32) for l in range(4)]
        kb.op("dve", lambda v: v.memset(G.ones[:], 1.0), writes=["G_ones"])
        kb.op("pool", lambda g_: g_.memset(idf[:], 1.0), writes=["G_idf"])
        kb.op("pool", lambda g_: g_.affine_select(out=idf[:], in_=idf[:], pattern=[[-1, 128]], compare_op=ALU.is_equal,
                                                 fill=0.0, base=0, channel_multiplier=1),
              reads=["G_idf"], writes=["G_idf"])
        kb.op("dve", lambda v: v.tensor_copy(out=G.idn[:], in_=idf[:]), reads=["G_idf"], writes=["G_idn"])
        kb.dma("sp", DmaSem(kb, "g_d1"), cs[:], cond, writes=["G_cs"])
        kb.dma("sp", DmaSem(kb, "g_d2"), G.bada[:], ba, writes=["G_bada"])
        kb.op("act", lambda a_: a_.activation(out=G.sg[:], in_=cs[:], func=AF.Silu), reads=["G_cs"], writes=["G_sg"])
        kb.barrier()
        ph = [0]

        def phase(fn):
            ph[0] += 1
            if ph[0] > n_phases:
                return
            with ExitStack() as pes:
                kb.cur_es = pes
                fn()
                kb.barrier()
            kb.cur_es = es

        def mixer_consts(l):
            return make_consts(kb, "cm%d" % l, modt[l], "G_mod%d" % l, ngm[l], 0)

        def ffn_consts(l):
            return make_consts(kb, "cf%d" % l, modt[l], "G_mod%d" % l, ngf[l], 1)

        phase(lambda: emit_ada(kb, G, 0, modt[0], "G_mod0", wa))
        for (qb0, qb1, kv0, kvlen, cq) in NA0_PASSES:
            phase(lambda: emit_na(kb, G, mixer_consts(0), xin, XA, qb0, qb1, kv0, kvlen, cq, NA_SPECIAL,
                                  na_wqkv[0], na_qkg[0], na_bI[0], na_bS[0], na_wo[0], oscr))
        phase(lambda: emit_ffn(kb, G, ffn_consts(0), XA, XB, 0, R_FFN0[0], R_FFN0[1], True,
                               f_wup[0], f_cwb[0], f_wdn[0]))
        phase(lambda: emit_ada(kb, G, 1, modt[1], "G_mod1", wa))
        phase(lambda: emit_gm(kb, G, mixer_consts(1), XB, XA, R_GM1[0], R_GM1[1], True,
                              gm_wu, gm_wv, gm_vg, gm_wsT, gm_bsb, gm_wout))
        phase(lambda: emit_ffn(kb, G, ffn_consts(1), XA, XB, 0, R_FFN1[0], R_FFN1[1], True,
                               f_wup[1], f_cwb[1], f_wdn[1]))
        phase(lambda: emit_ada(kb, G, 2, modt[2], "G_mod2", wa))
        phase(lambda: emit_sc(kb, G, mixer_consts(2), XB, XA, R_SC2[0], R_SC2[1], True, sc_win, sc_cw, sc_wout))
        phase(lambda: emit_ffn(kb, G, ffn_consts(2), XA, XB, 0, R_FFN2[0], R_FFN2[1], True,
                               f_wup[2], f_cwb[2], f_wdn[2]))
        phase(lambda: emit_ada(kb, G, 3, modt[3], "G_mod3", wa))
        for (qb0, qb1, kv0, kvlen, cq) in NA3_PASSES:
            phase(lambda: emit_na(kb, G, mixer_consts(3), XB, XA, qb0, qb1, kv0, kvlen, cq, NA_SPECIAL,
                                  na_wqkv[1], na_qkg[1], na_bI[1], na_bS[1], na_wo[1], oscr))
        phase(lambda: emit_ffn(kb, G, ffn_consts(3), XA, yT, OWN0, OWN0, OWN1, False,
                               f_wup[3], f_cwb[3], f_wdn[3]))
        kb.finish()
        print("fused program: ins=%d waits=%d sems=%d" % (kb.n_ins, kb.n_wait, kb.nsem))
    return nc


def fm(v):
    v = np.asarray(v, np.float32)
    k = v.shape[-1] // 128
    r = v.reshape(v.shape[:-1] + (k, 128))
    return np.ascontiguousarray(np.moveaxis(r, -1, 0))


def tile_wup(w_up):
    w = w_up.reshape(KD, 128, 2, KF, 128)
    return np.ascontiguousarray(w.transpose(3, 1, 2, 0, 4))


def tile_wdn(w_down):
    nk = w_down.shape[0] // 128
    w = w_down.reshape(nk, 128, KD, 128)
    return np.ascontiguousarray(w.transpose(2, 1, 0, 3))


def tile_win(w_in, nt):
    nj = w_in.shape[1] // (nt * 128)
    w = w_in.reshape(KD, 128, nt, nj, 128)
    return np.ascontiguousarray(w.transpose(3, 1, 2, 0, 4))


def _unused_tile_ada(w_ada_all, b_ada_all, core):
    L = w_ada_all.shape[0]
    per = L * 12288 // NCORES
    cpl = 12288 // per
    l, part = core // cpl, core % cpl
    w = w_ada_all[l][:, part * per:(part + 1) * per]
    w = w.reshape(KD, 128, ADA_G, 512)
    wt = np.ascontiguousarray(w.transpose(2, 1, 0, 3))
    b = b_ada_all[l][part * per:(part + 1) * per].reshape(ADA_CH, 128)
    return wt, np.ascontiguousarray(b.T)


def tile_cwb(conv_w, conv_b):
    a = np.concatenate([conv_w, conv_b[None]], 0)
    a = a.reshape(4, 2, KF, 128)
    return np.ascontiguousarray(a.transpose(3, 1, 2, 0))


def from_out(yT):
    return np.ascontiguousarray(yT.transpose(2, 1, 0).reshape(yT.shape[2], D))


def gm_weights(w_in, v_g, w_s, b_s, w_out):
    wu = tile_win(w_in[:, :GMW], 1)
    wv = w_in[:, GMW:].reshape(KD, 128, 8, 512).transpose(2, 1, 0, 3)
    return {
        "wu": wu,
        "wv": np.ascontiguousarray(wv),
        "vg": fm(v_g),
        "wsT": np.ascontiguousarray(w_s.transpose(2, 0, 1)),
        "bsb": np.ascontiguousarray(np.broadcast_to(b_s[None], (128, 16, 128))),
        "wout": tile_wdn(w_out),
    }


def _na_block_bias(rpb, r0, start, npair, rows=128):
    qq = np.arange(128)
    q_row, q_col = r0 + qq // 64, qq % 64
    kk = np.arange(128)
    r_start = np.clip(q_row - 4, 0, rows - 8)
    c_start = np.clip(q_col - 8, 0, 64 - 16)
    out = np.full((NH, 128, 6, 128), -30000.0, np.float32)
    for p in range(npair):
        k_row = (r0 + start + 2 * p + kk // 64)[:, None]
        k_col = (kk % 64)[:, None]
        ok = ((k_row >= 0) & (k_row < rows) & (k_row >= r_start[None]) & (k_row < r_start[None] + 8)
              & (k_col >= c_start[None]) & (k_col < c_start[None] + 16))
        dr = np.clip(k_row - q_row[None], -7, 7) + 7
        dc = np.clip(k_col - q_col[None], -15, 15) + 15
        out[:, :, p, :] = np.where(ok[None], rpb[:, dr, dc], np.float32(-30000.0))
    return out


def na_bias(rpb, core):
    R0 = (core % 4) * 32
    bI = _na_block_bias(rpb, 64, -4, 5)[:, :, 0:5, :]
    bS = np.stack([
        _na_block_bias(rpb, R0 + 0, -4, 6),
        _na_block_bias(rpb, R0 + 2, -4, 5),
        _na_block_bias(rpb, R0 + 28, -4, 5),
        _na_block_bias(rpb, R0 + 30, -6, 6),
    ], 0)
    return np.ascontiguousarray(bI), np.ascontiguousarray(bS)


def ext_cols(xfull, ctxfull, core):
    b, q = core // 4, core % 4
    t0 = q * NLAT - OWN0
    ext = np.zeros((NCA, D), np.float32)
    lo, hi = max(t0, 0), min(t0 + NE, xfull.shape[1])
    ext[lo - t0:hi - t0] = xfull[b, lo:hi]
    ext[CTXC:] = ctxfull[b]
    return np.ascontiguousarray(ext.reshape(NCA, KD, 128).transpose(2, 1, 0))


def cmask_for(core, seq):
    q = core % 4
    t0 = q * NLAT - OWN0
    g = t0 + np.arange(NE)
    m = np.ones((NCA,), np.float32)
    m[:NE] = ((g >= 0) & (g < seq)).astype(np.float32)
    return np.ascontiguousarray(np.broadcast_to(m[None], (128, NCA)))


_PROG = {}


def kernel(x, c, ctx, c_ctx, norm_mix_g, norm_ffn_g, w_ada, b_ada,
           na_w_qkv, na_q_g, na_k_g, na_rpb, na_w_o,
           gm_w_in, gm_v_g, gm_w_s, gm_b_s, gm_w_out,
           sc_w_in, sc_conv_w, sc_w_out,
           ffn_w_up, ffn_conv_w, ffn_conv_b, ffn_w_down):
    f32 = lambda a: np.ascontiguousarray(np.asarray(a, np.float32))
    x, c, ctx, c_ctx = f32(x), f32(c), f32(ctx), f32(c_ctx)
    w_ada, b_ada = f32(w_ada), f32(b_ada)
    if "nc" not in _PROG:
        nc = bass.Bass("TRN2", target_bir_lowering=False)
        build_fused(nc)
        _PROG["nc"] = nc
    nc = _PROG["nc"]
    shared = {
        "wa": np.ascontiguousarray(w_ada.reshape(4, KD, 128, ADA_NG, 512).transpose(0, 3, 2, 1, 4)),
        "ba": fm(b_ada),
        "ngm": np.ascontiguousarray(fm(f32(norm_mix_g)).transpose(1, 0, 2)),
        "ngf": np.ascontiguousarray(fm(f32(norm_ffn_g)).transpose(1, 0, 2)),
        "na_wqkv": np.stack([tile_win(f32(na_w_qkv[j]), 3) for j in range(2)], 0),
        "na_qkg": np.ascontiguousarray(np.stack([f32(na_q_g), f32(na_k_g)], -1)),
        "na_wo": np.stack([tile_wdn(f32(na_w_o[j])) for j in range(2)], 0),
        "sc_win": tile_win(f32(sc_w_in[0]), 3),
        "sc_cw": np.ascontiguousarray(fm(f32(sc_conv_w[0])).transpose(0, 2, 1)),
        "sc_wout": tile_wdn(f32(sc_w_out[0])),
        "f_wup": np.stack([tile_wup(f32(ffn_w_up[i])) for i in range(4)], 0),
        "f_cwb": np.stack([tile_cwb(f32(ffn_conv_w[i]), f32(ffn_conv_b[i])) for i in range(4)], 0),
        "f_wdn": np.stack([tile_wdn(f32(ffn_w_down[i])) for i in range(4)], 0),
    }
    gw = gm_weights(f32(gm_w_in[0]), f32(gm_v_g[0]), f32(gm_w_s[0]), f32(gm_b_s[0]), f32(gm_w_out[0]))
    for k, v in gw.items():
        shared["gm_" + k] = v
    rpb = f32(na_rpb)
    in_maps = []
    for core in range(NCORES):
        b = core // 4
        cond = np.stack([c[b], c_ctx], 0)
        bias = [na_bias(rpb[j], core) for j in range(2)]
        d = dict(shared)
        d["xin"] = ext_cols(x, ctx, core)
        d["cmask"] = cmask_for(core, x.shape[1])
        d["cond"] = np.ascontiguousarray(fm(cond).transpose(0, 2, 1))
        d["na_bI"] = np.stack([bias[0][0], bias[1][0]], 0)
        d["na_bS"] = np.stack([bias[0][1], bias[1][1]], 0)
        in_maps.append(d)
    res = run_bass_kernel_spmd(nc, in_maps, core_ids=list(range(NCORES)))
    out = np.empty_like(x)
    for core in range(NCORES):
        b, q = core // 4, core % 4
        out[b, q * NLAT:(q + 1) * NLAT] = from_out(np.asarray(res.results[core]["yT"]))
    return out
```

```python
import numpy as np
from contextlib import ExitStack
import concourse.bass as bass
import concourse.mybir as mybir
from concourse.bass_utils import run_bass_kernel_spmd

F32 = mybir.dt.float32
BF16 = mybir.dt.bfloat16
AF = mybir.ActivationFunctionType
ALU = mybir.AluOpType

D = 2048
KD = 16
FF = 5504
KF = 43
NLAT = 2048
NCTX = 256
EPS = 1e-6
NCORES = 8

SEM_ROT = 30000


class _DmaSem:
    def __init__(self, kb, name):
        self.kb = kb
        self.name = name
        self.sem = kb.newsem(name)
        self.n = 0

    def next(self):
        if (self.n + 1) * 16 > 60000:
            self.sem = self.kb.newsem(self.name)
            self.n = 0
        self.n += 1
        return self.sem, self.n * 16


def DmaSem(kb, name):
    d = kb.dsem_pool.get(name)
    if d is None:
        d = kb.dsem_pool[name] = _DmaSem(kb, name)
    return d


class KB:
    def __init__(self, nc, es):
        self.nc = nc
        self.es = es
        self.eng = {"pe": nc.tensor, "act": nc.scalar, "dve": nc.vector, "pool": nc.gpsimd, "sp": nc.sync}
        self.nsem = 0
        self.dsem_pool = {}
        self.cur_es = es
        self.nbuf = 0
        self.csem = {}
        self.cnt = {}
        for e in ("pe", "act", "dve", "pool"):
            self.csem[e] = self.newsem("c_" + e)
            self.cnt[e] = 0
        self.waited = {e: {} for e in self.eng}
        self.res = {}
        self.pending = {e: [] for e in self.eng}
        self.store_toks = []
        self.n_wait = 0
        self.n_ins = 0

    def newsem(self, name):
        self.nsem += 1
        return self.es.enter_context(self.nc.semaphore("%s_%d" % (name, self.nsem)))

    def sbuf(self, name, shape, dt):
        self.nbuf += 1
        return self.cur_es.enter_context(self.nc.sbuf_tensor("%s_%d" % (name, self.nbuf), shape, dt))

    def barrier(self):
        for e in self.pending:
            assert not self.pending[e], e
        toks = [(self.csem[e], self.cnt[e], e) for e in self.csem if self.cnt[e] > 0]
        toks += [(d.sem, d.n * 16, "dma") for d in self.dsem_pool.values() if d.n > 0]
        for e in self.eng:
            for tok in toks:
                self._wait(e, tok, True)
        self.res = {}
        self.store_toks = []

    def psum(self, name, shape, dt):
        return self.es.enter_context(self.nc.psum_tensor(name, shape, dt))

    def _wait(self, e, tok, raw):
        sem, val, src = tok
        if src == e and not raw:
            return
        w = self.waited[e]
        if w.get(id(sem), 0) >= val:
            return
        self.eng[e].wait_ge(sem, val)
        self.n_wait += 1
        w[id(sem)] = val

    def _deps(self, e, reads, writes):
        for k in reads:
            st = self.res.get(k)
            if st:
                for tok in st["w"]:
                    self._wait(e, tok, True)
                if isinstance(k, tuple) and len(k) > 1 and k[1] == "ps":
                    for tok in st["r"]:
                        self._wait(e, tok, False)
        for k in writes:
            st = self.res.get(k)
            if st:
                for tok in st["w"]:
                    self._wait(e, tok, False)
                for tok in st["r"]:
                    self._wait(e, tok, False)

    def _reg(self, tok, reads, writes):
        for k in writes:
            self.res[k] = {"w": [tok], "r": []}
        for k in reads:
            st = self.res.get(k)
            if st is None:
                st = self.res[k] = {"w": [], "r": []}
            st["r"].append(tok)
            if len(st["r"]) > 64:
                st["r"] = st["r"][-32:]

    def op(self, e, fn, reads=(), writes=(), inc=True):
        self._deps(e, reads, writes)
        ins = fn(self.eng[e])
        self.n_ins += 1
        if not inc:
            self.pending[e].append((tuple(reads), tuple(writes)))
            return ins
        if self.cnt[e] >= SEM_ROT:
            self.csem[e] = self.newsem("c_" + e)
            self.cnt[e] = 0
        self.cnt[e] += 1
        ins.then_inc(self.csem[e], 1)
        tok = (self.csem[e], self.cnt[e], e)
        for (r, w) in self.pending[e]:
            self._reg(tok, r, w)
        self.pending[e] = []
        self._reg(tok, reads, writes)
        return ins

    def dma(self, q, dsem, out, in_, reads=(), writes=(), store=False, partial=False):
        self._deps(q, reads, writes)
        ins = self.eng[q].dma_start(out=out, in_=in_)
        self.n_ins += 1
        sem, val = dsem.next()
        ins.then_inc(sem, 16)
        tok = (sem, val, "dma")
        if partial:
            for k in writes:
                st = self.res.get(k)
                if st is None:
                    st = self.res[k] = {"w": [], "r": []}
                st["w"] = [t for t in st["w"] if t[0] is not sem] + [tok]
            self._reg(tok, reads, ())
        else:
            self._reg(tok, reads, writes)
        if store:
            self.store_toks.append(tok)
        return ins

    def finish(self):
        for tok in self.store_toks:
            self._wait("sp", tok, True)
        self.store_toks = []


NE = 3584
OWN0 = 768
OWN1 = OWN0 + NLAT
CTXC = NE
NCA = NE + NCTX
NH = 16
GMW = 4096
KG = 32
ADA_NG = 24


def _split_even(a, b, maxn):
    n = b - a
    k = -(-n // maxn)
    out = []
    c = a
    for i in range(k):
        m = n // k + (1 if i < n % k else 0)
        out.append((c, m))
        c += m
    return out


def _cols_split(c0, c1, step):
    out = []
    c = c0
    while c < c1:
        e = min(c + step, c1)
        out.append((c, e - c))
        c = e
    return out


class Consts:
    pass


def make_consts(kb, name, modt, modkey, ng_ap, which):
    c = Consts()
    c.mod = modt
    c.ng = kb.sbuf(name + "_ng", [128, 16], F32)
    c.A = kb.sbuf(name + "_A", [128, 2, 16], F32)
    kb.dma("sp", DmaSem(kb, "cs_ng"), c.ng[:], ng_ap, writes=[name + "_ng"])
    o = 3 * which
    for s in range(2):
        kb.op("dve", lambda v: v.scalar_tensor_tensor(out=c.A[:, s, :], in0=c.mod[:, s, o + 1, :], scalar=1.0,
                                                     in1=c.ng[:], op0=ALU.add, op1=ALU.mult),
              reads=[modkey, name + "_ng"], writes=[name + "_A%d" % s])
    c.Akey = lambda s: name + "_A%d" % s
    c.Bap = lambda s, k: c.mod[:, s, o + 0, k:k + 1]
    c.Gap = lambda s, k: c.mod[:, s, o + 2, k:k + 1]
    c.Aap = lambda s, k: c.A[:, s, k:k + 1]
    c.modkey = modkey
    return c


class NormBufs:
    def __init__(self, kb, name, w, ones, okey):
        self.w = w
        self.xn = [kb.sbuf("%s_xn%d" % (name, i), [128, KD, w], F32) for i in range(2)]
        self.sq = kb.sbuf(name + "_sq", [128, KD, w], BF16)
        self.rs = kb.sbuf(name + "_rs", [128, w], F32)
        self.cm = kb.sbuf(name + "_cm", [128, w], F32)
        self.ones = ones
        self.okey = okey
        self.ds = [DmaSem(kb, "nb_xs%d" % i) for i in range(2)]
        self.dcm = DmaSem(kb, "nb_cm")
        self.name = name
        self.i = 0


def emit_norm(kb, nb, cst, x_ap, cmask_ap, pieces, hT, hkey, ps_ap, pskey):
    name = nb.name
    for (c0, n, s, hc, masked) in pieces:
        slot = nb.i % 2
        nb.i += 1
        xn = nb.xn[slot]
        xk = "%s_xn%d" % (name, slot)
        kb.dma("sp", nb.ds[slot], xn[:, :, 0:n], x_ap[:, :, c0:c0 + n], writes=[xk])
        if masked:
            kb.dma("sp", nb.dcm, nb.cm[:, 0:n], cmask_ap[:, c0:c0 + n], writes=[name + "_cm"])
        kb.op("act", lambda a: a.activation(out=nb.sq[:, :, 0:n], in_=xn[:, :, 0:n], func=AF.Square),
              reads=[xk], writes=[name + "_sq"])
        for k in range(KD):
            kb.op("pe", lambda t: t.matmul(ps_ap[:, 0:n], lhsT=nb.ones[:], rhs=nb.sq[:, k, 0:n],
                                           start=(k == 0), stop=(k == KD - 1)),
                  reads=[name + "_sq", nb.okey], writes=[pskey], inc=(k == KD - 1))
        kb.op("act", lambda a: a.activation(out=nb.rs[:, 0:n], in_=ps_ap[:, 0:n], func=AF.Sqrt,
                                            scale=1.0 / D, bias=EPS),
              reads=[pskey], writes=[name + "_rs"])
        kb.op("dve", lambda v: v.reciprocal(out=nb.rs[:, 0:n], in_=nb.rs[:, 0:n]),
              reads=[name + "_rs"], writes=[name + "_rs"])
        if masked:
            pass
        kb.op("dve", lambda v: v.tensor_tensor(out=xn[:, :, 0:n], in0=xn[:, :, 0:n],
                                               in1=nb.rs[:, 0:n].unsqueeze(1).broadcast_to([128, KD, n]),
                                               op=ALU.mult),
              reads=[xk, name + "_rs"], writes=[xk])
        for k in range(KD):
            kb.op("act", lambda a: a.activation(out=hT[:, k, hc:hc + n], in_=xn[:, k, 0:n],
                                                func=AF.Identity, scale=cst.Aap(s, k), bias=cst.Bap(s, k)),
                  reads=[xk, cst.Akey(s), cst.modkey], writes=[hkey])
        if masked:
            kb.op("dve", lambda v: v.tensor_tensor(out=hT[:, :, hc:hc + n], in0=hT[:, :, hc:hc + n],
                                                   in1=nb.cm[:, 0:n].unsqueeze(1).broadcast_to([128, KD, n]),
                                                   op=ALU.mult),
                  reads=[hkey, name + "_cm"], writes=[hkey])


class OutProj:
    def __init__(self, kb, name, kmax):
        self.name = name
        self.wd = [kb.sbuf("%s_wd%d" % (name, i), [128, kmax, 128], BF16) for i in range(2)]
        self.wds = [DmaSem(kb, "op_wds%d" % i) for i in range(2)]
        self.xr = [kb.sbuf("%s_xr%d" % (name, i), [128, 512], F32) for i in range(2)]
        self.xrs = [DmaSem(kb, "op_xrs%d" % i) for i in range(2)]
        self.ot = [kb.sbuf("%s_ot%d" % (name, i), [128, 512], F32) for i in range(2)]
        self.ots = [DmaSem(kb, "op_ots%d" % i) for i in range(2)]
        self.wd_i = 0
        self.dn = 0


def emit_outproj(kb, o, cst, g, gkey, nk, wdn, groups, xT, yT, ps, banks):
    name = o.name

    def load_wd(dc, wd_i):
        slot = wd_i % 2
        kb.dma("pool", o.wds[slot], o.wd[slot][:, 0:nk, :], wdn[dc, :, 0:nk, :], writes=[(name, "wd", slot)])

    load_wd(0, o.wd_i)
    for dc in range(KD):
        if dc + 1 < KD:
            load_wd(dc + 1, o.wd_i + 1)
        slot = o.wd_i % 2
        wkey = (name, "wd", slot)
        wdt = o.wd[slot]
        for (g0, xc, oc, n, s) in groups:
            r = o.dn % 2
            o.dn += 1
            b = banks[r]
            xr, ot = o.xr[r], o.ot[r]
            kb.dma("sp", o.xrs[r], xr[:, 0:n], xT[:, dc, xc:xc + n], writes=[(name, "xr", r)])
            for j in range(nk):
                kb.op("pe", lambda t: t.matmul(ps[:, b, 0:n], lhsT=wdt[:, j, :], rhs=g[:, j, g0:g0 + n],
                                               start=(j == 0), stop=(j == nk - 1)),
                      reads=[wkey, gkey], writes=[("P", "ps", b)], inc=(j == nk - 1))
            kb.op("dve", lambda v: v.scalar_tensor_tensor(out=ot[:, 0:n], in0=ps[:, b, 0:n],
                                                          scalar=cst.Gap(s, dc), in1=xr[:, 0:n],
                                                          op0=ALU.mult, op1=ALU.add),
                  reads=[("P", "ps", b), (name, "xr", r), cst.modkey], writes=[(name, "ot", r)])
            kb.dma("sp", o.ots[r], yT[:, dc, oc:oc + n], ot[:, 0:n], reads=[(name, "ot", r)], store=True)
        o.wd_i += 1


def _conv_supers(a, b, with_ctx, cap):
    tot = (b - a) + (NCTX if with_ctx else 0)
    ns = -(-tot // cap)
    per = -(-tot // ns)
    supers = []
    c = a
    for si in range(ns):
        room = per if si < ns - 1 else cap
        tiles = []
        nl = min(room, b - c)
        if nl > 0:
            tiles += [("lat", c0, n) for (c0, n) in _split_even(c, c + nl, 510)]
            c += nl
            room -= nl
        supers.append(tiles)
    if with_ctx:
        used = sum(t[2] for t in supers[-1])
        if used + NCTX <= cap:
            supers[-1].append(("ctx", CTXC, NCTX))
        else:
            supers.append([("ctx", CTXC, NCTX)])
    assert c == b
    return supers


def _conv_plan(tiles):
    descs, spans, zero = [], [], []
    h = 0
    lat = [t for t in tiles if t[0] == "lat"]
    if lat:
        lo = lat[0][1] - 1
        hi = lat[-1][1] + lat[-1][2] + 1
        spans.append((lo, hi - lo, 0, h))
        for (_, c0, n) in lat:
            descs.append((c0, n, h + (c0 - 1 - lo), 0))
        h += hi - lo
    for (kind, c0, n) in tiles:
        if kind == "ctx":
            zero += [h, h + n + 1]
            spans.append((c0, n, 1, h + 1))
            descs.append((c0, n, h, 1))
            h += n + 2
    return descs, spans, zero, h


def _norm_pieces(spans, w):
    pieces = []
    for (c0, n, s, hc) in spans:
        for (p0, pn) in _cols_split(c0, c0 + n, w):
            masked = (s == 0) and (p0 < OWN0 or p0 + pn > OWN1)
            pieces.append((p0, pn, s, hc + (p0 - c0), masked))
    return pieces


def emit_ffn(kb, G, cst, src, dst, dst_off, a, b, with_ctx, wup, cwb, wdn):
    name = "f"
    ps = G.ps
    cw = kb.sbuf(name + "_cw", [128, 2, KF, 4], F32)
    kb.dma("sp", DmaSem(kb, "f_c2"), cw[:], cwb, writes=[name + "_cw"])
    nb = NormBufs(kb, name + "n", 64, G.ones, G.okey)
    CAP = 1024
    WMAX = CAP + 8
    hT = kb.sbuf(name + "_hT", [128, KD, WMAX], BF16)
    g = kb.sbuf(name + "_g", [128, KF, CAP], BF16)
    NWB = 2
    wu = [kb.sbuf("%s_wu%d" % (name, i), [128, 2, KD, 128], BF16) for i in range(NWB)]
    wus = [DmaSem(kb, "f_wus%d" % i) for i in range(NWB)]
    ya = [kb.sbuf("%s_ya%d" % (name, i), [128, 512], F32) for i in range(2)]
    yb = [kb.sbuf("%s_yb%d" % (name, i), [128, 512], F32) for i in range(2)]
    sg = [kb.sbuf("%s_sg%d" % (name, i), [128, 512], F32) for i in range(2)]
    opj = OutProj(kb, name + "o", KF)
    hkey, gkey, cwk = name + "_hT", name + "_g", name + "_cw"
    zb = ev = wu_i = 0
    for tiles in _conv_supers(a, b, with_ctx, CAP):
        descs, spans, zero, width = _conv_plan(tiles)
        assert width <= WMAX, width
        emit_norm(kb, nb, cst, src, G.cmask, _norm_pieces(spans, nb.w), hT, hkey, ps[:, 6, :], ("P", "ps", 6))
        for zc in zero:
            kb.op("dve", lambda v: v.memset(hT[:, :, zc:zc + 1], 0.0), writes=[hkey])
        gcol = {}
        gc = 0
        for (c0, n, hb, s) in descs:
            gcol[c0] = gc
            gc += n
        assert gc <= CAP

        def load_wu(j, i):
            slot = i % NWB
            kb.dma("pool", wus[slot], wu[slot][:], wup[j], writes=[(name, "wu", slot)])

        load_wu(0, wu_i)
        for j in range(KF):
            if j + 1 < KF:
                load_wu(j + 1, wu_i + 1)
            slot = wu_i % NWB
            wkey = (name, "wu", slot)
            for (c0, n, hb, s) in descs:
                banks = []
                for half in range(2):
                    bk = zb % 6
                    zb += 1
                    banks.append(bk)
                    for k in range(KD):
                        kb.op("pe", lambda t: t.matmul(ps[:, bk, 0:n + 2], lhsT=wu[slot][:, half, k, :],
                                                       rhs=hT[:, k, hb:hb + n + 2],
                                                       start=(k == 0), stop=(k == KD - 1)),
                              reads=[wkey, hkey], writes=[("P", "ps", bk)], inc=(k == KD - 1))
                e = ev % 2
                ev += 1
                bg, bu = banks
                kg, ku = ("P", "ps", bg), ("P", "ps", bu)
                kb.op("act", lambda a_: a_.activation(out=ya[e][:, 0:n], in_=ps[:, bg, 1:n + 1], func=AF.Identity,
                                                      scale=cw[:, 0, j, 1:2], bias=cw[:, 0, j, 3:4]),
                      reads=[kg, cwk], writes=[(name, "ya", e)])
                kb.op("act", lambda a_: a_.activation(out=yb[e][:, 0:n], in_=ps[:, bu, 1:n + 1], func=AF.Identity,
                                                      scale=cw[:, 1, j, 1:2], bias=cw[:, 1, j, 3:4]),
                      reads=[ku, cwk], writes=[(name, "yb", e)])
                for (tap, off) in ((0, 0), (2, 2)):
                    kb.op("dve", lambda v: v.scalar_tensor_tensor(
                        out=ya[e][:, 0:n], in0=ps[:, bg, off:off + n], scalar=cw[:, 0, j, tap:tap + 1],
                        in1=ya[e][:, 0:n], op0=ALU.mult, op1=ALU.add),
                        reads=[kg, cwk, (name, "ya", e)], writes=[(name, "ya", e)])
                    kb.op("dve", lambda v: v.scalar_tensor_tensor(
                        out=yb[e][:, 0:n], in0=ps[:, bu, off:off + n], scalar=cw[:, 1, j, tap:tap + 1],
                        in1=yb[e][:, 0:n], op0=ALU.mult, op1=ALU.add),
                        reads=[ku, cwk, (name, "yb", e)], writes=[(name, "yb", e)])
                kb.op("act", lambda a_: a_.activation(out=sg[e][:, 0:n], in_=ya[e][:, 0:n], func=AF.Silu),
                      reads=[(name, "ya", e)], writes=[(name, "sg", e)])
                g0 = gcol[c0]
                kb.op("dve", lambda v: v.tensor_tensor(out=g[:, j, g0:g0 + n], in0=yb[e][:, 0:n], in1=sg[e][:, 0:n],
                                                       op=ALU.mult),
                      reads=[(name, "yb", e), (name, "sg", e)], writes=[gkey])
            wu_i += 1
        groups = [(gcol[c0], c0, c0 - dst_off, n, s) for (c0, n, hb, s) in descs]
        emit_outproj(kb, opj, cst, g, gkey, KF, wdn, groups, src, dst, ps, (6, 7))


def emit_sc(kb, G, cst, src, dst, a, b, with_ctx, win, cwd, wout):
    name = "s"
    ps = G.ps
    cw = kb.sbuf(name + "_cw", [128, KD, 3], F32)
    kb.dma("sp", DmaSem(kb, "s_c2"), cw[:], cwd, writes=[name + "_cw"])
    nb = NormBufs(kb, name + "n", 64, G.ones, G.okey)
    CAP = 1024
    WMAX = CAP + 8
    hT = kb.sbuf(name + "_hT", [128, KD, WMAX], BF16)
    g = kb.sbuf(name + "_g", [128, KD, CAP], BF16)
    wu = [kb.sbuf("%s_wu%d" % (name, i), [128, 3, KD, 128], BF16) for i in range(2)]
    wus = [DmaSem(kb, "s_wus%d" % i) for i in range(2)]
    ux = [kb.sbuf("%s_ux%d" % (name, i), [128, 512], F32) for i in range(2)]
    uu = [kb.sbuf("%s_uu%d" % (name, i), [128, 512], F32) for i in range(2)]
    vv = [kb.sbuf("%s_vv%d" % (name, i), [128, 512], F32) for i in range(2)]
    opj = OutProj(kb, name + "o", KD)
    hkey, gkey, cwk = name + "_hT", name + "_g", name + "_cw"
    zb = ev = wi = 0
    for tiles in _conv_supers(a, b, with_ctx, CAP):
        descs, spans, zero, width = _conv_plan(tiles)
        assert width <= WMAX
        emit_norm(kb, nb, cst, src, G.cmask, _norm_pieces(spans, nb.w), hT, hkey, ps[:, 6, :], ("P", "ps", 6))
        for zc in zero:
            kb.op("dve", lambda v: v.memset(hT[:, :, zc:zc + 1], 0.0), writes=[hkey])
        gcol = {}
        gc = 0
        for (c0, n, hb, s) in descs:
            gcol[c0] = gc
            gc += n

        def load_w(j, i):
            slot = i % 2
            kb.dma("pool", wus[slot], wu[slot][:], win[j], writes=[(name, "wu", slot)])

        load_w(0, wi)
        for j in range(KD):
            if j + 1 < KD:
                load_w(j + 1, wi + 1)
            slot = wi % 2
            wkey = (name, "wu", slot)
            for (c0, n, hb, s) in descs:
                banks = []
                for t3 in range(3):
                    bk = zb % 6
                    zb += 1
                    banks.append(bk)
                    for k in range(KD):
                        kb.op("pe", lambda t: t.matmul(ps[:, bk, 0:n + 2], lhsT=wu[slot][:, t3, k, :],
                                                       rhs=hT[:, k, hb:hb + n + 2],
                                                       start=(k == 0), stop=(k == KD - 1)),
                              reads=[wkey, hkey], writes=[("P", "ps", bk)], inc=(k == KD - 1))
                e = ev % 2
                ev += 1
                bb, bc, bx = banks
                kb.op("act", lambda a_: a_.activation(out=ux[e][:, 0:n + 2], in_=ps[:, bx, 0:n + 2], func=AF.Copy),
                      reads=[("P", "ps", bx)], writes=[(name, "ux", e)])
                kb.op("dve", lambda v: v.tensor_tensor(out=uu[e][:, 0:n + 2], in0=ps[:, bc, 0:n + 2],
                                                       in1=ux[e][:, 0:n + 2], op=ALU.mult),
                      reads=[("P", "ps", bc), (name, "ux", e)], writes=[(name, "uu", e)])
                kb.op("act", lambda a_: a_.activation(out=vv[e][:, 0:n], in_=uu[e][:, 1:n + 1], func=AF.Identity,
                                                      scale=cw[:, j, 1:2]),
                      reads=[(name, "uu", e), cwk], writes=[(name, "vv", e)])
                for (tap, off) in ((0, 0), (2, 2)):
                    kb.op("dve", lambda v: v.scalar_tensor_tensor(
                        out=vv[e][:, 0:n], in0=uu[e][:, off:off + n], scalar=cw[:, j, tap:tap + 1],
                        in1=vv[e][:, 0:n], op0=ALU.mult, op1=ALU.add),
                        reads=[(name, "uu", e), cwk, (name, "vv", e)], writes=[(name, "vv", e)])
                g0 = gcol[c0]
                kb.op("dve", lambda v: v.tensor_tensor(out=g[:, j, g0:g0 + n], in0=ps[:, bb, 1:n + 1],
                                                       in1=vv[e][:, 0:n], op=ALU.mult),
                      reads=[("P", "ps", bb), (name, "vv", e)], writes=[gkey])
            wi += 1
        groups = [(gcol[c0], c0, c0, n, s) for (c0, n, hb, s) in descs]
        emit_outproj(kb, opj, cst, g, gkey, KD, wout, groups, src, dst, ps, (6, 7))


def emit_gm(kb, G, cst, src, dst, a, b, with_ctx, wu_d, wv_d, vg_d, wsT_d, bsb_d, wout):
    name = "m"
    ps = G.ps
    vg = kb.sbuf(name + "_vg", [128, KG], F32)
    wsT = kb.sbuf(name + "_wsT", [128, 16, 128], F32)
    bsb = kb.sbuf(name + "_bsb", [128, 16, 128], F32)
    kb.dma("sp", DmaSem(kb, "m_d1"), vg[:], vg_d, writes=[name + "_vg"])
    kb.dma("sp", DmaSem(kb, "m_d2"), wsT[:], wsT_d, writes=[name + "_wsT"])
    kb.dma("sp", DmaSem(kb, "m_d3"), bsb[:], bsb_d, writes=[name + "_bsb"])
    nb = NormBufs(kb, name + "n", 128, G.ones, G.okey)
    TS = 512
    hT = kb.sbuf(name + "_hT", [128, KD, TS], BF16)
    uT = kb.sbuf(name + "_uT", [128, KG, TS], BF16)
    vraw = [kb.sbuf("%s_vr%d" % (name, i), [128, GMW], BF16) for i in range(4)]
    sqj = kb.sbuf(name + "_sqj", [128, 512], BF16)
    ssp = kb.sbuf(name + "_ssp", [128, 4, 8], F32)
    rst = kb.sbuf(name + "_rst", [128, 4], F32)
    NWU = 3
    wu = [kb.sbuf("%s_wu%d" % (name, i), [128, KD, 128], BF16) for i in range(NWU)]
    wus = [DmaSem(kb, "m_wus%d" % i) for i in range(NWU)]
    wv = [kb.sbuf("%s_wv%d" % (name, i), [128, KD, 512], BF16) for i in range(2)]
    wvs = [DmaSem(kb, "m_wvs%d" % i) for i in range(2)]
    wss = [kb.sbuf("%s_wss%d" % (name, i), [128, 16, 128], BF16) for i in range(2)]
    tmp = [kb.sbuf("%s_tmp%d" % (name, i), [128, 4, 128], F32) for i in range(2)]
    opj = OutProj(kb, name + "o", KG)
    hkey, ukey = name + "_hT", name + "_uT"
    zb = wu_i = wv_i = sp_i = 0
    supers = [(c0, n, 0) for (c0, n) in _cols_split(a, b, TS)]
    if with_ctx:
        supers.append((CTXC, NCTX, 1))
    for (t0, T, sset) in supers:
        ntc = T // 128
        pieces = [(c0, n, sset, c0 - t0, False) for (c0, n) in _cols_split(t0, t0 + T, nb.w)]
        emit_norm(kb, nb, cst, src, G.cmask, pieces, hT, hkey, ps[:, 6, :], ("P", "ps", 6))

        def load_wu(j, i):
            slot = i % NWU
            kb.dma("pool", wus[slot], wu[slot][:], wu_d[j, :, 0], writes=[(name, "wu", slot)])

        load_wu(0, wu_i)
        load_wu(1, wu_i + 1)
        for j in range(KG):
            if j + 2 < KG:
                load_wu(j + 2, wu_i + 2)
            slot = wu_i % NWU
            bk = zb % 4
            zb += 1
            for k in range(KD):
                kb.op("pe", lambda t: t.matmul(ps[:, bk, 0:T], lhsT=wu[slot][:, k, :], rhs=hT[:, k, 0:T],
                                               start=(k == 0), stop=(k == KD - 1)),
                      reads=[(name, "wu", slot), hkey], writes=[("P", "ps", bk)], inc=(k == KD - 1))
            kb.op("act", lambda a_: a_.activation(out=uT[:, j, 0:T], in_=ps[:, bk, 0:T], func=AF.Gelu_apprx_tanh),
                  reads=[("P", "ps", bk)], writes=[ukey])
            wu_i += 1

        def load_wv(cg, i):
            slot = i % 2
            kb.dma("pool", wvs[slot], wv[slot][:], wv_d[cg], writes=[(name, "wv", slot)])

        load_wv(0, wv_i)
        for cg in range(8):
            if cg + 1 < 8:
                load_wv(cg + 1, wv_i + 1)
            slot = wv_i % 2
            for tc in range(ntc):
                bk = zb % 4
                zb += 1
                for k in range(KD):
                    kb.op("pe", lambda t: t.matmul(ps[:, bk, :], lhsT=hT[:, k, tc * 128:(tc + 1) * 128],
                                                   rhs=wv[slot][:, k, :], start=(k == 0), stop=(k == KD - 1)),
                          reads=[(name, "wv", slot), hkey], writes=[("P", "ps", bk)], inc=(k == KD - 1))
                vslice = vraw[tc][:, cg * 512:(cg + 1) * 512]
                kb.op("act", lambda a_: a_.activation(out=vslice, in_=ps[:, bk, :], func=AF.Gelu_apprx_tanh),
                      reads=[("P", "ps", bk)], writes=[(name, "vr", tc)])
                kb.op("act", lambda a_: a_.activation(out=sqj[:], in_=vslice, func=AF.Square,
                                                      accum_out=ssp[:, tc, cg:cg + 1]),
                      reads=[(name, "vr", tc)], writes=[name + "_sqj", (name, "ssp", tc)])
            wv_i += 1

        for tc in range(ntc):
            kb.op("dve", lambda v: v.tensor_reduce(out=rst[:, tc:tc + 1], in_=ssp[:, tc, :], op=ALU.add,
                                                   axis=mybir.AxisListType.X),
                  reads=[(name, "ssp", tc)], writes=[(name, "rst", tc)])
            kb.op("act", lambda a_: a_.activation(out=rst[:, tc:tc + 1], in_=rst[:, tc:tc + 1], func=AF.Sqrt,
                                                  scale=1.0 / GMW, bias=EPS),
                  reads=[(name, "rst", tc)], writes=[(name, "rst", tc)])
            kb.op("dve", lambda v: v.reciprocal(out=rst[:, tc:tc + 1], in_=rst[:, tc:tc + 1]),
                  reads=[(name, "rst", tc)], writes=[(name, "rst", tc)])
            w = sp_i % 2
            sp_i += 1
            kb.op("dve", lambda v: v.tensor_scalar(out=wss[w][:], in0=wsT[:], scalar1=rst[:, tc:tc + 1], scalar2=None,
                                                   op0=ALU.mult),
                  reads=[name + "_wsT", (name, "rst", tc)], writes=[(name, "wss", w)])
            for q4 in range(8):
                bk = 4 + (q4 % 2)
                for cc in range(4):
                    c = q4 * 4 + cc
                    kb.op("pe", lambda t: t.matmul(ps[:, bk, cc * 128:(cc + 1) * 128],
                                                   lhsT=vraw[tc][:, c * 128:(c + 1) * 128],
                                                   rhs=wss[w][:, c // 2, :], start=True, stop=True),
                          reads=[(name, "vr", tc), (name, "wss", w)], writes=[("P", "ps", bk)], inc=(cc == 3))
                tm = tmp[q4 % 2]
                tk = (name, "tmp", q4 % 2)
                for cc in range(4):
                    c = q4 * 4 + cc
                    kb.op("dve", lambda v: v.scalar_tensor_tensor(
                        out=tm[:, cc, :], in0=ps[:, bk, cc * 128:(cc + 1) * 128], scalar=vg[:, c:c + 1],
                        in1=bsb[:, c // 2, :], op0=ALU.mult, op1=ALU.add),
                        reads=[("P", "ps", bk), name + "_vg", name + "_bsb"], writes=[tk])
                c0 = q4 * 4
                kb.op("dve", lambda v: v.tensor_tensor(out=uT[:, c0:c0 + 4, tc * 128:(tc + 1) * 128],
                                                       in0=uT[:, c0:c0 + 4, tc * 128:(tc + 1) * 128],
                                                       in1=tm[:], op=ALU.mult),
                      reads=[tk, ukey], writes=[ukey])
        emit_outproj(kb, opj, cst, uT, ukey, KG, wout, [(0, t0, t0, T, sset)], src, dst, ps, (6, 7))


def emit_na(kb, G, cst, src, dst, qb0, qb1, kv0, kvlen, ctx_q, special, wqkv, qkg_d, biasI, biasS, wo, oscr):
    name = "a"
    ps = G.ps
    psb = ps[:, 7, :].bitcast(BF16)
    ones, okey, idn = G.ones, G.okey, G.idn
    nkv = kvlen + NCTX
    nqb = qb1 - qb0
    nq = nqb * 128 + (NCTX if ctx_q else 0)
    nvb = nkv // 128
    qkg = kb.sbuf(name + "_qkg", [128, 2], F32)
    kb.dma("sp", DmaSem(kb, "a_d1"), qkg[:], qkg_d, writes=[name + "_qkg"])
    kb.op("dve", lambda v: v.tensor_scalar(out=qkg[:, 0:1], in0=qkg[:, 0:1], scalar1=float(128 ** -0.5), scalar2=None,
                                           op0=ALU.mult), reads=[name + "_qkg"], writes=[name + "_qkg"])
    nb = NormBufs(kb, name + "n", 64, ones, okey)
    hT = kb.sbuf(name + "_hT", [128, KD, nkv], BF16)
    hkey = name + "_hT"
    wq = [kb.sbuf("%s_wq%d" % (name, i), [128, 3, KD, 128], BF16) for i in range(2)]
    wqs = [DmaSem(kb, "a_wqs%d" % i) for i in range(2)]
    qT = kb.sbuf(name + "_qT", [128, nq], BF16)
    kT = kb.sbuf(name + "_kT", [128, nkv], BF16)
    vT = kb.sbuf(name + "_vT", [128, nkv], BF16)
    V = kb.sbuf(name + "_V", [128, nvb, 128], BF16)
    qraw = [kb.sbuf("%s_qraw%d" % (name, i), [128, 512], F32) for i in range(2)]
    sq = kb.sbuf(name + "_sqh", [128, 512], BF16)
    rs = kb.sbuf(name + "_rsh", [128, 512], F32)
    bI = [kb.sbuf("%s_bI%d" % (name, i), [128, 5, 128], F32) for i in range(2)]
    bIs = [DmaSem(kb, "a_bIs%d" % i) for i in range(2)]
    bS = [kb.sbuf("%s_bS%d" % (name, i), [128, 6, 128], F32) for i in range(2)]
    bSs = [DmaSem(kb, "a_bSs%d" % i) for i in range(2)]
    sb = [kb.sbuf("%s_sb%d" % (name, i), [128, 6, 128], F32) for i in range(2)]
    PT = [kb.sbuf("%s_PT%d" % (name, i), [128, 8, 128], BF16) for i in range(2)]
    rden = [kb.sbuf("%s_rden%d" % (name, i), [128, 128], F32) for i in range(2)]
    oTh = [kb.sbuf("%s_oTh%d" % (name, i), [128, nq], BF16) for i in range(2)]
    oTs = [DmaSem(kb, "a_oTs%d" % i) for i in range(2)]
    opj = OutProj(kb, name + "o", NH)
    P = ("P", "ps")

    pieces = [(c0, n, 0, c0 - kv0, False) for (c0, n) in _cols_split(kv0, kv0 + kvlen, nb.w)]
    pieces += [(c0, n, 1, kvlen + c0 - CTXC, False) for (c0, n) in _cols_split(CTXC, CTXC + NCTX, nb.w)]
    emit_norm(kb, nb, cst, src, G.cmask, pieces, hT, hkey, ps[:, 4, :], P + (4,))

    q_tiles = [(c0 - kv0, n, c0 - qb0 * 128) for (c0, n) in _cols_split(qb0 * 128, qb1 * 128, 512)]
    if ctx_q:
        q_tiles.append((kvlen, NCTX, nqb * 128))
    kv_tiles = [(c0, n, c0) for (c0, n) in _cols_split(0, nkv, 512)]
    zb = qr_i = sb_i = pv_i = bs_i = 0

    def load_w(h):
        slot = h % 2
        kb.dma("pool", wqs[slot], wq[slot][:], wqkv[h], writes=[(name, "wq", slot)])

    load_w(0)
    for h in range(NH):
        if h + 1 < NH:
            load_w(h + 1)
        hs = h % 2
        wkey = (name, "wq", hs)
        w = wq[hs]
        kb.dma("sp", bIs[hs], bI[hs][:], biasI[h], writes=[(name, "bI", hs)])
        ptiles = []
        for t3 in range(3):
            for (c0, n, dc0) in (q_tiles if t3 == 0 else kv_tiles):
                ptiles.append((t3, c0, n, dc0))
        pbank = {}

        def proj_mm(i):
            nonlocal zb
            (t3, c0, n, dc0) = ptiles[i]
            bk = zb % 4
            zb += 1
            pbank[i] = bk
            for k in range(KD):
                kb.op("pe", lambda t: t.matmul(ps[:, bk, 0:n], lhsT=w[:, t3, k, :], rhs=hT[:, k, c0:c0 + n],
                                               start=(k == 0), stop=(k == KD - 1)),
                      reads=[wkey, hkey], writes=[P + (bk,)], inc=(k == KD - 1))

        def proj_post(i):
            nonlocal qr_i
            (t3, c0, n, dc0) = ptiles[i]
            bk = pbank[i]
            if t3 == 2:
                kb.op("act", lambda a_: a_.activation(out=vT[:, c0:c0 + n], in_=ps[:, bk, 0:n], func=AF.Copy),
                      reads=[P + (bk,)], writes=[name + "_vT"])
                return
            r = qr_i % 2
            qr_i += 1
            kb.op("dve", lambda v: v.tensor_copy(out=qraw[r][:, 0:n], in_=ps[:, bk, 0:n]),
                  reads=[P + (bk,)], writes=[(name, "qraw", r)])
            kb.op("act", lambda a_: a_.activation(out=sq[:, 0:n], in_=qraw[r][:, 0:n], func=AF.Square),
                  reads=[(name, "qraw", r)], writes=[name + "_sqh"])
            kb.op("pe", lambda t: t.matmul(ps[:, 4, 0:n], lhsT=ones[:], rhs=sq[:, 0:n], start=True, stop=True),
                  reads=[name + "_sqh", okey], writes=[P + (4,)])
            kb.op("act", lambda a_: a_.activation(out=rs[:, 0:n], in_=ps[:, 4, 0:n], func=AF.Ln,
                                                  scale=1.0 / 128, bias=EPS),
                  reads=[P + (4,)], writes=[name + "_rsh"])
            kb.op("act", lambda a_: a_.activation(out=rs[:, 0:n], in_=rs[:, 0:n], func=AF.Exp, scale=-0.5),
                  reads=[name + "_rsh"], writes=[name + "_rsh"])
            dstt = qT if t3 == 0 else kT
            dkey = name + ("_qT" if t3 == 0 else "_kT")
            kb.op("dve", lambda v: v.scalar_tensor_tensor(out=dstt[:, dc0:dc0 + n], in0=qraw[r][:, 0:n],
                                                          scalar=qkg[:, t3:t3 + 1], in1=rs[:, 0:n],
                                                          op0=ALU.mult, op1=ALU.mult),
                  reads=[(name, "qraw", r), name + "_qkg", name + "_rsh"], writes=[dkey])

        proj_mm(0)
        for i in range(len(ptiles)):
            if i + 1 < len(ptiles):
                proj_mm(i + 1)
            proj_post(i)
        for b0 in range(0, nvb, 8):
            nbk = min(8, nvb - b0)
            for i in range(nbk):
                kb.op("pe", lambda t: t.transpose(psb[:, i * 128:(i + 1) * 128],
                                                  vT[:, (b0 + i) * 128:(b0 + i + 1) * 128], idn[:]),
                      reads=[name + "_vT", G.ikey], writes=[P + (7,)], inc=(i == nbk - 1))
            kb.op("dve", lambda v: v.tensor_copy(out=V[:, b0:b0 + nbk, :],
                                                 in_=psb[:, 0:nbk * 128].rearrange("p (b c) -> p b c", c=128)),
                  reads=[P + (7,)], writes=[name + "_V"])
        oT = oTh[hs]
        okh = (name, "oTh", hs)
        qblocks = []
        for bq in range(qb0, qb1):
            st, npair, sidx = special.get(bq, (-4, 5, -1))
            qblocks.append(("lat", bq, st, npair, sidx, (bq - qb0) * 128))
        if ctx_q:
            qblocks += [("ctx", 0, 0, 0, -1, nqb * 128), ("ctx", 0, 0, 0, -1, nqb * 128 + 128)]
        bst = {}

        def stage_a(i):
            nonlocal sb_i, bs_i
            (kind, bq, start, npair, sidx, qc0) = qblocks[i]
            s2 = sb_i % 2
            sb_i += 1
            chunks = []
            if kind == "lat":
                for p in range(npair):
                    er = 2 * bq + start + 2 * p
                    kc = er * 64 - kv0
                    assert 0 <= kc and kc + 128 <= kvlen, (bq, er, kv0, kvlen)
                    chunks.append((kc, kc // 128, p))
            for cc in range(2):
                chunks.append((kvlen + cc * 128, kvlen // 128 + cc, 6 + cc))
            if sidx >= 0:
                bsl = bs_i % 2
                bs_i += 1
                kb.dma("sp", bSs[bsl], bS[bsl][:], biasS[sidx, h], writes=[(name, "bS", bsl)])
                bias_ap, bias_key = bS[bsl], (name, "bS", bsl)
            else:
                bias_ap, bias_key = bI[hs], (name, "bI", hs)
            bkA, bkB = 2 * s2, 2 * s2 + 1
            for (kc, vb, slot) in chunks:
                bk = bkA if slot < 4 else bkB
                col = (slot % 4) * 128
                kb.op("pe", lambda t: t.matmul(ps[:, bk, col:col + 128], lhsT=kT[:, kc:kc + 128],
                                               rhs=qT[:, qc0:qc0 + 128], start=True, stop=True),
                      reads=[name + "_kT", name + "_qT"], writes=[P + (bk,)], inc=(slot == 7))
            bst[i] = (s2, chunks, bias_ap, bias_key)

        def stage_b(i):
            (kind, bq, start, npair, sidx, qc0) = qblocks[i]
            (s2, chunks, bias_ap, bias_key) = bst[i]
            bkA, bkB = 2 * s2, 2 * s2 + 1
            if kind == "lat":
                n1 = min(npair, 4)
                kb.op("dve", lambda v: v.tensor_tensor(out=sb[s2][:, 0:n1, :],
                                                       in0=ps[:, bkA, 0:n1 * 128].rearrange("p (c q) -> p c q", q=128),
                                                       in1=bias_ap[:, 0:n1, :], op=ALU.add),
                      reads=[P + (bkA,), bias_key], writes=[(name, "sb", s2)])
                n2 = npair - 4
                kb.op("dve", lambda v: v.tensor_tensor(out=sb[s2][:, 4:4 + n2, :],
                                                       in0=ps[:, bkB, 0:n2 * 128].rearrange("p (c q) -> p c q", q=128),
                                                       in1=bias_ap[:, 4:4 + n2, :], op=ALU.add),
                      reads=[P + (bkB,), bias_key], writes=[(name, "sb", s2)])
                kb.op("act", lambda a_: a_.activation(out=PT[s2][:, 0:npair, :], in_=sb[s2][:, 0:npair, :],
                                                      func=AF.Exp),
                      reads=[(name, "sb", s2)], writes=[(name, "PT", s2)])
            kb.op("act", lambda a_: a_.activation(out=PT[s2][:, 6:8, :],
                                                  in_=ps[:, bkB, 256:512].rearrange("p (c q) -> p c q", q=128),
                                                  func=AF.Exp),
                  reads=[P + (bkB,)], writes=[(name, "PT", s2)])

        def stage_c(i):
            nonlocal pv_i
            (kind, bq, start, npair, sidx, qc0) = qblocks[i]
            (s2, chunks, bias_ap, bias_key) = bst[i]
            pv = pv_i % 2
            pv_i += 1
            pvb = 5 + pv
            nch = len(chunks)
            for ii, (kc, vb, slot) in enumerate(chunks):
                kb.op("pe", lambda t: t.matmul(ps[:, pvb, 0:128], lhsT=V[:, vb, :], rhs=PT[s2][:, slot, :],
                                               start=(ii == 0), stop=(ii == nch - 1)),
                      reads=[name + "_V", (name, "PT", s2)], writes=[P + (pvb,)], inc=False)
            for ii, (kc, vb, slot) in enumerate(chunks):
                kb.op("pe", lambda t: t.matmul(ps[:, pvb, 128:256], lhsT=ones[:], rhs=PT[s2][:, slot, :],
                                               start=(ii == 0), stop=(ii == nch - 1)),
                      reads=[okey, (name, "PT", s2)], writes=[P + (pvb,)], inc=(ii == nch - 1))
            kb.op("act", lambda a_: a_.activation(out=rden[pv][:], in_=ps[:, pvb, 128:256], func=AF.Ln),
                  reads=[P + (pvb,)], writes=[(name, "rden", pv)])
            kb.op("act", lambda a_: a_.activation(out=rden[pv][:], in_=rden[pv][:], func=AF.Exp, scale=-1.0),
                  reads=[(name, "rden", pv)], writes=[(name, "rden", pv)])
            kb.op("dve", lambda v: v.tensor_tensor(out=oT[:, qc0:qc0 + 128], in0=ps[:, pvb, 0:128],
                                                   in1=rden[pv][:], op=ALU.mult),
                  reads=[P + (pvb,), (name, "rden", pv)], writes=[okh])

        nblk = len(qblocks)
        for i in range(nblk + 1):
            if i < nblk:
                stage_a(i)
                stage_b(i)
            if i >= 1:
                stage_c(i - 1)
        kb.dma("sp", oTs[hs], oscr[h, :, 0:nq], oT[:, 0:nq], reads=[okh], writes=[name + "_oscr"], partial=True)
    lds = DmaSem(kb, "a_old")
    for h in range(NH):
        kb.dma("sp", lds, hT[:, h, 0:nq], oscr[h, :, 0:nq], reads=[name + "_oscr"], writes=[hkey], partial=(h > 0))
    groups = [(c0 - qb0 * 128, c0, c0, n, 0) for (c0, n) in _cols_split(qb0 * 128, qb1 * 128, 512)]
    if ctx_q:
        groups.append((nqb * 128, CTXC, CTXC, NCTX, 1))
    emit_outproj(kb, opj, cst, hT, hkey, NH, wo, groups, src, dst, ps, (5, 6))


def emit_ada(kb, G, l, modt, modkey, wa):
    ps = G.ps
    w = [kb.sbuf("ada_w%d" % i, [128, KD, 512], BF16) for i in range(2)]
    ws = [DmaSem(kb, "ada_ws%d" % i) for i in range(2)]
    kb.dma("pool", ws[0], w[0][:], wa[l, 0], writes=["ada_w0"])
    for gi in range(ADA_NG):
        if gi + 1 < ADA_NG:
            kb.dma("pool", ws[(gi + 1) % 2], w[(gi + 1) % 2][:], wa[l, gi + 1], writes=["ada_w%d" % ((gi + 1) % 2)])
        wt = w[gi % 2]
        for c4 in range(4):
            ch = gi * 4 + c4
            bk = ch % 4
            for k in range(KD):
                kb.op("pe", lambda t: t.matmul(ps[:, bk, 0:2], lhsT=wt[:, k, c4 * 128:(c4 + 1) * 128],
                                               rhs=G.sg[:, k, :], start=(k == 0), stop=(k == KD - 1)),
                      reads=["ada_w%d" % (gi % 2), "G_sg"], writes=[("P", "ps", bk)], inc=(k == KD - 1))
            kb.op("act", lambda a_: a_.activation(out=modt[:, :, ch // 16, ch % 16], in_=ps[:, bk, 0:2],
                                                  func=AF.Identity, bias=G.bada[:, l, ch:ch + 1]),
                  reads=[("P", "ps", bk), "G_bada"], writes=[modkey])


class Globals:
    pass


R_FFN0 = (384, 3200)
R_GM1 = (384, 3200)
R_FFN1 = (446, 3138)
R_SC2 = (447, 3137)
R_FFN2 = (448, 3136)
NA_SPECIAL = {6: (-4, 6, 0), 7: (-4, 5, 1), 20: (-4, 5, 2), 21: (-6, 6, 3)}
NA0_PASSES = [(2, 14, 0, 2048, True), (14, 26, 1536, 2048, False)]
NA3_PASSES = [(5, 14, 384, 1664, False), (14, 23, 1536, 1664, False)]


def build_fused(nc, n_phases=99, dbg_out=None):
    def din(name, shape, dt=F32):
        return nc.dram_tensor(name, shape, dt, kind="ExternalInput").ap()
    xin = din("xin", [128, KD, NCA])
    cmask = din("cmask", [128, NCA])
    cond = din("cond", [128, KD, 2])
    wa = din("wa", [4, ADA_NG, 128, KD, 512])
    ba = din("ba", [128, 4, 96])
    ngm = din("ngm", [4, 128, 16])
    ngf = din("ngf", [4, 128, 16])
    na_wqkv = din("na_wqkv", [2, NH, 128, 3, KD, 128])
    na_qkg = din("na_qkg", [2, 128, 2])
    na_bI = din("na_bI", [2, NH, 128, 5, 128])
    na_bS = din("na_bS", [2, 4, NH, 128, 6, 128])
    na_wo = din("na_wo", [2, KD, 128, NH, 128])
    gm_wu = din("gm_wu", [KG, 128, 1, KD, 128])
    gm_wv = din("gm_wv", [8, 128, KD, 512])
    gm_vg = din("gm_vg", [128, KG])
    gm_wsT = din("gm_wsT", [128, 16, 128])
    gm_bsb = din("gm_bsb", [128, 16, 128])
    gm_wout = din("gm_wout", [KD, 128, KG, 128])
    sc_win = din("sc_win", [KD, 128, 3, KD, 128])
    sc_cw = din("sc_cw", [128, KD, 3])
    sc_wout = din("sc_wout", [KD, 128, KD, 128])
    f_wup = din("f_wup", [4, KF, 128, 2, KD, 128])
    f_cwb = din("f_cwb", [4, 128, 2, KF, 4])
    f_wdn = din("f_wdn", [4, KD, 128, KF, 128])
    yT = nc.dram_tensor("yT", [128, KD, NLAT], F32, kind="ExternalOutput").ap()
    kind = "ExternalOutput" if dbg_out else "Internal"
    XA = nc.dram_tensor("XA", [128, KD, NCA], F32, kind=kind).ap()
    XB = nc.dram_tensor("XB", [128, KD, NCA], F32, kind=kind).ap()
    oscr = nc.dram_tensor("oscr", [NH, 128, 12 * 128 + NCTX], BF16, kind="Internal").ap()
    with ExitStack() as es:
        kb = KB(nc, es)
        G = Globals()
        G.ps = kb.psum("ps", [128, 8, 512], F32)
        G.cmask = cmask
        G.ones = kb.sbuf("G_ones", [128, 128], BF16)
        G.okey = "G_ones"
        idf = kb.sbuf("G_idf", [128, 128], F32)
        G.idn = kb.sbuf("G_idn", [128, 128], BF16)
        G.ikey = "G_idn"
        cs = kb.sbuf("G_cs", [128, KD, 2], F32)
        G.sg = kb.sbuf("G_sg", [128, KD, 2], BF16)
        G.bada = kb.sbuf("G_bada", [128, 4, 96], F32)
        modt = [kb.sbuf("G_mod%d" % l, [128, 2, 6, 16], F32) for l in range(4)]
        kb.op("dve", lambda v: v.memset(G.ones[:], 1.0), writes=["G_ones"])
        kb.op("pool", lambda g_: g_.memset(idf[:], 1.0), writes=["G_idf"])
        kb.op("pool", lambda g_: g_.affine_select(out=idf[:], in_=idf[:], pattern=[[-1, 128]], compare_op=ALU.is_equal,
                                                 fill=0.0, base=0, channel_multiplier=1),
              reads=["G_idf"], writes=["G_idf"])
        kb.op("dve", lambda v: v.tensor_copy(out=G.idn[:], in_=idf[:]), reads=["G_idf"], writes=["G_idn"])
        kb.dma("sp", DmaSem(kb, "g_d1"), cs[:], cond, writes=["G_cs"])
        kb.dma("sp", DmaSem(kb, "g_d2"), G.bada[:], ba, writes=["G_bada"])
        kb.op("act", lambda a_: a_.activation(out=G.sg[:], in_=cs[:], func=AF.Silu), reads=["G_cs"], writes=["G_sg"])
        kb.barrier()
        ph = [0]

        def phase(fn):
            ph[0] += 1
            if ph[0] > n_phases:
                return
            with ExitStack() as pes:
                kb.cur_es = pes
                fn()
                kb.barrier()
            kb.cur_es = es

        def mixer_consts(l):
            return make_consts(kb, "cm%d" % l, modt[l], "G_mod%d" % l, ngm[l], 0)

        def ffn_consts(l):
            return make_consts(kb, "cf%d" % l, modt[l], "G_mod%d" % l, ngf[l], 1)

        phase(lambda: emit_ada(kb, G, 0, modt[0], "G_mod0", wa))
        for (qb0, qb1, kv0, kvlen, cq) in NA0_PASSES:
            phase(lambda: emit_na(kb, G, mixer_consts(0), xin, XA, qb0, qb1, kv0, kvlen, cq, NA_SPECIAL,
                                  na_wqkv[0], na_qkg[0], na_bI[0], na_bS[0], na_wo[0], oscr))
        phase(lambda: emit_ffn(kb, G, ffn_consts(0), XA, XB, 0, R_FFN0[0], R_FFN0[1], True,
                               f_wup[0], f_cwb[0], f_wdn[0]))
        phase(lambda: emit_ada(kb, G, 1, modt[1], "G_mod1", wa))
        phase(lambda: emit_gm(kb, G, mixer_consts(1), XB, XA, R_GM1[0], R_GM1[1], True,
                              gm_wu, gm_wv, gm_vg, gm_wsT, gm_bsb, gm_wout))
        phase(lambda: emit_ffn(kb, G, ffn_consts(1), XA, XB, 0, R_FFN1[0], R_FFN1[1], True,
                               f_wup[1], f_cwb[1], f_wdn[1]))
        phase(lambda: emit_ada(kb, G, 2, modt[2], "G_mod2", wa))
        phase(lambda: emit_sc(kb, G, mixer_consts(2), XB, XA, R_SC2[0], R_SC2[1], True, sc_win, sc_cw, sc_wout))
        phase(lambda: emit_ffn(kb, G, ffn_consts(2), XA, XB, 0, R_FFN2[0], R_FFN2[1], True,
                               f_wup[2], f_cwb[2], f_wdn[2]))
        phase(lambda: emit_ada(kb, G, 3, modt[3], "G_mod3", wa))
        for (qb0, qb1, kv0, kvlen, cq) in NA3_PASSES:
            phase(lambda: emit_na(kb, G, mixer_consts(3), XB, XA, qb0, qb1, kv0, kvlen, cq, NA_SPECIAL,
                                  na_wqkv[1], na_qkg[1], na_bI[1], na_bS[1], na_wo[1], oscr))
        phase(lambda: emit_ffn(kb, G, ffn_consts(3), XA, yT, OWN0, OWN0, OWN1, False,
                               f_wup[3], f_cwb[3], f_wdn[3]))
        kb.finish()
        print("fused program: ins=%d waits=%d sems=%d" % (kb.n_ins, kb.n_wait, kb.nsem))
    return nc


def fm(v):
    v = np.asarray(v, np.float32)
    k = v.shape[-1] // 128
    r = v.reshape(v.shape[:-1] + (k, 128))
    return np.ascontiguousarray(np.moveaxis(r, -1, 0))


def tile_wup(w_up):
    w = w_up.reshape(KD, 128, 2, KF, 128)
    return np.ascontiguousarray(w.transpose(3, 1, 2, 0, 4))


def tile_wdn(w_down):
    nk = w_down.shape[0] // 128
    w = w_down.reshape(nk, 128, KD, 128)
    return np.ascontiguousarray(w.transpose(2, 1, 0, 3))


def tile_win(w_in, nt):
    nj = w_in.shape[1] // (nt * 128)
    w = w_in.reshape(KD, 128, nt, nj, 128)
    return np.ascontiguousarray(w.transpose(3, 1, 2, 0, 4))


def _unused_tile_ada(w_ada_all, b_ada_all, core):
    L = w_ada_all.shape[0]
    per = L * 12288 // NCORES
    cpl = 12288 // per
    l, part = core // cpl, core % cpl
    w = w_ada_all[l][:, part * per:(part + 1) * per]
    w = w.reshape(KD, 128, ADA_G, 512)
    wt = np.ascontiguousarray(w.transpose(2, 1, 0, 3))
    b = b_ada_all[l][part * per:(part + 1) * per].reshape(ADA_CH, 128)
    return wt, np.ascontiguousarray(b.T)


def tile_cwb(conv_w, conv_b):
    a = np.concatenate([conv_w, conv_b[None]], 0)
    a = a.reshape(4, 2, KF, 128)
    return np.ascontiguousarray(a.transpose(3, 1, 2, 0))


def from_out(yT):
    return np.ascontiguousarray(yT.transpose(2, 1, 0).reshape(yT.shape[2], D))


def gm_weights(w_in, v_g, w_s, b_s, w_out):
    wu = tile_win(w_in[:, :GMW], 1)
    wv = w_in[:, GMW:].reshape(KD, 128, 8, 512).transpose(2, 1, 0, 3)
    return {
        "wu": wu,
        "wv": np.ascontiguousarray(wv),
        "vg": fm(v_g),
        "wsT": np.ascontiguousarray(w_s.transpose(2, 0, 1)),
        "bsb": np.ascontiguousarray(np.broadcast_to(b_s[None], (128, 16, 128))),
        "wout": tile_wdn(w_out),
    }


def _na_block_bias(rpb, r0, start, npair, rows=128):
    qq = np.arange(128)
    q_row, q_col = r0 + qq // 64, qq % 64
    kk = np.arange(128)
    r_start = np.clip(q_row - 4, 0, rows - 8)
    c_start = np.clip(q_col - 8, 0, 64 - 16)
    out = np.full((NH, 128, 6, 128), -30000.0, np.float32)
    for p in range(npair):
        k_row = (r0 + start + 2 * p + kk // 64)[:, None]
        k_col = (kk % 64)[:, None]
        ok = ((k_row >= 0) & (k_row < rows) & (k_row >= r_start[None]) & (k_row < r_start[None] + 8)
              & (k_col >= c_start[None]) & (k_col < c_start[None] + 16))
        dr = np.clip(k_row - q_row[None], -7, 7) + 7
        dc = np.clip(k_col - q_col[None], -15, 15) + 15
        out[:, :, p, :] = np.where(ok[None], rpb[:, dr, dc], np.float32(-30000.0))
    return out


def na_bias(rpb, core):
    R0 = (core % 4) * 32
    bI = _na_block_bias(rpb, 64, -4, 5)[:, :, 0:5, :]
    bS = np.stack([
        _na_block_bias(rpb, R0 + 0, -4, 6),
        _na_block_bias(rpb, R0 + 2, -4, 5),
        _na_block_bias(rpb, R0 + 28, -4, 5),
        _na_block_bias(rpb, R0 + 30, -6, 6),
    ], 0)
    return np.ascontiguousarray(bI), np.ascontiguousarray(bS)


def ext_cols(xfull, ctxfull, core):
    b, q = core // 4, core % 4
    t0 = q * NLAT - OWN0
    ext = np.zeros((NCA, D), np.float32)
    lo, hi = max(t0, 0), min(t0 + NE, xfull.shape[1])
    ext[lo - t0:hi - t0] = xfull[b, lo:hi]
    ext[CTXC:] = ctxfull[b]
    return np.ascontiguousarray(ext.reshape(NCA, KD, 128).transpose(2, 1, 0))


def cmask_for(core, seq):
    q = core % 4
    t0 = q * NLAT - OWN0
    g = t0 + np.arange(NE)
    m = np.ones((NCA,), np.float32)
    m[:NE] = ((g >= 0) & (g < seq)).astype(np.float32)
    return np.ascontiguousarray(np.broadcast_to(m[None], (128, NCA)))


_PROG = {}


def kernel(x, c, ctx, c_ctx, norm_mix_g, norm_ffn_g, w_ada, b_ada,
           na_w_qkv, na_q_g, na_k_g, na_rpb, na_w_o,
           gm_w_in, gm_v_g, gm_w_s, gm_b_s, gm_w_out,
           sc_w_in, sc_conv_w, sc_w_out,
           ffn_w_up, ffn_conv_w, ffn_conv_b, ffn_w_down):
    f32 = lambda a: np.ascontiguousarray(np.asarray(a, np.float32))
    x, c, ctx, c_ctx = f32(x), f32(c), f32(ctx), f32(c_ctx)
    w_ada, b_ada = f32(w_ada), f32(b_ada)
    if "nc" not in _PROG:
        nc = bass.Bass("TRN2", target_bir_lowering=False)
        build_fused(nc)
        _PROG["nc"] = nc
    nc = _PROG["nc"]
    shared = {
        "wa": np.ascontiguousarray(w_ada.reshape(4, KD, 128, ADA_NG, 512).transpose(0, 3, 2, 1, 4)),
        "ba": fm(b_ada),
        "ngm": np.ascontiguousarray(fm(f32(norm_mix_g)).transpose(1, 0, 2)),
        "ngf": np.ascontiguousarray(fm(f32(norm_ffn_g)).transpose(1, 0, 2)),
        "na_wqkv": np.stack([tile_win(f32(na_w_qkv[j]), 3) for j in range(2)], 0),
        "na_qkg": np.ascontiguousarray(np.stack([f32(na_q_g), f32(na_k_g)], -1)),
        "na_wo": np.stack([tile_wdn(f32(na_w_o[j])) for j in range(2)], 0),
        "sc_win": tile_win(f32(sc_w_in[0]), 3),
        "sc_cw": np.ascontiguousarray(fm(f32(sc_conv_w[0])).transpose(0, 2, 1)),
        "sc_wout": tile_wdn(f32(sc_w_out[0])),
        "f_wup": np.stack([tile_wup(f32(ffn_w_up[i])) for i in range(4)], 0),
        "f_cwb": np.stack([tile_cwb(f32(ffn_conv_w[i]), f32(ffn_conv_b[i])) for i in range(4)], 0),
        "f_wdn": np.stack([tile_wdn(f32(ffn_w_down[i])) for i in range(4)], 0),
    }
    gw = gm_weights(f32(gm_w_in[0]), f32(gm_v_g[0]), f32(gm_w_s[0]), f32(gm_b_s[0]), f32(gm_w_out[0]))
    for k, v in gw.items():
        shared["gm_" + k] = v
    rpb = f32(na_rpb)
    in_maps = []
    for core in range(NCORES):
        b = core // 4
        cond = np.stack([c[b], c_ctx], 0)
        bias = [na_bias(rpb[j], core) for j in range(2)]
        d = dict(shared)
        d["xin"] = ext_cols(x, ctx, core)
        d["cmask"] = cmask_for(core, x.shape[1])
        d["cond"] = np.ascontiguousarray(fm(cond).transpose(0, 2, 1))
        d["na_bI"] = np.stack([bias[0][0], bias[1][0]], 0)
        d["na_bS"] = np.stack([bias[0][1], bias[1][1]], 0)
        in_maps.append(d)
    res = run_bass_kernel_spmd(nc, in_maps, core_ids=list(range(NCORES)))
    out = np.empty_like(x)
    for core in range(NCORES):
        b, q = core // 4, core % 4
        out[b, q * NLAT:(q + 1) * NLAT] = from_out(np.asarray(res.results[core]["yT"]))
    return out
```

```python
import numpy as np
from contextlib import ExitStack
import concourse.bass as bass
import concourse.mybir as mybir
from concourse.bass_utils import run_bass_kernel_spmd

F32 = mybir.dt.float32
BF16 = mybir.dt.bfloat16
AF = mybir.ActivationFunctionType
ALU = mybir.AluOpType

D = 2048
KD = 16
FF = 5504
KF = 43
NLAT = 2048
NCTX = 256
EPS = 1e-6
NCORES = 8

SEM_ROT = 30000


class _DmaSem:
    def __init__(self, kb, name):
        self.kb = kb
        self.name = name
        self.sem = kb.newsem(name)
        self.n = 0

    def next(self):
        if (self.n + 1) * 16 > 60000:
            self.sem = self.kb.newsem(self.name)
            self.n = 0
        self.n += 1
        return self.sem, self.n * 16


def DmaSem(kb, name):
    d = kb.dsem_pool.get(name)
    if d is None:
        d = kb.dsem_pool[name] = _DmaSem(kb, name)
    return d


class KB:
    def __init__(self, nc, es):
        self.nc = nc
        self.es = es
        self.eng = {"pe": nc.tensor, "act": nc.scalar, "dve": nc.vector, "pool": nc.gpsimd, "sp": nc.sync}
        self.nsem = 0
        self.dsem_pool = {}
        self.cur_es = es
        self.nbuf = 0
        self.csem = {}
        self.cnt = {}
        for e in ("pe", "act", "dve", "pool"):
            self.csem[e] = self.newsem("c_" + e)
            self.cnt[e] = 0
        self.waited = {e: {} for e in self.eng}
        self.res = {}
        self.pending = {e: [] for e in self.eng}
        self.store_toks = []
        self.n_wait = 0
        self.n_ins = 0

    def newsem(self, name):
        self.nsem += 1
        return self.es.enter_context(self.nc.semaphore("%s_%d" % (name, self.nsem)))

    def sbuf(self, name, shape, dt):
        self.nbuf += 1
        return self.cur_es.enter_context(self.nc.sbuf_tensor("%s_%d" % (name, self.nbuf), shape, dt))

    def barrier(self):
        for e in self.pending:
            assert not self.pending[e], e
        toks = [(self.csem[e], self.cnt[e], e) for e in self.csem if self.cnt[e] > 0]
        toks += [(d.sem, d.n * 16, "dma") for d in self.dsem_pool.values() if d.n > 0]
        for e in self.eng:
            for tok in toks:
                self._wait(e, tok, True)
        self.res = {}
        self.store_toks = []

    def psum(self, name, shape, dt):
        return self.es.enter_context(self.nc.psum_tensor(name, shape, dt))

    def _wait(self, e, tok, raw):
        sem, val, src = tok
        if src == e and not raw:
            return
        w = self.waited[e]
        if w.get(id(sem), 0) >= val:
            return
        self.eng[e].wait_ge(sem, val)
        self.n_wait += 1
        w[id(sem)] = val

    def _deps(self, e, reads, writes):
        for k in reads:
            st = self.res.get(k)
            if st:
                for tok in st["w"]:
                    self._wait(e, tok, True)
                if isinstance(k, tuple) and len(k) > 1 and k[1] == "ps":
                    for tok in st["r"]:
                        self._wait(e, tok, False)
        for k in writes:
            st = self.res.get(k)
            if st:
                for tok in st["w"]:
                    self._wait(e, tok, False)
                for tok in st["r"]:
                    self._wait(e, tok, False)

    def _reg(self, tok, reads, writes):
        for k in writes:
            self.res[k] = {"w": [tok], "r": []}
        for k in reads:
            st = self.res.get(k)
            if st is None:
                st = self.res[k] = {"w": [], "r": []}
            st["r"].append(tok)
            if len(st["r"]) > 64:
                st["r"] = st["r"][-32:]

    def op(self, e, fn, reads=(), writes=(), inc=True):
        self._deps(e, reads, writes)
        ins = fn(self.eng[e])
        self.n_ins += 1
        if not inc:
            self.pending[e].append((tuple(reads), tuple(writes)))
            return ins
        if self.cnt[e] >= SEM_ROT:
            self.csem[e] = self.newsem("c_" + e)
            self.cnt[e] = 0
        self.cnt[e] += 1
        ins.then_inc(self.csem[e], 1)
        tok = (self.csem[e], self.cnt[e], e)
        for (r, w) in self.pending[e]:
            self._reg(tok, r, w)
        self.pending[e] = []
        self._reg(tok, reads, writes)
        return ins

    def dma(self, q, dsem, out, in_, reads=(), writes=(), store=False, partial=False):
        self._deps(q, reads, writes)
        ins = self.eng[q].dma_start(out=out, in_=in_)
        self.n_ins += 1
        sem, val = dsem.next()
        ins.then_inc(sem, 16)
        tok = (sem, val, "dma")
        if partial:
            for k in writes:
                st = self.res.get(k)
                if st is None:
                    st = self.res[k] = {"w": [], "r": []}
                st["w"] = [t for t in st["w"] if t[0] is not sem] + [tok]
            self._reg(tok, reads, ())
        else:
            self._reg(tok, reads, writes)
        if store:
            self.store_toks.append(tok)
        return ins

    def finish(self):
        for tok in self.store_toks:
            self._wait("sp", tok, True)
        self.store_toks = []


NE = 3584
OWN0 = 768
OWN1 = OWN0 + NLAT
CTXC = NE
NCA = NE + NCTX
NH = 16
GMW = 4096
KG = 32
ADA_NG = 24


def _split_even(a, b, maxn):
    n = b - a
    k = -(-n // maxn)
    out = []
    c = a
    for i in range(k):
        m = n // k + (1 if i < n % k else 0)
        out.append((c, m))
        c += m
    return out


def _cols_split(c0, c1, step):
    out = []
    c = c0
    while c < c1:
        e = min(c + step, c1)
        out.append((c, e - c))
        c = e
    return out


class Consts:
    pass


def make_consts(kb, name, modt, modkey, ng_ap, which):
    c = Consts()
    c.mod = modt
    c.ng = kb.sbuf(name + "_ng", [128, 16], F32)
    c.A = kb.sbuf(name + "_A", [128, 2, 16], F32)
    kb.dma("sp", DmaSem(kb, "cs_ng"), c.ng[:], ng_ap, writes=[name + "_ng"])
    o = 3 * which
    for s in range(2):
        kb.op("dve", lambda v: v.scalar_tensor_tensor(out=c.A[:, s, :], in0=c.mod[:, s, o + 1, :], scalar=1.0,
                                                     in1=c.ng[:], op0=ALU.add, op1=ALU.mult),
              reads=[modkey, name + "_ng"], writes=[name + "_A%d" % s])
    c.Akey = lambda s: name + "_A%d" % s
    c.Bap = lambda s, k: c.mod[:, s, o + 0, k:k + 1]
    c.Gap = lambda s, k: c.mod[:, s, o + 2, k:k + 1]
    c.Aap = lambda s, k: c.A[:, s, k:k + 1]
    c.modkey = modkey
    return c


class NormBufs:
    def __init__(self, kb, name, w, ones, okey):
        self.w = w
        self.xn = [kb.sbuf("%s_xn%d" % (name, i), [128, KD, w], F32) for i in range(2)]
        self.sq = kb.sbuf(name + "_sq", [128, KD, w], BF16)
        self.rs = kb.sbuf(name + "_rs", [128, w], F32)
        self.cm = kb.sbuf(name + "_cm", [128, w], F32)
        self.ones = ones
        self.okey = okey
        self.ds = [DmaSem(kb, "nb_xs%d" % i) for i in range(2)]
        self.dcm = DmaSem(kb, "nb_cm")
        self.name = name
        self.i = 0


def emit_norm(kb, nb, cst, x_ap, cmask_ap, pieces, hT, hkey, ps_ap, pskey):
    for piece in pieces:
        emit_norm_piece(kb, nb, cst, x_ap, cmask_ap, piece, hT, hkey, ps_ap, pskey)


def emit_norm_piece(kb, nb, cst, x_ap, cmask_ap, piece, hT, hkey, ps_ap, pskey):
    name = nb.name
    for (c0, n, s, hc, masked) in [piece]:
        slot = nb.i % 2
        nb.i += 1
        xn = nb.xn[slot]
        xk = "%s_xn%d" % (name, slot)
        kb.dma("sp", nb.ds[slot], xn[:, :, 0:n], x_ap[:, :, c0:c0 + n], writes=[xk])
        if masked:
            kb.dma("sp", nb.dcm, nb.cm[:, 0:n], cmask_ap[:, c0:c0 + n], writes=[name + "_cm"])
        kb.op("act", lambda a: a.activation(out=nb.sq[:, :, 0:n], in_=xn[:, :, 0:n], func=AF.Square),
              reads=[xk], writes=[name + "_sq"])
        for k in range(KD):
            kb.op("pe", lambda t: t.matmul(ps_ap[:, 0:n], lhsT=nb.ones[:], rhs=nb.sq[:, k, 0:n],
                                           start=(k == 0), stop=(k == KD - 1)),
                  reads=[name + "_sq", nb.okey], writes=[pskey], inc=(k == KD - 1))
        kb.op("act", lambda a: a.activation(out=nb.rs[:, 0:n], in_=ps_ap[:, 0:n], func=AF.Sqrt,
                                            scale=1.0 / D, bias=EPS),
              reads=[pskey], writes=[name + "_rs"])
        kb.op("dve", lambda v: v.reciprocal(out=nb.rs[:, 0:n], in_=nb.rs[:, 0:n]),
              reads=[name + "_rs"], writes=[name + "_rs"])
        if masked:
            pass
        kb.op("dve", lambda v: v.tensor_tensor(out=xn[:, :, 0:n], in0=xn[:, :, 0:n],
                                               in1=nb.rs[:, 0:n].unsqueeze(1).broadcast_to([128, KD, n]),
                                               op=ALU.mult),
              reads=[xk, name + "_rs"], writes=[xk])
        for k in range(KD):
            kb.op("act", lambda a: a.activation(out=hT[:, k, hc:hc + n], in_=xn[:, k, 0:n],
                                                func=AF.Identity, scale=cst.Aap(s, k), bias=cst.Bap(s, k)),
                  reads=[xk, cst.Akey(s), cst.modkey], writes=[hkey])
        if masked:
            kb.op("dve", lambda v: v.tensor_tensor(out=hT[:, :, hc:hc + n], in0=hT[:, :, hc:hc + n],
                                                   in1=nb.cm[:, 0:n].unsqueeze(1).broadcast_to([128, KD, n]),
                                                   op=ALU.mult),
                  reads=[hkey, name + "_cm"], writes=[hkey])


class OutProj:
    def __init__(self, kb, name, kmax):
        self.name = name
        self.wd = [kb.sbuf("%s_wd%d" % (name, i), [128, kmax, 128], BF16) for i in range(2)]
        self.wds = [DmaSem(kb, "op_wds%d" % i) for i in range(2)]
        self.xr = [kb.sbuf("%s_xr%d" % (name, i), [128, 512], F32) for i in range(2)]
        self.xrs = [DmaSem(kb, "op_xrs%d" % i) for i in range(2)]
        self.ot = [kb.sbuf("%s_ot%d" % (name, i), [128, 512], F32) for i in range(2)]
        self.ots = [DmaSem(kb, "op_ots%d" % i) for i in range(2)]
        self.wd_i = 0
        self.dn = 0


def emit_outproj(kb, o, cst, g, gkey, nk, wdn, groups, xT, yT, ps, banks, side=None):
    name = o.name

    def load_wd(dc, wd_i):
        slot = wd_i % 2
        kb.dma("pool", o.wds[slot], o.wd[slot][:, 0:nk, :], wdn[dc, :, 0:nk, :], writes=[(name, "wd", slot)])

    load_wd(0, o.wd_i)
    for dc in range(KD):
        if dc + 1 < KD:
            load_wd(dc + 1, o.wd_i + 1)
        slot = o.wd_i % 2
        wkey = (name, "wd", slot)
        wdt = o.wd[slot]
        for (g0, xc, oc, n, s) in groups:
            r = o.dn % 2
            o.dn += 1
            b = banks[r]
            xr, ot = o.xr[r], o.ot[r]
            kb.dma("sp", o.xrs[r], xr[:, 0:n], xT[:, dc, xc:xc + n], writes=[(name, "xr", r)])
            for j in range(nk):
                kb.op("pe", lambda t: t.matmul(ps[:, b, 0:n], lhsT=wdt[:, j, :], rhs=g[:, j, g0:g0 + n],
                                               start=(j == 0), stop=(j == nk - 1)),
                      reads=[wkey, gkey], writes=[("P", "ps", b)], inc=(j == nk - 1))
            kb.op("dve", lambda v: v.scalar_tensor_tensor(out=ot[:, 0:n], in0=ps[:, b, 0:n],
                                                          scalar=cst.Gap(s, dc), in1=xr[:, 0:n],
                                                          op0=ALU.mult, op1=ALU.add),
                  reads=[("P", "ps", b), (name, "xr", r), cst.modkey], writes=[(name, "ot", r)])
            kb.dma("sp", o.ots[r], yT[:, dc, oc:oc + n], ot[:, 0:n], reads=[(name, "ot", r)], store=True)
        o.wd_i += 1
        if side:
            for _ in range(-(-len(side) // (KD - dc))):
                side.pop(0)()
    while side:
        side.pop(0)()


def _conv_supers(a, b, with_ctx, cap):
    tot = (b - a) + (NCTX if with_ctx else 0)
    ns = -(-tot // cap)
    per = -(-tot // ns)
    supers = []
    c = a
    for si in range(ns):
        room = per if si < ns - 1 else cap
        tiles = []
        nl = min(room, b - c)
        if nl > 0:
            tiles += [("lat", c0, n) for (c0, n) in _split_even(c, c + nl, 350)]
            c += nl
            room -= nl
        supers.append(tiles)
    if with_ctx:
        used = sum(t[2] for t in supers[-1])
        if used + NCTX <= cap:
            supers[-1].append(("ctx", CTXC, NCTX))
        else:
            supers.append([("ctx", CTXC, NCTX)])
    assert c == b
    return supers


def _conv_plan(tiles):
    descs, spans, zero = [], [], []
    h = 0
    lat = [t for t in tiles if t[0] == "lat"]
    if lat:
        lo = lat[0][1] - 1
        hi = lat[-1][1] + lat[-1][2] + 1
        spans.append((lo, hi - lo, 0, h))
        for (_, c0, n) in lat:
            descs.append((c0, n, h + (c0 - 1 - lo), 0))
        h += hi - lo
    for (kind, c0, n) in tiles:
        if kind == "ctx":
            zero += [h, h + n + 1]
            spans.append((c0, n, 1, h + 1))
            descs.append((c0, n, h, 1))
            h += n + 2
    return descs, spans, zero, h


def _norm_pieces(spans, w):
    pieces = []
    for (c0, n, s, hc) in spans:
        for (p0, pn) in _cols_split(c0, c0 + n, w):
            masked = (s == 0) and (p0 < OWN0 or p0 + pn > OWN1)
            pieces.append((p0, pn, s, hc + (p0 - c0), masked))
    return pieces


def emit_ffn(kb, G, cst, src, dst, dst_off, a, b, with_ctx, wup, cwb, wdn, up_side=None):
    name = "f"
    up_side = up_side or []
    ps = G.ps
    cw = kb.sbuf(name + "_cw", [128, 2, KF, 4], F32)
    kb.dma("sp", DmaSem(kb, "f_c2"), cw[:], cwb, writes=[name + "_cw"])
    nb = NormBufs(kb, name + "n", 64, G.ones, G.okey)
    CAP = 1024
    WMAX = CAP + 8
    hT = kb.sbuf(name + "_hT", [128, KD, WMAX], BF16)
    g = kb.sbuf(name + "_g", [128, KF, CAP], BF16)
    NWB = 2
    wu = [kb.sbuf("%s_wu%d" % (name, i), [128, 2, KD, 128], BF16) for i in range(NWB)]
    wus = [DmaSem(kb, "f_wus%d" % i) for i in range(NWB)]
    ya = [kb.sbuf("%s_ya%d" % (name, i), [128, 352], F32) for i in range(2)]
    yb = [kb.sbuf("%s_yb%d" % (name, i), [128, 352], F32) for i in range(2)]
    sg = [kb.sbuf("%s_sg%d" % (name, i), [128, 352], F32) for i in range(2)]
    opj = OutProj(kb, name + "o", KF)
    hkey, gkey, cwk = name + "_hT", name + "_g", name + "_cw"
    zb = ev = wu_i = 0
    supers = _conv_supers(a, b, with_ctx, CAP)
    plans = [_conv_plan(t) for t in supers]

    def norm_jobs(si):
        descs_, spans_, zero_, width_ = plans[si]
        assert width_ <= WMAX, width_
        jobs = []
        for piece in _norm_pieces(spans_, nb.w):
            jobs.append(lambda piece=piece: emit_norm_piece(kb, nb, cst, src, G.cmask, piece, hT, hkey,
                                                            ps[:, 5, :], ("P", "ps", 5)))
        for zc in zero_:
            jobs.append(lambda zc=zc: kb.op("dve", lambda v: v.memset(hT[:, :, zc:zc + 1], 0.0), writes=[hkey]))
        return jobs

    for job in norm_jobs(0):
        job()
    n_j = 0
    for si, tiles in enumerate(supers):
        descs, spans, zero, width = plans[si]
        gcol = {}
        gc = 0
        for (c0, n, hb, s) in descs:
            gcol[c0] = gc
            gc += n
        assert gc <= CAP

        def load_wu(j, i):
            slot = i % NWB
            kb.dma("pool", wus[slot], wu[slot][:], wup[j], writes=[(name, "wu", slot)])

        load_wu(0, wu_i)
        for j in range(KF):
            if j + 1 < KF:
                load_wu(j + 1, wu_i + 1)
            slot = wu_i % NWB
            wkey = (name, "wu", slot)
            for (c0, n, hb, s) in descs:
                banks = []
                for half in range(2):
                    bk = zb % 5
                    zb += 1
                    banks.append(bk)
                    for k in range(KD):
                        kb.op("pe", lambda t: t.matmul(ps[:, bk, 0:n + 2], lhsT=wu[slot][:, half, k, :],
                                                       rhs=hT[:, k, hb:hb + n + 2],
                                                       start=(k == 0), stop=(k == KD - 1)),
                              reads=[wkey, hkey], writes=[("P", "ps", bk)], inc=(k == KD - 1))
                e = ev % 2
                ev += 1
                bg, bu = banks
                kg, ku = ("P", "ps", bg), ("P", "ps", bu)
                kb.op("act", lambda a_: a_.activation(out=ya[e][:, 0:n], in_=ps[:, bg, 1:n + 1], func=AF.Identity,
                                                      scale=cw[:, 0, j, 1:2], bias=cw[:, 0, j, 3:4]),
                      reads=[kg, cwk], writes=[(name, "ya", e)])
                kb.op("act", lambda a_: a_.activation(out=yb[e][:, 0:n], in_=ps[:, bu, 1:n + 1], func=AF.Identity,
                                                      scale=cw[:, 1, j, 1:2], bias=cw[:, 1, j, 3:4]),
                      reads=[ku, cwk], writes=[(name, "yb", e)])
                for (tap, off) in ((0, 0), (2, 2)):
                    kb.op("dve", lambda v: v.scalar_tensor_tensor(
                        out=ya[e][:, 0:n], in0=ps[:, bg, off:off + n], scalar=cw[:, 0, j, tap:tap + 1],
                        in1=ya[e][:, 0:n], op0=ALU.mult, op1=ALU.add),
                        reads=[kg, cwk, (name, "ya", e)], writes=[(name, "ya", e)])
                    kb.op("dve", lambda v: v.scalar_tensor_tensor(
                        out=yb[e][:, 0:n], in0=ps[:, bu, off:off + n], scalar=cw[:, 1, j, tap:tap + 1],
                        in1=yb[e][:, 0:n], op0=ALU.mult, op1=ALU.add),
                        reads=[ku, cwk, (name, "yb", e)], writes=[(name, "yb", e)])
                kb.op("act", lambda a_: a_.activation(out=sg[e][:, 0:n], in_=ya[e][:, 0:n], func=AF.Silu),
                      reads=[(name, "ya", e)], writes=[(name, "sg", e)])
                g0 = gcol[c0]
                kb.op("dve", lambda v: v.tensor_tensor(out=g[:, j, g0:g0 + n], in0=yb[e][:, 0:n], in1=sg[e][:, 0:n],
                                                       op=ALU.mult),
                      reads=[(name, "yb", e), (name, "sg", e)], writes=[gkey])
            wu_i += 1
            if up_side:
                up_side.pop(0)()
        groups = [(gcol[c0], c0, c0 - dst_off, n, s) for (c0, n, hb, s) in descs]
        side = norm_jobs(si + 1) if si + 1 < len(supers) else None
        emit_outproj(kb, opj, cst, g, gkey, KF, wdn, groups, src, dst, ps, (6, 7), side=side)
    while up_side:
        up_side.pop(0)()


def emit_sc(kb, G, cst, src, dst, a, b, with_ctx, win, cwd, wout):
    name = "s"
    ps = G.ps
    cw = kb.sbuf(name + "_cw", [128, KD, 3], F32)
    kb.dma("sp", DmaSem(kb, "s_c2"), cw[:], cwd, writes=[name + "_cw"])
    nb = NormBufs(kb, name + "n", 64, G.ones, G.okey)
    CAP = 1024
    WMAX = CAP + 8
    hT = kb.sbuf(name + "_hT", [128, KD, WMAX], BF16)
    g = kb.sbuf(name + "_g", [128, KD, CAP], BF16)
    wu = [kb.sbuf("%s_wu%d" % (name, i), [128, 3, KD, 128], BF16) for i in range(2)]
    wus = [DmaSem(kb, "s_wus%d" % i) for i in range(2)]
    ux = [kb.sbuf("%s_ux%d" % (name, i), [128, 512], F32) for i in range(2)]
    uu = [kb.sbuf("%s_uu%d" % (name, i), [128, 512], F32) for i in range(2)]
    vv = [kb.sbuf("%s_vv%d" % (name, i), [128, 512], F32) for i in range(2)]
    opj = OutProj(kb, name + "o", KD)
    hkey, gkey, cwk = name + "_hT", name + "_g", name + "_cw"
    zb = ev = wi = 0
    for tiles in _conv_supers(a, b, with_ctx, CAP):
        descs, spans, zero, width = _conv_plan(tiles)
        assert width <= WMAX
        emit_norm(kb, nb, cst, src, G.cmask, _norm_pieces(spans, nb.w), hT, hkey, ps[:, 6, :], ("P", "ps", 6))
        for zc in zero:
            kb.op("dve", lambda v: v.memset(hT[:, :, zc:zc + 1], 0.0), writes=[hkey])
        gcol = {}
        gc = 0
        for (c0, n, hb, s) in descs:
            gcol[c0] = gc
            gc += n

        def load_w(j, i):
            slot = i % 2
            kb.dma("pool", wus[slot], wu[slot][:], win[j], writes=[(name, "wu", slot)])

        load_w(0, wi)
        for j in range(KD):
            if j + 1 < KD:
                load_w(j + 1, wi + 1)
            slot = wi % 2
            wkey = (name, "wu", slot)
            for (c0, n, hb, s) in descs:
                banks = []
                for t3 in range(3):
                    bk = zb % 6
                    zb += 1
                    banks.append(bk)
                    for k in range(KD):
                        kb.op("pe", lambda t: t.matmul(ps[:, bk, 0:n + 2], lhsT=wu[slot][:, t3, k, :],
                                                       rhs=hT[:, k, hb:hb + n + 2],
                                                       start=(k == 0), stop=(k == KD - 1)),
                              reads=[wkey, hkey], writes=[("P", "ps", bk)], inc=(k == KD - 1))
                e = ev % 2
                ev += 1
                bb, bc, bx = banks
                kb.op("act", lambda a_: a_.activation(out=ux[e][:, 0:n + 2], in_=ps[:, bx, 0:n + 2], func=AF.Copy),
                      reads=[("P", "ps", bx)], writes=[(name, "ux", e)])
                kb.op("dve", lambda v: v.tensor_tensor(out=uu[e][:, 0:n + 2], in0=ps[:, bc, 0:n + 2],
                                                       in1=ux[e][:, 0:n + 2], op=ALU.mult),
                      reads=[("P", "ps", bc), (name, "ux", e)], writes=[(name, "uu", e)])
                kb.op("act", lambda a_: a_.activation(out=vv[e][:, 0:n], in_=uu[e][:, 1:n + 1], func=AF.Identity,
                                                      scale=cw[:, j, 1:2]),
                      reads=[(name, "uu", e), cwk], writes=[(name, "vv", e)])
                for (tap, off) in ((0, 0), (2, 2)):
                    kb.op("dve", lambda v: v.scalar_tensor_tensor(
                        out=vv[e][:, 0:n], in0=uu[e][:, off:off + n], scalar=cw[:, j, tap:tap + 1],
                        in1=vv[e][:, 0:n], op0=ALU.mult, op1=ALU.add),
                        reads=[(name, "uu", e), cwk, (name, "vv", e)], writes=[(name, "vv", e)])
                g0 = gcol[c0]
                kb.op("dve", lambda v: v.tensor_tensor(out=g[:, j, g0:g0 + n], in0=ps[:, bb, 1:n + 1],
                                                       in1=vv[e][:, 0:n], op=ALU.mult),
                      reads=[("P", "ps", bb), (name, "vv", e)], writes=[gkey])
            wi += 1
        groups = [(gcol[c0], c0, c0, n, s) for (c0, n, hb, s) in descs]
        emit_outproj(kb, opj, cst, g, gkey, KD, wout, groups, src, dst, ps, (6, 7))


def emit_gm(kb, G, cst, src, dst, a, b, with_ctx, wu_d, wv_d, vg_d, wsT_d, bsb_d, wout):
    name = "m"
    ps = G.ps
    vg = kb.sbuf(name + "_vg", [128, KG], F32)
    wsT = kb.sbuf(name + "_wsT", [128, 16, 128], F32)
    bsb = kb.sbuf(name + "_bsb", [128, 16, 128], F32)
    kb.dma("sp", DmaSem(kb, "m_d1"), vg[:], vg_d, writes=[name + "_vg"])
    kb.dma("sp", DmaSem(kb, "m_d2"), wsT[:], wsT_d, writes=[name + "_wsT"])
    kb.dma("sp", DmaSem(kb, "m_d3"), bsb[:], bsb_d, writes=[name + "_bsb"])
    nb = NormBufs(kb, name + "n", 128, G.ones, G.okey)
    TS = 512
    hT = kb.sbuf(name + "_hT", [128, KD, TS], BF16)
    uT = kb.sbuf(name + "_uT", [128, KG, TS], BF16)
    vraw = [kb.sbuf("%s_vr%d" % (name, i), [128, GMW], BF16) for i in range(4)]
    sqj = kb.sbuf(name + "_sqj", [128, 512], BF16)
    ssp = kb.sbuf(name + "_ssp", [128, 4, 8], F32)
    rst = kb.sbuf(name + "_rst", [128, 4], F32)
    NWU = 3
    wu = [kb.sbuf("%s_wu%d" % (name, i), [128, KD, 128], BF16) for i in range(NWU)]
    wus = [DmaSem(kb, "m_wus%d" % i) for i in range(NWU)]
    wv = [kb.sbuf("%s_wv%d" % (name, i), [128, KD, 512], BF16) for i in range(2)]
    wvs = [DmaSem(kb, "m_wvs%d" % i) for i in range(2)]
    wss = [kb.sbuf("%s_wss%d" % (name, i), [128, 16, 128], BF16) for i in range(2)]
    tmp = [kb.sbuf("%s_tmp%d" % (name, i), [128, 4, 128], F32) for i in range(2)]
    opj = OutProj(kb, name + "o", KG)
    hkey, ukey = name + "_hT", name + "_uT"
    zb = wu_i = wv_i = sp_i = 0
    supers = [(c0, n, 0) for (c0, n) in _cols_split(a, b, TS)]
    if with_ctx:
        supers.append((CTXC, NCTX, 1))
    for (t0, T, sset) in supers:
        ntc = T // 128
        pieces = [(c0, n, sset, c0 - t0, False) for (c0, n) in _cols_split(t0, t0 + T, nb.w)]
        emit_norm(kb, nb, cst, src, G.cmask, pieces, hT, hkey, ps[:, 6, :], ("P", "ps", 6))

        def load_wu(j, i):
            slot = i % NWU
            kb.dma("pool", wus[slot], wu[slot][:], wu_d[j, :, 0], writes=[(name, "wu", slot)])

        load_wu(0, wu_i)
        load_wu(1, wu_i + 1)
        for j in range(KG):
            if j + 2 < KG:
                load_wu(j + 2, wu_i + 2)
            slot = wu_i % NWU
            bk = zb % 4
            zb += 1
            for k in range(KD):
                kb.op("pe", lambda t: t.matmul(ps[:, bk, 0:T], lhsT=wu[slot][:, k, :], rhs=hT[:, k, 0:T],
                                               start=(k == 0), stop=(k == KD - 1)),
                      reads=[(name, "wu", slot), hkey], writes=[("P", "ps", bk)], inc=(k == KD - 1))
            kb.op("act", lambda a_: a_.activation(out=uT[:, j, 0:T], in_=ps[:, bk, 0:T], func=AF.Gelu_apprx_tanh),
                  reads=[("P", "ps", bk)], writes=[ukey])
            wu_i += 1

        def load_wv(cg, i):
            slot = i % 2
            kb.dma("pool", wvs[slot], wv[slot][:], wv_d[cg], writes=[(name, "wv", slot)])

        load_wv(0, wv_i)
        for cg in range(8):
            if cg + 1 < 8:
                load_wv(cg + 1, wv_i + 1)
            slot = wv_i % 2
            for tc in range(ntc):
                bk = zb % 4
                zb += 1
                for k in range(KD):
                    kb.op("pe", lambda t: t.matmul(ps[:, bk, :], lhsT=hT[:, k, tc * 128:(tc + 1) * 128],
                                                   rhs=wv[slot][:, k, :], start=(k == 0), stop=(k == KD - 1)),
                          reads=[(name, "wv", slot), hkey], writes=[("P", "ps", bk)], inc=(k == KD - 1))
                vslice = vraw[tc][:, cg * 512:(cg + 1) * 512]
                kb.op("act", lambda a_: a_.activation(out=vslice, in_=ps[:, bk, :], func=AF.Gelu_apprx_tanh),
                      reads=[("P", "ps", bk)], writes=[(name, "vr", tc)])
                kb.op("act", lambda a_: a_.activation(out=sqj[:], in_=vslice, func=AF.Square,
                                                      accum_out=ssp[:, tc, cg:cg + 1]),
                      reads=[(name, "vr", tc)], writes=[name + "_sqj", (name, "ssp", tc)])
            wv_i += 1

        for tc in range(ntc):
            kb.op("dve", lambda v: v.tensor_reduce(out=rst[:, tc:tc + 1], in_=ssp[:, tc, :], op=ALU.add,
                                                   axis=mybir.AxisListType.X),
                  reads=[(name, "ssp", tc)], writes=[(name, "rst", tc)])
            kb.op("act", lambda a_: a_.activation(out=rst[:, tc:tc + 1], in_=rst[:, tc:tc + 1], func=AF.Sqrt,
                                                  scale=1.0 / GMW, bias=EPS),
                  reads=[(name, "rst", tc)], writes=[(name, "rst", tc)])
            kb.op("dve", lambda v: v.reciprocal(out=rst[:, tc:tc + 1], in_=rst[:, tc:tc + 1]),
                  reads=[(name, "rst", tc)], writes=[(name, "rst", tc)])
            w = sp_i % 2
            sp_i += 1
            kb.op("dve", lambda v: v.tensor_scalar(out=wss[w][:], in0=wsT[:], scalar1=rst[:, tc:tc + 1], scalar2=None,
                                                   op0=ALU.mult),
                  reads=[name + "_wsT", (name, "rst", tc)], writes=[(name, "wss", w)])
            for q4 in range(8):
                bk = 4 + (q4 % 2)
                for cc in range(4):
                    c = q4 * 4 + cc
                    kb.op("pe", lambda t: t.matmul(ps[:, bk, cc * 128:(cc + 1) * 128],
                                                   lhsT=vraw[tc][:, c * 128:(c + 1) * 128],
                                                   rhs=wss[w][:, c // 2, :], start=True, stop=True),
                          reads=[(name, "vr", tc), (name, "wss", w)], writes=[("P", "ps", bk)], inc=(cc == 3))
                tm = tmp[q4 % 2]
                tk = (name, "tmp", q4 % 2)
                for cc in range(4):
                    c = q4 * 4 + cc
                    kb.op("dve", lambda v: v.scalar_tensor_tensor(
                        out=tm[:, cc, :], in0=ps[:, bk, cc * 128:(cc + 1) * 128], scalar=vg[:, c:c + 1],
                        in1=bsb[:, c // 2, :], op0=ALU.mult, op1=ALU.add),
                        reads=[("P", "ps", bk), name + "_vg", name + "_bsb"], writes=[tk])
                c0 = q4 * 4
                kb.op("dve", lambda v: v.tensor_tensor(out=uT[:, c0:c0 + 4, tc * 128:(tc + 1) * 128],
                                                       in0=uT[:, c0:c0 + 4, tc * 128:(tc + 1) * 128],
                                                       in1=tm[:], op=ALU.mult),
                      reads=[tk, ukey], writes=[ukey])
        emit_outproj(kb, opj, cst, uT, ukey, KG, wout, [(0, t0, t0, T, sset)], src, dst, ps, (6, 7))


def emit_na(kb, G, cst, src, dst, qb0, qb1, kv0, kvlen, ctx_q, special, wqkv, qkg_d, biasI, biasS, wo, oscr):
    name = "a"
    ps = G.ps
    psb = ps[:, 7, :].bitcast(BF16)
    ones, okey, idn = G.ones, G.okey, G.idn
    nkv = kvlen + NCTX
    nqb = qb1 - qb0
    nq = nqb * 128 + (NCTX if ctx_q else 0)
    nvb = nkv // 128
    qkg = kb.sbuf(name + "_qkg", [128, 2], F32)
    kb.dma("sp", DmaSem(kb, "a_d1"), qkg[:], qkg_d, writes=[name + "_qkg"])
    kb.op("dve", lambda v: v.tensor_scalar(out=qkg[:, 0:1], in0=qkg[:, 0:1], scalar1=float(128 ** -0.5), scalar2=None,
                                           op0=ALU.mult), reads=[name + "_qkg"], writes=[name + "_qkg"])
    nb = NormBufs(kb, name + "n", 64, ones, okey)
    hT = kb.sbuf(name + "_hT", [128, KD, nkv], BF16)
    hkey = name + "_hT"
    wq = [kb.sbuf("%s_wq%d" % (name, i), [128, 3, KD, 128], BF16) for i in range(2)]
    wqs = [DmaSem(kb, "a_wqs%d" % i) for i in range(2)]
    qT = kb.sbuf(name + "_qT", [128, nq], BF16)
    kT = kb.sbuf(name + "_kT", [128, nkv], BF16)
    vT = kb.sbuf(name + "_vT", [128, nkv], BF16)
    V = kb.sbuf(name + "_V", [128, nvb, 128], BF16)
    qraw = [kb.sbuf("%s_qraw%d" % (name, i), [128, 512], F32) for i in range(2)]
    sq = kb.sbuf(name + "_sqh", [128, 512], BF16)
    rs = kb.sbuf(name + "_rsh", [128, 512], F32)
    bI = [kb.sbuf("%s_bI%d" % (name, i), [128, 5, 128], F32) for i in range(2)]
    bIs = [DmaSem(kb, "a_bIs%d" % i) for i in range(2)]
    bS = [kb.sbuf("%s_bS%d" % (name, i), [128, 6, 128], F32) for i in range(2)]
    bSs = [DmaSem(kb, "a_bSs%d" % i) for i in range(2)]
    sb = [kb.sbuf("%s_sb%d" % (name, i), [128, 6, 128], F32) for i in range(2)]
    PT = [kb.sbuf("%s_PT%d" % (name, i), [128, 8, 128], BF16) for i in range(2)]
    rden = [kb.sbuf("%s_rden%d" % (name, i), [128, 128], F32) for i in range(2)]
    oTh = [kb.sbuf("%s_oTh%d" % (name, i), [128, nq], BF16) for i in range(2)]
    oTs = [DmaSem(kb, "a_oTs%d" % i) for i in range(2)]
    opj = OutProj(kb, name + "o", NH)
    P = ("P", "ps")

    pieces = [(c0, n, 0, c0 - kv0, False) for (c0, n) in _cols_split(kv0, kv0 + kvlen, nb.w)]
    pieces += [(c0, n, 1, kvlen + c0 - CTXC, False) for (c0, n) in _cols_split(CTXC, CTXC + NCTX, nb.w)]
    emit_norm(kb, nb, cst, src, G.cmask, pieces, hT, hkey, ps[:, 4, :], P + (4,))

    q_tiles = [(c0 - kv0, n, c0 - qb0 * 128) for (c0, n) in _cols_split(qb0 * 128, qb1 * 128, 512)]
    if ctx_q:
        q_tiles.append((kvlen, NCTX, nqb * 128))
    kv_tiles = [(c0, n, c0) for (c0, n) in _cols_split(0, nkv, 512)]
    zb = qr_i = sb_i = pv_i = bs_i = 0

    def load_w(h):
        slot = h % 2
        kb.dma("pool", wqs[slot], wq[slot][:], wqkv[h], writes=[(name, "wq", slot)])

    load_w(0)
    for h in range(NH):
        if h + 1 < NH:
            load_w(h + 1)
        hs = h % 2
        wkey = (name, "wq", hs)
        w = wq[hs]
        kb.dma("sp", bIs[hs], bI[hs][:], biasI[h], writes=[(name, "bI", hs)])
        ptiles = []
        for t3 in range(3):
            for (c0, n, dc0) in (q_tiles if t3 == 0 else kv_tiles):
                ptiles.append((t3, c0, n, dc0))
        pbank = {}

        def proj_mm(i):
            nonlocal zb
            (t3, c0, n, dc0) = ptiles[i]
            bk = zb % 4
            zb += 1
            pbank[i] = bk
            for k in range(KD):
                kb.op("pe", lambda t: t.matmul(ps[:, bk, 0:n], lhsT=w[:, t3, k, :], rhs=hT[:, k, c0:c0 + n],
                                               start=(k == 0), stop=(k == KD - 1)),
                      reads=[wkey, hkey], writes=[P + (bk,)], inc=(k == KD - 1))

        def proj_post(i):
            nonlocal qr_i
            (t3, c0, n, dc0) = ptiles[i]
            bk = pbank[i]
            if t3 == 2:
                kb.op("act", lambda a_: a_.activation(out=vT[:, c0:c0 + n], in_=ps[:, bk, 0:n], func=AF.Copy),
                      reads=[P + (bk,)], writes=[name + "_vT"])
                return
            r = qr_i % 2
            qr_i += 1
            kb.op("dve", lambda v: v.tensor_copy(out=qraw[r][:, 0:n], in_=ps[:, bk, 0:n]),
                  reads=[P + (bk,)], writes=[(name, "qraw", r)])
            kb.op("act", lambda a_: a_.activation(out=sq[:, 0:n], in_=qraw[r][:, 0:n], func=AF.Square),
                  reads=[(name, "qraw", r)], writes=[name + "_sqh"])
            kb.op("pe", lambda t: t.matmul(ps[:, 4, 0:n], lhsT=ones[:], rhs=sq[:, 0:n], start=True, stop=True),
                  reads=[name + "_sqh", okey], writes=[P + (4,)])
            kb.op("act", lambda a_: a_.activation(out=rs[:, 0:n], in_=ps[:, 4, 0:n], func=AF.Ln,
                                                  scale=1.0 / 128, bias=EPS),
                  reads=[P + (4,)], writes=[name + "_rsh"])
            kb.op("act", lambda a_: a_.activation(out=rs[:, 0:n], in_=rs[:, 0:n], func=AF.Exp, scale=-0.5),
                  reads=[name + "_rsh"], writes=[name + "_rsh"])
            dstt = qT if t3 == 0 else kT
            dkey = name + ("_qT" if t3 == 0 else "_kT")
            kb.op("dve", lambda v: v.scalar_tensor_tensor(out=dstt[:, dc0:dc0 + n], in0=qraw[r][:, 0:n],
                                                          scalar=qkg[:, t3:t3 + 1], in1=rs[:, 0:n],
                                                          op0=ALU.mult, op1=ALU.mult),
                  reads=[(name, "qraw", r), name + "_qkg", name + "_rsh"], writes=[dkey])

        proj_mm(0)
        for i in range(len(ptiles)):
            if i + 1 < len(ptiles):
                proj_mm(i + 1)
            proj_post(i)
        for b0 in range(0, nvb, 8):
            nbk = min(8, nvb - b0)
            for i in range(nbk):
                kb.op("pe", lambda t: t.transpose(psb[:, i * 128:(i + 1) * 128],
                                                  vT[:, (b0 + i) * 128:(b0 + i + 1) * 128], idn[:]),
                      reads=[name + "_vT", G.ikey], writes=[P + (7,)], inc=(i == nbk - 1))
            kb.op("dve", lambda v: v.tensor_copy(out=V[:, b0:b0 + nbk, :],
                                                 in_=psb[:, 0:nbk * 128].rearrange("p (b c) -> p b c", c=128)),
                  reads=[P + (7,)], writes=[name + "_V"])
        oT = oTh[hs]
        okh = (name, "oTh", hs)
        qblocks = []
        for bq in range(qb0, qb1):
            st, npair, sidx = special.get(bq, (-4, 5, -1))
            qblocks.append(("lat", bq, st, npair, sidx, (bq - qb0) * 128))
        if ctx_q:
            qblocks += [("ctx", 0, 0, 0, -1, nqb * 128), ("ctx", 0, 0, 0, -1, nqb * 128 + 128)]
        bst = {}

        def stage_a(i):
            nonlocal sb_i, bs_i
            (kind, bq, start, npair, sidx, qc0) = qblocks[i]
            s2 = sb_i % 2
            sb_i += 1
            chunks = []
            if kind == "lat":
                for p in range(npair):
                    er = 2 * bq + start + 2 * p
                    kc = er * 64 - kv0
                    assert 0 <= kc and kc + 128 <= kvlen, (bq, er, kv0, kvlen)
                    chunks.append((kc, kc // 128, p))
            for cc in range(2):
                chunks.append((kvlen + cc * 128, kvlen // 128 + cc, 6 + cc))
            if sidx >= 0:
                bsl = bs_i % 2
                bs_i += 1
                kb.dma("sp", bSs[bsl], bS[bsl][:], biasS[sidx, h], writes=[(name, "bS", bsl)])
                bias_ap, bias_key = bS[bsl], (name, "bS", bsl)
            else:
                bias_ap, bias_key = bI[hs], (name, "bI", hs)
            bkA, bkB = 2 * s2, 2 * s2 + 1
            for (kc, vb, slot) in chunks:
                bk = bkA if slot < 4 else bkB
                col = (slot % 4) * 128
                kb.op("pe", lambda t: t.matmul(ps[:, bk, col:col + 128], lhsT=kT[:, kc:kc + 128],
                                               rhs=qT[:, qc0:qc0 + 128], start=True, stop=True),
                      reads=[name + "_kT", name + "_qT"], writes=[P + (bk,)], inc=(slot == 7))
            bst[i] = (s2, chunks, bias_ap, bias_key)

        def stage_b(i):
            (kind, bq, start, npair, sidx, qc0) = qblocks[i]
            (s2, chunks, bias_ap, bias_key) = bst[i]
            bkA, bkB = 2 * s2, 2 * s2 + 1
            if kind == "lat":
                n1 = min(npair, 4)
                kb.op("dve", lambda v: v.tensor_tensor(out=sb[s2][:, 0:n1, :],
                                                       in0=ps[:, bkA, 0:n1 * 128].rearrange("p (c q) -> p c q", q=128),
                                                       in1=bias_ap[:, 0:n1, :], op=ALU.add),
                      reads=[P + (bkA,), bias_key], writes=[(name, "sb", s2)])
                n2 = npair - 4
                kb.op("dve", lambda v: v.tensor_tensor(out=sb[s2][:, 4:4 + n2, :],
                                                       in0=ps[:, bkB, 0:n2 * 128].rearrange("p (c q) -> p c q", q=128),
                                                       in1=bias_ap[:, 4:4 + n2, :], op=ALU.add),
                      reads=[P + (bkB,), bias_key], writes=[(name, "sb", s2)])
                kb.op("act", lambda a_: a_.activation(out=PT[s2][:, 0:npair, :], in_=sb[s2][:, 0:npair, :],
                                                      func=AF.Exp),
                      reads=[(name, "sb", s2)], writes=[(name, "PT", s2)])
            kb.op("act", lambda a_: a_.activation(out=PT[s2][:, 6:8, :],
                                                  in_=ps[:, bkB, 256:512].rearrange("p (c q) -> p c q", q=128),
                                                  func=AF.Exp),
                  reads=[P + (bkB,)], writes=[(name, "PT", s2)])

        def stage_c(i):
            nonlocal pv_i
            (kind, bq, start, npair, sidx, qc0) = qblocks[i]
            (s2, chunks, bias_ap, bias_key) = bst[i]
            pv = pv_i % 2
            pv_i += 1
            pvb = 5 + pv
            nch = len(chunks)
            for ii, (kc, vb, slot) in enumerate(chunks):
                kb.op("pe", lambda t: t.matmul(ps[:, pvb, 0:128], lhsT=V[:, vb, :], rhs=PT[s2][:, slot, :],
                                               start=(ii == 0), stop=(ii == nch - 1)),
                      reads=[name + "_V", (name, "PT", s2)], writes=[P + (pvb,)], inc=False)
            for ii, (kc, vb, slot) in enumerate(chunks):
                kb.op("pe", lambda t: t.matmul(ps[:, pvb, 128:256], lhsT=ones[:], rhs=PT[s2][:, slot, :],
                                               start=(ii == 0), stop=(ii == nch - 1)),
                      reads=[okey, (name, "PT", s2)], writes=[P + (pvb,)], inc=(ii == nch - 1))
            kb.op("act", lambda a_: a_.activation(out=rden[pv][:], in_=ps[:, pvb, 128:256], func=AF.Ln),
                  reads=[P + (pvb,)], writes=[(name, "rden", pv)])
            kb.op("act", lambda a_: a_.activation(out=rden[pv][:], in_=rden[pv][:], func=AF.Exp, scale=-1.0),
                  reads=[(name, "rden", pv)], writes=[(name, "rden", pv)])
            kb.op("dve", lambda v: v.tensor_tensor(out=oT[:, qc0:qc0 + 128], in0=ps[:, pvb, 0:128],
                                                   in1=rden[pv][:], op=ALU.mult),
                  reads=[P + (pvb,), (name, "rden", pv)], writes=[okh])

        nblk = len(qblocks)
        for i in range(nblk + 1):
            if i < nblk:
                stage_a(i)
                stage_b(i)
            if i >= 1:
                stage_c(i - 1)
        kb.dma("sp", oTs[hs], oscr[h, :, 0:nq], oT[:, 0:nq], reads=[okh], writes=[name + "_oscr"], partial=True)
    lds = DmaSem(kb, "a_old")
    for h in range(NH):
        kb.dma("sp", lds, hT[:, h, 0:nq], oscr[h, :, 0:nq], reads=[name + "_oscr"], writes=[hkey], partial=(h > 0))
    groups = [(c0 - qb0 * 128, c0, c0, n, 0) for (c0, n) in _cols_split(qb0 * 128, qb1 * 128, 512)]
    if ctx_q:
        groups.append((nqb * 128, CTXC, CTXC, NCTX, 1))
    emit_outproj(kb, opj, cst, hT, hkey, NH, wo, groups, src, dst, ps, (5, 6))


def ada_jobs(kb, G, l, modt, modkey, wa, w, ws):
    ps = G.ps
    jobs = []

    def load(ch):
        kb.dma("pool", ws[ch % 2], w[ch % 2][:], wa[l, ch], writes=["ada_w%d" % (ch % 2)])

    def job(ch):
        if ch == 0:
            load(0)
        if ch + 1 < 96:
            load(ch + 1)
        wt = w[ch % 2]
        for k in range(KD):
            kb.op("pe", lambda t: t.matmul(ps[:, 5, 0:2], lhsT=wt[:, k, :], rhs=G.sg[:, k, :],
                                           start=(k == 0), stop=(k == KD - 1)),
                  reads=["ada_w%d" % (ch % 2), "G_sg"], writes=[("P", "ps", 5)], inc=(k == KD - 1))
        kb.op("act", lambda a_: a_.activation(out=modt[:, :, ch // 16, ch % 16], in_=ps[:, 5, 0:2],
                                              func=AF.Identity, bias=G.bada[:, l, ch:ch + 1]),
              reads=[("P", "ps", 5), "G_bada"], writes=[modkey])

    for ch in range(96):
        jobs.append(lambda ch=ch: job(ch))
    return jobs


def ada_bufs(kb):
    w = [kb.sbuf("ada_w%d" % i, [128, KD, 128], BF16) for i in range(2)]
    ws = [DmaSem(kb, "ada_ws%d" % i) for i in range(2)]
    return w, ws


class Globals:
    pass


R_FFN0 = (384, 3200)
R_GM1 = (384, 3200)
R_FFN1 = (446, 3138)
R_SC2 = (447, 3137)
R_FFN2 = (448, 3136)
NA_SPECIAL = {6: (-4, 6, 0), 7: (-4, 5, 1), 20: (-4, 5, 2), 21: (-6, 6, 3)}
NA0_PASSES = [(2, 14, 0, 2048, True), (14, 26, 1536, 2048, False)]
NA3_PASSES = [(5, 14, 384, 1664, False), (14, 23, 1536, 1664, False)]


def build_fused(nc, n_phases=99, dbg_out=None):
    def din(name, shape, dt=F32):
        return nc.dram_tensor(name, shape, dt, kind="ExternalInput").ap()
    xin = din("xin", [128, KD, NCA])
    cmask = din("cmask", [128, NCA])
    cond = din("cond", [128, KD, 2])
    wa = din("wa", [4, 96, 128, KD, 128])
    ba = din("ba", [128, 4, 96])
    ngm = din("ngm", [4, 128, 16])
    ngf = din("ngf", [4, 128, 16])
    na_wqkv = din("na_wqkv", [2, NH, 128, 3, KD, 128])
    na_qkg = din("na_qkg", [2, 128, 2])
    na_bI = din("na_bI", [2, NH, 128, 5, 128])
    na_bS = din("na_bS", [2, 4, NH, 128, 6, 128])
    na_wo = din("na_wo", [2, KD, 128, NH, 128])
    gm_wu = din("gm_wu", [KG, 128, 1, KD, 128])
    gm_wv = din("gm_wv", [8, 128, KD, 512])
    gm_vg = din("gm_vg", [128, KG])
    gm_wsT = din("gm_wsT", [128, 16, 128])
    gm_bsb = din("gm_bsb", [128, 16, 128])
    gm_wout = din("gm_wout", [KD, 128, KG, 128])
    sc_win = din("sc_win", [KD, 128, 3, KD, 128])
    sc_cw = din("sc_cw", [128, KD, 3])
    sc_wout = din("sc_wout", [KD, 128, KD, 128])
    f_wup = din("f_wup", [4, KF, 128, 2, KD, 128])
    f_cwb = din("f_cwb", [4, 128, 2, KF, 4])
    f_wdn = din("f_wdn", [4, KD, 128, KF, 128])
    yT = nc.dram_tensor("yT", [128, KD, NLAT], F32, kind="ExternalOutput").ap()
    kind = "ExternalOutput" if dbg_out else "Internal"
    XA = nc.dram_tensor("XA", [128, KD, NCA], F32, kind=kind).ap()
    XB = nc.dram_tensor("XB", [128, KD, NCA], F32, kind=kind).ap()
    oscr = nc.dram_tensor("oscr", [NH, 128, 12 * 128 + NCTX], BF16, kind="Internal").ap()
    with ExitStack() as es:
        kb = KB(nc, es)
        G = Globals()
        G.ps = kb.psum("ps", [128, 8, 512], F32)
        G.cmask = cmask
        G.ones = kb.sbuf("G_ones", [128, 128], BF16)
        G.okey = "G_ones"
        idf = kb.sbuf("G_idf", [128, 128], F32)
        G.idn = kb.sbuf("G_idn", [128, 128], BF16)
        G.ikey = "G_idn"
        cs = kb.sbuf("G_cs", [128, KD, 2], F32)
        G.sg = kb.sbuf("G_sg", [128, KD, 2], BF16)
        G.bada = kb.sbuf("G_bada", [128, 4, 96], F32)
        modt = [kb.sbuf("G_mod%d" % l, [128, 2, 6, 16], F32) for l in range(4)]
        kb.op("dve", lambda v: v.memset(G.ones[:], 1.0), writes=["G_ones"])
        kb.op("pool", lambda g_: g_.memset(idf[:], 1.0), writes=["G_idf"])
        kb.op("pool", lambda g_: g_.affine_select(out=idf[:], in_=idf[:], pattern=[[-1, 128]], compare_op=ALU.is_equal,
                                                 fill=0.0, base=0, channel_multiplier=1),
              reads=["G_idf"], writes=["G_idf"])
        kb.op("dve", lambda v: v.tensor_copy(out=G.idn[:], in_=idf[:]), reads=["G_idf"], writes=["G_idn"])
        kb.dma("sp", DmaSem(kb, "g_d1"), cs[:], cond, writes=["G_cs"])
        kb.dma("sp", DmaSem(kb, "g_d2"), G.bada[:], ba, writes=["G_bada"])
        kb.op("act", lambda a_: a_.activation(out=G.sg[:], in_=cs[:], func=AF.Silu), reads=["G_cs"], writes=["G_sg"])
        kb.barrier()
        ph = [0]

        def phase(fn):
            ph[0] += 1
            if ph[0] > n_phases:
                return
            with ExitStack() as pes:
                kb.cur_es = pes
                fn()
                kb.barrier()
            kb.cur_es = es

        def mixer_consts(l):
            return make_consts(kb, "cm%d" % l, modt[l], "G_mod%d" % l, ngm[l], 0)

        def ffn_consts(l):
            return make_consts(kb, "cf%d" % l, modt[l], "G_mod%d" % l, ngf[l], 1)

        def ada_phase(l):
            w, ws = ada_bufs(kb)
            for j in ada_jobs(kb, G, l, modt[l], "G_mod%d" % l, wa, w, ws):
                j()

        def ffn_phase(l, src, dst, dst_off, rng, with_ctx):
            side = None
            if l + 1 < 4:
                w, ws = ada_bufs(kb)
                side = ada_jobs(kb, G, l + 1, modt[l + 1], "G_mod%d" % (l + 1), wa, w, ws)
            emit_ffn(kb, G, ffn_consts(l), src, dst, dst_off, rng[0], rng[1], with_ctx,
                     f_wup[l], f_cwb[l], f_wdn[l], up_side=side)

        phase(lambda: ada_phase(0))
        for (qb0, qb1, kv0, kvlen, cq) in NA0_PASSES:
            phase(lambda: emit_na(kb, G, mixer_consts(0), xin, XA, qb0, qb1, kv0, kvlen, cq, NA_SPECIAL,
                                  na_wqkv[0], na_qkg[0], na_bI[0], na_bS[0], na_wo[0], oscr))
        phase(lambda: ffn_phase(0, XA, XB, 0, R_FFN0, True))
        phase(lambda: emit_gm(kb, G, mixer_consts(1), XB, XA, R_GM1[0], R_GM1[1], True,
                              gm_wu, gm_wv, gm_vg, gm_wsT, gm_bsb, gm_wout))
        phase(lambda: ffn_phase(1, XA, XB, 0, R_FFN1, True))
        phase(lambda: emit_sc(kb, G, mixer_consts(2), XB, XA, R_SC2[0], R_SC2[1], True, sc_win, sc_cw, sc_wout))
        phase(lambda: ffn_phase(2, XA, XB, 0, R_FFN2, True))
        for (qb0, qb1, kv0, kvlen, cq) in NA3_PASSES:
            phase(lambda: emit_na(kb, G, mixer_consts(3), XB, XA, qb0, qb1, kv0, kvlen, cq, NA_SPECIAL,
                                  na_wqkv[1], na_qkg[1], na_bI[1], na_bS[1], na_wo[1], oscr))
        phase(lambda: ffn_phase(3, XA, yT, OWN0, (OWN0, OWN1), False))
        kb.finish()
        print("fused program: ins=%d waits=%d sems=%d" % (kb.n_ins, kb.n_wait, kb.nsem))
    return nc


def fm(v):
    v = np.asarray(v, np.float32)
    k = v.shape[-1] // 128
    r = v.reshape(v.shape[:-1] + (k, 128))
    return np.ascontiguousarray(np.moveaxis(r, -1, 0))


def tile_wup(w_up):
    w = w_up.reshape(KD, 128, 2, KF, 128)
    return np.ascontiguousarray(w.transpose(3, 1, 2, 0, 4))


def tile_wdn(w_down):
    nk = w_down.shape[0] // 128
    w = w_down.reshape(nk, 128, KD, 128)
    return np.ascontiguousarray(w.transpose(2, 1, 0, 3))


def tile_win(w_in, nt):
    nj = w_in.shape[1] // (nt * 128)
    w = w_in.reshape(KD, 128, nt, nj, 128)
    return np.ascontiguousarray(w.transpose(3, 1, 2, 0, 4))


def _unused_tile_ada(w_ada_all, b_ada_all, core):
    L = w_ada_all.shape[0]
    per = L * 12288 // NCORES
    cpl = 12288 // per
    l, part = core // cpl, core % cpl
    w = w_ada_all[l][:, part * per:(part + 1) * per]
    w = w.reshape(KD, 128, ADA_G, 512)
    wt = np.ascontiguousarray(w.transpose(2, 1, 0, 3))
    b = b_ada_all[l][part * per:(part + 1) * per].reshape(ADA_CH, 128)
    return wt, np.ascontiguousarray(b.T)


def tile_cwb(conv_w, conv_b):
    a = np.concatenate([conv_w, conv_b[None]], 0)
    a = a.reshape(4, 2, KF, 128)
    return np.ascontiguousarray(a.transpose(3, 1, 2, 0))


def from_out(yT):
    return np.ascontiguousarray(yT.transpose(2, 1, 0).reshape(yT.shape[2], D))


def gm_weights(w_in, v_g, w_s, b_s, w_out):
    wu = tile_win(w_in[:, :GMW], 1)
    wv = w_in[:, GMW:].reshape(KD, 128, 8, 512).transpose(2, 1, 0, 3)
    return {
        "wu": wu,
        "wv": np.ascontiguousarray(wv),
        "vg": fm(v_g),
        "wsT": np.ascontiguousarray(w_s.transpose(2, 0, 1)),
        "bsb": np.ascontiguousarray(np.broadcast_to(b_s[None], (128, 16, 128))),
        "wout": tile_wdn(w_out),
    }


def _na_block_bias(rpb, r0, start, npair, rows=128):
    qq = np.arange(128)
    q_row, q_col = r0 + qq // 64, qq % 64
    kk = np.arange(128)
    r_start = np.clip(q_row - 4, 0, rows - 8)
    c_start = np.clip(q_col - 8, 0, 64 - 16)
    out = np.full((NH, 128, 6, 128), -30000.0, np.float32)
    for p in range(npair):
        k_row = (r0 + start + 2 * p + kk // 64)[:, None]
        k_col = (kk % 64)[:, None]
        ok = ((k_row >= 0) & (k_row < rows) & (k_row >= r_start[None]) & (k_row < r_start[None] + 8)
              & (k_col >= c_start[None]) & (k_col < c_start[None] + 16))
        dr = np.clip(k_row - q_row[None], -7, 7) + 7
        dc = np.clip(k_col - q_col[None], -15, 15) + 15
        out[:, :, p, :] = np.where(ok[None], rpb[:, dr, dc], np.float32(-30000.0))
    return out


def na_bias(rpb, core):
    R0 = (core % 4) * 32
    bI = _na_block_bias(rpb, 64, -4, 5)[:, :, 0:5, :]
    bS = np.stack([
        _na_block_bias(rpb, R0 + 0, -4, 6),
        _na_block_bias(rpb, R0 + 2, -4, 5),
        _na_block_bias(rpb, R0 + 28, -4, 5),
        _na_block_bias(rpb, R0 + 30, -6, 6),
    ], 0)
    return np.ascontiguousarray(bI), np.ascontiguousarray(bS)


def ext_cols(xfull, ctxfull, core):
    b, q = core // 4, core % 4
    t0 = q * NLAT - OWN0
    ext = np.zeros((NCA, D), np.float32)
    lo, hi = max(t0, 0), min(t0 + NE, xfull.shape[1])
    ext[lo - t0:hi - t0] = xfull[b, lo:hi]
    ext[CTXC:] = ctxfull[b]
    return np.ascontiguousarray(ext.reshape(NCA, KD, 128).transpose(2, 1, 0))


def cmask_for(core, seq):
    q = core % 4
    t0 = q * NLAT - OWN0
    g = t0 + np.arange(NE)
    m = np.ones((NCA,), np.float32)
    m[:NE] = ((g >= 0) & (g < seq)).astype(np.float32)
    return np.ascontiguousarray(np.broadcast_to(m[None], (128, NCA)))


_PROG = {}


def kernel(x, c, ctx, c_ctx, norm_mix_g, norm_ffn_g, w_ada, b_ada,
           na_w_qkv, na_q_g, na_k_g, na_rpb, na_w_o,
           gm_w_in, gm_v_g, gm_w_s, gm_b_s, gm_w_out,
           sc_w_in, sc_conv_w, sc_w_out,
           ffn_w_up, ffn_conv_w, ffn_conv_b, ffn_w_down):
    f32 = lambda a: np.ascontiguousarray(np.asarray(a, np.float32))
    x, c, ctx, c_ctx = f32(x), f32(c), f32(ctx), f32(c_ctx)
    w_ada, b_ada = f32(w_ada), f32(b_ada)
    if "nc" not in _PROG:
        nc = bass.Bass("TRN2", target_bir_lowering=False)
        build_fused(nc)
        _PROG["nc"] = nc
    nc = _PROG["nc"]
    shared = {
        "wa": np.ascontiguousarray(w_ada.reshape(4, KD, 128, 96, 128).transpose(0, 3, 2, 1, 4)),
        "ba": fm(b_ada),
        "ngm": np.ascontiguousarray(fm(f32(norm_mix_g)).transpose(1, 0, 2)),
        "ngf": np.ascontiguousarray(fm(f32(norm_ffn_g)).transpose(1, 0, 2)),
        "na_wqkv": np.stack([tile_win(f32(na_w_qkv[j]), 3) for j in range(2)], 0),
        "na_qkg": np.ascontiguousarray(np.stack([f32(na_q_g), f32(na_k_g)], -1)),
        "na_wo": np.stack([tile_wdn(f32(na_w_o[j])) for j in range(2)], 0),
        "sc_win": tile_win(f32(sc_w_in[0]), 3),
        "sc_cw": np.ascontiguousarray(fm(f32(sc_conv_w[0])).transpose(0, 2, 1)),
        "sc_wout": tile_wdn(f32(sc_w_out[0])),
        "f_wup": np.stack([tile_wup(f32(ffn_w_up[i])) for i in range(4)], 0),
        "f_cwb": np.stack([tile_cwb(f32(ffn_conv_w[i]), f32(ffn_conv_b[i])) for i in range(4)], 0),
        "f_wdn": np.stack([tile_wdn(f32(ffn_w_down[i])) for i in range(4)], 0),
    }
    gw = gm_weights(f32(gm_w_in[0]), f32(gm_v_g[0]), f32(gm_w_s[0]), f32(gm_b_s[0]), f32(gm_w_out[0]))
    for k, v in gw.items():
        shared["gm_" + k] = v
    rpb = f32(na_rpb)
    in_maps = []
    for core in range(NCORES):
        b = core // 4
        cond = np.stack([c[b], c_ctx], 0)
        bias = [na_bias(rpb[j], core) for j in range(2)]
        d = dict(shared)
        d["xin"] = ext_cols(x, ctx, core)
        d["cmask"] = cmask_for(core, x.shape[1])
        d["cond"] = np.ascontiguousarray(fm(cond).transpose(0, 2, 1))
        d["na_bI"] = np.stack([bias[0][0], bias[1][0]], 0)
        d["na_bS"] = np.stack([bias[0][1], bias[1][1]], 0)
        in_maps.append(d)
    res = run_bass_kernel_spmd(nc, in_maps, core_ids=list(range(NCORES)))
    out = np.empty_like(x)
    for core in range(NCORES):
        b, q = core // 4, core % 4
        out[b, q * NLAT:(q + 1) * NLAT] = from_out(np.asarray(res.results[core]["yT"]))
    return out
```

```python
import numpy as np
from contextlib import ExitStack
import concourse.bass as bass
import concourse.mybir as mybir
from concourse.bass_utils import run_bass_kernel_spmd

F32 = mybir.dt.float32
BF16 = mybir.dt.bfloat16
AF = mybir.ActivationFunctionType
ALU = mybir.AluOpType

D = 2048
KD = 16
FF = 5504
KF = 43
NLAT = 2048
NCTX = 256
EPS = 1e-6
NCORES = 8

SEM_ROT = 30000


class _DmaSem:
    def __init__(self, kb, name):
        self.kb = kb
        self.name = name
        self.sem = kb.newsem(name)
        self.n = 0

    def next(self):
        if (self.n + 1) * 16 > 60000:
            self.sem = self.kb.newsem(self.name)
            self.n = 0
        self.n += 1
        return self.sem, self.n * 16


def DmaSem(kb, name):
    d = kb.dsem_pool.get(name)
    if d is None:
        d = kb.dsem_pool[name] = _DmaSem(kb, name)
    return d


class KB:
    def __init__(self, nc, es):
        self.nc = nc
        self.es = es
        self.eng = {"pe": nc.tensor, "act": nc.scalar, "dve": nc.vector, "pool": nc.gpsimd, "sp": nc.sync}
        self.nsem = 0
        self.dsem_pool = {}
        self.cur_es = es
        self.nbuf = 0
        self.csem = {}
        self.cnt = {}
        for e in ("pe", "act", "dve", "pool"):
            self.csem[e] = self.newsem("c_" + e)
            self.cnt[e] = 0
        self.waited = {e: {} for e in self.eng}
        self.res = {}
        self.pending = {e: [] for e in self.eng}
        self.store_toks = []
        self.n_wait = 0
        self.n_ins = 0

    def newsem(self, name):
        self.nsem += 1
        return self.es.enter_context(self.nc.semaphore("%s_%d" % (name, self.nsem)))

    def sbuf(self, name, shape, dt):
        self.nbuf += 1
        return self.cur_es.enter_context(self.nc.sbuf_tensor("%s_%d" % (name, self.nbuf), shape, dt))

    def barrier(self):
        for e in self.pending:
            assert not self.pending[e], e
        toks = [(self.csem[e], self.cnt[e], e) for e in self.csem if self.cnt[e] > 0]
        toks += [(d.sem, d.n * 16, "dma") for d in self.dsem_pool.values() if d.n > 0]
        for e in self.eng:
            for tok in toks:
                self._wait(e, tok, True)
        self.res = {}
        self.store_toks = []

    def psum(self, name, shape, dt):
        return self.es.enter_context(self.nc.psum_tensor(name, shape, dt))

    def _wait(self, e, tok, raw):
        sem, val, src = tok
        if src == e and not raw:
            return
        w = self.waited[e]
        if w.get(id(sem), 0) >= val:
            return
        self.eng[e].wait_ge(sem, val)
        self.n_wait += 1
        w[id(sem)] = val

    def _deps(self, e, reads, writes):
        for k in reads:
            st = self.res.get(k)
            if st:
                for tok in st["w"]:
                    self._wait(e, tok, True)
                if isinstance(k, tuple) and len(k) > 1 and k[1] == "ps":
                    for tok in st["r"]:
                        self._wait(e, tok, False)
        for k in writes:
            st = self.res.get(k)
            if st:
                for tok in st["w"]:
                    self._wait(e, tok, False)
                for tok in st["r"]:
                    self._wait(e, tok, False)

    def _reg(self, tok, reads, writes):
        for k in writes:
            self.res[k] = {"w": [tok], "r": []}
        for k in reads:
            st = self.res.get(k)
            if st is None:
                st = self.res[k] = {"w": [], "r": []}
            st["r"].append(tok)
            if len(st["r"]) > 64:
                st["r"] = st["r"][-32:]

    def op(self, e, fn, reads=(), writes=(), inc=True):
        self._deps(e, reads, writes)
        ins = fn(self.eng[e])
        self.n_ins += 1
        if not inc:
            self.pending[e].append((tuple(reads), tuple(writes)))
            return ins
        if self.cnt[e] >= SEM_ROT:
            self.csem[e] = self.newsem("c_" + e)
            self.cnt[e] = 0
        self.cnt[e] += 1
        ins.then_inc(self.csem[e], 1)
        tok = (self.csem[e], self.cnt[e], e)
        for (r, w) in self.pending[e]:
            self._reg(tok, r, w)
        self.pending[e] = []
        self._reg(tok, reads, writes)
        return ins

    def dma(self, q, dsem, out, in_, reads=(), writes=(), store=False, partial=False):
        self._deps(q, reads, writes)
        ins = self.eng[q].dma_start(out=out, in_=in_)
        self.n_ins += 1
        sem, val = dsem.next()
        ins.then_inc(sem, 16)
        tok = (sem, val, "dma")
        if partial:
            for k in writes:
                st = self.res.get(k)
                if st is None:
                    st = self.res[k] = {"w": [], "r": []}
                st["w"] = [t for t in st["w"] if t[0] is not sem] + [tok]
            self._reg(tok, reads, ())
        else:
            self._reg(tok, reads, writes)
        if store:
            self.store_toks.append(tok)
        return ins

    def finish(self):
        for tok in self.store_toks:
            self._wait("sp", tok, True)
        self.store_toks = []


NE = 3584
OWN0 = 768
OWN1 = OWN0 + NLAT
CTXC = NE
NCA = NE + NCTX
NH = 16
GMW = 4096
KG = 32
ADA_NG = 24


def _split_even(a, b, maxn):
    n = b - a
    k = -(-n // maxn)
    out = []
    c = a
    for i in range(k):
        m = n // k + (1 if i < n % k else 0)
        out.append((c, m))
        c += m
    return out


def _cols_split(c0, c1, step):
    out = []
    c = c0
    while c < c1:
        e = min(c + step, c1)
        out.append((c, e - c))
        c = e
    return out


class Consts:
    pass


def make_consts(kb, name, modt, modkey, ng_ap, which):
    c = Consts()
    c.mod = modt
    c.ng = kb.sbuf(name + "_ng", [128, 16], F32)
    c.A = kb.sbuf(name + "_A", [128, 2, 16], F32)
    kb.dma("sp", DmaSem(kb, "cs_ng"), c.ng[:], ng_ap, writes=[name + "_ng"])
    o = 3 * which
    for s in range(2):
        kb.op("dve", lambda v: v.scalar_tensor_tensor(out=c.A[:, s, :], in0=c.mod[:, s, o + 1, :], scalar=1.0,
                                                     in1=c.ng[:], op0=ALU.add, op1=ALU.mult),
              reads=[modkey, name + "_ng"], writes=[name + "_A%d" % s])
    c.Akey = lambda s: name + "_A%d" % s
    c.o = o
    c.Bap = lambda s, k: c.mod[:, s, o + 0, k:k + 1]
    c.Gap = lambda s, k: c.mod[:, s, o + 2, k:k + 1]
    c.Aap = lambda s, k: c.A[:, s, k:k + 1]
    c.modkey = modkey
    return c


class NormBufs:
    def __init__(self, kb, name, w, ones, okey):
        self.w = w
        self.xn = [kb.sbuf("%s_xn%d" % (name, i), [128, KD, w], F32) for i in range(2)]
        self.sq = kb.sbuf(name + "_sq", [128, KD, w], BF16)
        self.rs = kb.sbuf(name + "_rs", [128, w], F32)
        self.cm = kb.sbuf(name + "_cm", [128, w], F32)
        self.ones = ones
        self.okey = okey
        self.ds = [DmaSem(kb, "nb_xs%d" % i) for i in range(2)]
        self.dcm = DmaSem(kb, "nb_cm")
        self.name = name
        self.i = 0


def emit_norm(kb, nb, cst, x_ap, cmask_ap, pieces, hT, hkey, ps_ap, pskey):
    name = nb.name
    for (c0, n, s, hc, masked) in pieces:
        slot = nb.i % 2
        nb.i += 1
        xn = nb.xn[slot]
        xk = "%s_xn%d" % (name, slot)
        kb.dma("sp", nb.ds[slot], xn[:, :, 0:n], x_ap[:, :, c0:c0 + n], writes=[xk])
        if masked:
            kb.dma("sp", nb.dcm, nb.cm[:, 0:n], cmask_ap[:, c0:c0 + n], writes=[name + "_cm"])
        kb.op("act", lambda a: a.activation(out=nb.sq[:, :, 0:n], in_=xn[:, :, 0:n], func=AF.Square),
              reads=[xk], writes=[name + "_sq"])
        for k in range(KD):
            kb.op("pe", lambda t: t.matmul(ps_ap[:, 0:n], lhsT=nb.ones[:], rhs=nb.sq[:, k, 0:n],
                                           start=(k == 0), stop=(k == KD - 1)),
                  reads=[name + "_sq", nb.okey], writes=[pskey], inc=(k == KD - 1))
        kb.op("act", lambda a: a.activation(out=nb.rs[:, 0:n], in_=ps_ap[:, 0:n], func=AF.Sqrt,
                                            scale=1.0 / D, bias=EPS),
              reads=[pskey], writes=[name + "_rs"])
        kb.op("dve", lambda v: v.reciprocal(out=nb.rs[:, 0:n], in_=nb.rs[:, 0:n]),
              reads=[name + "_rs"], writes=[name + "_rs"])
        if masked:
            pass
        kb.op("dve", lambda v: v.tensor_tensor(out=xn[:, :, 0:n], in0=xn[:, :, 0:n],
                                               in1=nb.rs[:, 0:n].unsqueeze(1).broadcast_to([128, KD, n]),
                                               op=ALU.mult),
              reads=[xk, name + "_rs"], writes=[xk])
        kb.op("dve", lambda v: v.tensor_tensor(out=xn[:, :, 0:n], in0=xn[:, :, 0:n],
                                               in1=cst.A[:, s, :].unsqueeze(2).broadcast_to([128, KD, n]),
                                               op=ALU.mult),
              reads=[xk, cst.Akey(s)], writes=[xk])
        kb.op("dve", lambda v: v.tensor_tensor(out=hT[:, :, hc:hc + n], in0=xn[:, :, 0:n],
                                               in1=cst.mod[:, s, cst.o, :].unsqueeze(2).broadcast_to([128, KD, n]),
                                               op=ALU.add),
              reads=[xk, cst.modkey], writes=[hkey])
        if masked:
            kb.op("dve", lambda v: v.tensor_tensor(out=hT[:, :, hc:hc + n], in0=hT[:, :, hc:hc + n],
                                                   in1=nb.cm[:, 0:n].unsqueeze(1).broadcast_to([128, KD, n]),
                                                   op=ALU.mult),
                  reads=[hkey, name + "_cm"], writes=[hkey])


class OutProj:
    def __init__(self, kb, name, kmax):
        self.name = name
        self.wd = [kb.sbuf("%s_wd%d" % (name, i), [128, kmax, 128], BF16) for i in range(2)]
        self.wds = [DmaSem(kb, "op_wds%d" % i) for i in range(2)]
        self.xr = [kb.sbuf("%s_xr%d" % (name, i), [128, 512], F32) for i in range(2)]
        self.xrs = [DmaSem(kb, "op_xrs%d" % i) for i in range(2)]
        self.ot = [kb.sbuf("%s_ot%d" % (name, i), [128, 512], F32) for i in range(2)]
        self.ots = [DmaSem(kb, "op_ots%d" % i) for i in range(2)]
        self.wd_i = 0
        self.dn = 0


def emit_outproj(kb, o, cst, g, gkey, nk, wdn, groups, xT, yT, ps, banks):
    name = o.name

    def load_wd(dc, wd_i):
        slot = wd_i % 2
        kb.dma("pool", o.wds[slot], o.wd[slot][:, 0:nk, :], wdn[dc, :, 0:nk, :], writes=[(name, "wd", slot)])

    load_wd(0, o.wd_i)
    for dc in range(KD):
        if dc + 1 < KD:
            load_wd(dc + 1, o.wd_i + 1)
        slot = o.wd_i % 2
        wkey = (name, "wd", slot)
        wdt = o.wd[slot]
        for (g0, xc, oc, n, s) in groups:
            r = o.dn % 2
            o.dn += 1
            b = banks[r]
            xr, ot = o.xr[r], o.ot[r]
            kb.dma("sp", o.xrs[r], xr[:, 0:n], xT[:, dc, xc:xc + n], writes=[(name, "xr", r)])
            for j in range(nk):
                kb.op("pe", lambda t: t.matmul(ps[:, b, 0:n], lhsT=wdt[:, j, :], rhs=g[:, j, g0:g0 + n],
                                               start=(j == 0), stop=(j == nk - 1)),
                      reads=[wkey, gkey], writes=[("P", "ps", b)], inc=(j == nk - 1))
            kb.op("dve", lambda v: v.scalar_tensor_tensor(out=ot[:, 0:n], in0=ps[:, b, 0:n],
                                                          scalar=cst.Gap(s, dc), in1=xr[:, 0:n],
                                                          op0=ALU.mult, op1=ALU.add),
                  reads=[("P", "ps", b), (name, "xr", r), cst.modkey], writes=[(name, "ot", r)])
            kb.dma("sp", o.ots[r], yT[:, dc, oc:oc + n], ot[:, 0:n], reads=[(name, "ot", r)], store=True)
        o.wd_i += 1


def _conv_supers(a, b, with_ctx, cap):
    tot = (b - a) + (NCTX if with_ctx else 0)
    ns = -(-tot // cap)
    per = -(-tot // ns)
    supers = []
    c = a
    for si in range(ns):
        room = per if si < ns - 1 else cap
        tiles = []
        nl = min(room, b - c)
        if nl > 0:
            tiles += [("lat", c0, n) for (c0, n) in _split_even(c, c + nl, 510)]
            c += nl
            room -= nl
        supers.append(tiles)
    if with_ctx:
        used = sum(t[2] for t in supers[-1])
        if used + NCTX <= cap:
            supers[-1].append(("ctx", CTXC, NCTX))
        else:
            supers.append([("ctx", CTXC, NCTX)])
    assert c == b
    return supers


def _conv_plan(tiles):
    descs, spans, zero = [], [], []
    h = 0
    lat = [t for t in tiles if t[0] == "lat"]
    if lat:
        lo = lat[0][1] - 1
        hi = lat[-1][1] + lat[-1][2] + 1
        spans.append((lo, hi - lo, 0, h))
        for (_, c0, n) in lat:
            descs.append((c0, n, h + (c0 - 1 - lo), 0))
        h += hi - lo
    for (kind, c0, n) in tiles:
        if kind == "ctx":
            zero += [h, h + n + 1]
            spans.append((c0, n, 1, h + 1))
            descs.append((c0, n, h, 1))
            h += n + 2
    return descs, spans, zero, h


def _norm_pieces(spans, w):
    pieces = []
    for (c0, n, s, hc) in spans:
        for (p0, pn) in _cols_split(c0, c0 + n, w):
            masked = (s == 0) and (p0 < OWN0 or p0 + pn > OWN1)
            pieces.append((p0, pn, s, hc + (p0 - c0), masked))
    return pieces


def emit_ffn(kb, G, cst, src, dst, dst_off, a, b, with_ctx, wup, cwb, wdn):
    name = "f"
    ps = G.ps
    cw = kb.sbuf(name + "_cw", [128, 2, KF, 4], F32)
    kb.dma("sp", DmaSem(kb, "f_c2"), cw[:], cwb, writes=[name + "_cw"])
    nb = NormBufs(kb, name + "n", 64, G.ones, G.okey)
    CAP = 1024
    WMAX = CAP + 8
    hT = kb.sbuf(name + "_hT", [128, KD, WMAX], BF16)
    g = kb.sbuf(name + "_g", [128, KF, CAP], BF16)
    NWB = 2
    wu = [kb.sbuf("%s_wu%d" % (name, i), [128, 2, KD, 128], BF16) for i in range(NWB)]
    wus = [DmaSem(kb, "f_wus%d" % i) for i in range(NWB)]
    ya = [kb.sbuf("%s_ya%d" % (name, i), [128, 512], F32) for i in range(2)]
    yb = [kb.sbuf("%s_yb%d" % (name, i), [128, 512], F32) for i in range(2)]
    sg = [kb.sbuf("%s_sg%d" % (name, i), [128, 512], F32) for i in range(2)]
    opj = OutProj(kb, name + "o", KF)
    hkey, gkey, cwk = name + "_hT", name + "_g", name + "_cw"
    zb = ev = wu_i = 0
    for tiles in _conv_supers(a, b, with_ctx, CAP):
        descs, spans, zero, width = _conv_plan(tiles)
        assert width <= WMAX, width
        emit_norm(kb, nb, cst, src, G.cmask, _norm_pieces(spans, nb.w), hT, hkey, ps[:, 6, :], ("P", "ps", 6))
        for zc in zero:
            kb.op("dve", lambda v: v.memset(hT[:, :, zc:zc + 1], 0.0), writes=[hkey])
        gcol = {}
        gc = 0
        for (c0, n, hb, s) in descs:
            gcol[c0] = gc
            gc += n
        assert gc <= CAP

        def load_wu(j, i):
            slot = i % NWB
            kb.dma("pool", wus[slot], wu[slot][:], wup[j], writes=[(name, "wu", slot)])

        load_wu(0, wu_i)
        for j in range(KF):
            if j + 1 < KF:
                load_wu(j + 1, wu_i + 1)
            slot = wu_i % NWB
            wkey = (name, "wu", slot)
            for (c0, n, hb, s) in descs:
                banks = []
                for half in range(2):
                    bk = zb % 6
                    zb += 1
                    banks.append(bk)
                    for k in range(KD):
                        kb.op("pe", lambda t: t.matmul(ps[:, bk, 0:n + 2], lhsT=wu[slot][:, half, k, :],
                                                       rhs=hT[:, k, hb:hb + n + 2],
                                                       start=(k == 0), stop=(k == KD - 1)),
                              reads=[wkey, hkey], writes=[("P", "ps", bk)], inc=(k == KD - 1))
                e = ev % 2
                ev += 1
                bg, bu = banks
                kg, ku = ("P", "ps", bg), ("P", "ps", bu)
                kb.op("act", lambda a_: a_.activation(out=ya[e][:, 0:n], in_=ps[:, bg, 1:n + 1], func=AF.Identity,
                                                      scale=cw[:, 0, j, 1:2], bias=cw[:, 0, j, 3:4]),
                      reads=[kg, cwk], writes=[(name, "ya", e)])
                kb.op("act", lambda a_: a_.activation(out=yb[e][:, 0:n], in_=ps[:, bu, 1:n + 1], func=AF.Identity,
                                                      scale=cw[:, 1, j, 1:2], bias=cw[:, 1, j, 3:4]),
                      reads=[ku, cwk], writes=[(name, "yb", e)])
                for (tap, off) in ((0, 0), (2, 2)):
                    kb.op("dve", lambda v: v.scalar_tensor_tensor(
                        out=ya[e][:, 0:n], in0=ps[:, bg, off:off + n], scalar=cw[:, 0, j, tap:tap + 1],
                        in1=ya[e][:, 0:n], op0=ALU.mult, op1=ALU.add),
                        reads=[kg, cwk, (name, "ya", e)], writes=[(name, "ya", e)])
                    kb.op("dve", lambda v: v.scalar_tensor_tensor(
                        out=yb[e][:, 0:n], in0=ps[:, bu, off:off + n], scalar=cw[:, 1, j, tap:tap + 1],
                        in1=yb[e][:, 0:n], op0=ALU.mult, op1=ALU.add),
                        reads=[ku, cwk, (name, "yb", e)], writes=[(name, "yb", e)])
                kb.op("act", lambda a_: a_.activation(out=sg[e][:, 0:n], in_=ya[e][:, 0:n], func=AF.Silu),
                      reads=[(name, "ya", e)], writes=[(name, "sg", e)])
                g0 = gcol[c0]
                kb.op("dve", lambda v: v.tensor_tensor(out=g[:, j, g0:g0 + n], in0=yb[e][:, 0:n], in1=sg[e][:, 0:n],
                                                       op=ALU.mult),
                      reads=[(name, "yb", e), (name, "sg", e)], writes=[gkey])
            wu_i += 1
        groups = [(gcol[c0], c0, c0 - dst_off, n, s) for (c0, n, hb, s) in descs]
        emit_outproj(kb, opj, cst, g, gkey, KF, wdn, groups, src, dst, ps, (6, 7))


def emit_sc(kb, G, cst, src, dst, a, b, with_ctx, win, cwd, wout):
    name = "s"
    ps = G.ps
    cw = kb.sbuf(name + "_cw", [128, KD, 3], F32)
    kb.dma("sp", DmaSem(kb, "s_c2"), cw[:], cwd, writes=[name + "_cw"])
    nb = NormBufs(kb, name + "n", 64, G.ones, G.okey)
    CAP = 1024
    WMAX = CAP + 8
    hT = kb.sbuf(name + "_hT", [128, KD, WMAX], BF16)
    g = kb.sbuf(name + "_g", [128, KD, CAP], BF16)
    wu = [kb.sbuf("%s_wu%d" % (name, i), [128, 3, KD, 128], BF16) for i in range(2)]
    wus = [DmaSem(kb, "s_wus%d" % i) for i in range(2)]
    ux = [kb.sbuf("%s_ux%d" % (name, i), [128, 512], F32) for i in range(2)]
    uu = [kb.sbuf("%s_uu%d" % (name, i), [128, 512], F32) for i in range(2)]
    vv = [kb.sbuf("%s_vv%d" % (name, i), [128, 512], F32) for i in range(2)]
    opj = OutProj(kb, name + "o", KD)
    hkey, gkey, cwk = name + "_hT", name + "_g", name + "_cw"
    zb = ev = wi = 0
    for tiles in _conv_supers(a, b, with_ctx, CAP):
        descs, spans, zero, width = _conv_plan(tiles)
        assert width <= WMAX
        emit_norm(kb, nb, cst, src, G.cmask, _norm_pieces(spans, nb.w), hT, hkey, ps[:, 6, :], ("P", "ps", 6))
        for zc in zero:
            kb.op("dve", lambda v: v.memset(hT[:, :, zc:zc + 1], 0.0), writes=[hkey])
        gcol = {}
        gc = 0
        for (c0, n, hb, s) in descs:
            gcol[c0] = gc
            gc += n

        def load_w(j, i):
            slot = i % 2
            kb.dma("pool", wus[slot], wu[slot][:], win[j], writes=[(name, "wu", slot)])

        load_w(0, wi)
        for j in range(KD):
            if j + 1 < KD:
                load_w(j + 1, wi + 1)
            slot = wi % 2
            wkey = (name, "wu", slot)
            for (c0, n, hb, s) in descs:
                banks = []
                for t3 in range(3):
                    bk = zb % 6
                    zb += 1
                    banks.append(bk)
                    for k in range(KD):
                        kb.op("pe", lambda t: t.matmul(ps[:, bk, 0:n + 2], lhsT=wu[slot][:, t3, k, :],
                                                       rhs=hT[:, k, hb:hb + n + 2],
                                                       start=(k == 0), stop=(k == KD - 1)),
                              reads=[wkey, hkey], writes=[("P", "ps", bk)], inc=(k == KD - 1))
                e = ev % 2
                ev += 1
                bb, bc, bx = banks
                kb.op("act", lambda a_: a_.activation(out=ux[e][:, 0:n + 2], in_=ps[:, bx, 0:n + 2], func=AF.Copy),
                      reads=[("P", "ps", bx)], writes=[(name, "ux", e)])
                kb.op("dve", lambda v: v.tensor_tensor(out=uu[e][:, 0:n + 2], in0=ps[:, bc, 0:n + 2],
                                                       in1=ux[e][:, 0:n + 2], op=ALU.mult),
                      reads=[("P", "ps", bc), (name, "ux", e)], writes=[(name, "uu", e)])
                kb.op("act", lambda a_: a_.activation(out=vv[e][:, 0:n], in_=uu[e][:, 1:n + 1], func=AF.Identity,
                                                      scale=cw[:, j, 1:2]),
                      reads=[(name, "uu", e), cwk], writes=[(name, "vv", e)])
                for (tap, off) in ((0, 0), (2, 2)):
                    kb.op("dve", lambda v: v.scalar_tensor_tensor(
                        out=vv[e][:, 0:n], in0=uu[e][:, off:off + n], scalar=cw[:, j, tap:tap + 1],
                        in1=vv[e][:, 0:n], op0=ALU.mult, op1=ALU.add),
                        reads=[(name, "uu", e), cwk, (name, "vv", e)], writes=[(name, "vv", e)])
                g0 = gcol[c0]
                kb.op("dve", lambda v: v.tensor_tensor(out=g[:, j, g0:g0 + n], in0=ps[:, bb, 1:n + 1],
                                                       in1=vv[e][:, 0:n], op=ALU.mult),
                      reads=[("P", "ps", bb), (name, "vv", e)], writes=[gkey])
            wi += 1
        groups = [(gcol[c0], c0, c0, n, s) for (c0, n, hb, s) in descs]
        emit_outproj(kb, opj, cst, g, gkey, KD, wout, groups, src, dst, ps, (6, 7))


def emit_gm(kb, G, cst, src, dst, a, b, with_ctx, wu_d, wv_d, vg_d, wsT_d, bsb_d, wout):
    name = "m"
    ps = G.ps
    vg = kb.sbuf(name + "_vg", [128, KG], F32)
    wsT = kb.sbuf(name + "_wsT", [128, 16, 128], F32)
    bsb = kb.sbuf(name + "_bsb", [128, 16, 128], F32)
    kb.dma("sp", DmaSem(kb, "m_d1"), vg[:], vg_d, writes=[name + "_vg"])
    kb.dma("sp", DmaSem(kb, "m_d2"), wsT[:], wsT_d, writes=[name + "_wsT"])
    kb.dma("sp", DmaSem(kb, "m_d3"), bsb[:], bsb_d, writes=[name + "_bsb"])
    nb = NormBufs(kb, name + "n", 128, G.ones, G.okey)
    TS = 512
    hT = kb.sbuf(name + "_hT", [128, KD, TS], BF16)
    uT = kb.sbuf(name + "_uT", [128, KG, TS], BF16)
    vraw = [kb.sbuf("%s_vr%d" % (name, i), [128, GMW], BF16) for i in range(4)]
    sqj = kb.sbuf(name + "_sqj", [128, 512], BF16)
    ssp = kb.sbuf(name + "_ssp", [128, 4, 8], F32)
    rst = kb.sbuf(name + "_rst", [128, 4], F32)
    NWU = 3
    wu = [kb.sbuf("%s_wu%d" % (name, i), [128, KD, 128], BF16) for i in range(NWU)]
    wus = [DmaSem(kb, "m_wus%d" % i) for i in range(NWU)]
    wv = [kb.sbuf("%s_wv%d" % (name, i), [128, KD, 512], BF16) for i in range(2)]
    wvs = [DmaSem(kb, "m_wvs%d" % i) for i in range(2)]
    wss = [kb.sbuf("%s_wss%d" % (name, i), [128, 16, 128], BF16) for i in range(2)]
    tmp = [kb.sbuf("%s_tmp%d" % (name, i), [128, 4, 128], F32) for i in range(2)]
    opj = OutProj(kb, name + "o", KG)
    hkey, ukey = name + "_hT", name + "_uT"
    zb = wu_i = wv_i = sp_i = 0
    supers = [(c0, n, 0) for (c0, n) in _cols_split(a, b, TS)]
    if with_ctx:
        supers.append((CTXC, NCTX, 1))
    for (t0, T, sset) in supers:
        ntc = T // 128
        pieces = [(c0, n, sset, c0 - t0, False) for (c0, n) in _cols_split(t0, t0 + T, nb.w)]
        emit_norm(kb, nb, cst, src, G.cmask, pieces, hT, hkey, ps[:, 6, :], ("P", "ps", 6))

        def load_wu(j, i):
            slot = i % NWU
            kb.dma("pool", wus[slot], wu[slot][:], wu_d[j, :, 0], writes=[(name, "wu", slot)])

        load_wu(0, wu_i)
        load_wu(1, wu_i + 1)
        for j in range(KG):
            if j + 2 < KG:
                load_wu(j + 2, wu_i + 2)
            slot = wu_i % NWU
            bk = zb % 4
            zb += 1
            for k in range(KD):
                kb.op("pe", lambda t: t.matmul(ps[:, bk, 0:T], lhsT=wu[slot][:, k, :], rhs=hT[:, k, 0:T],
                                               start=(k == 0), stop=(k == KD - 1)),
                      reads=[(name, "wu", slot), hkey], writes=[("P", "ps", bk)], inc=(k == KD - 1))
            kb.op("act", lambda a_: a_.activation(out=uT[:, j, 0:T], in_=ps[:, bk, 0:T], func=AF.Gelu_apprx_tanh),
                  reads=[("P", "ps", bk)], writes=[ukey])
            wu_i += 1

        def load_wv(cg, i):
            slot = i % 2
            kb.dma("pool", wvs[slot], wv[slot][:], wv_d[cg], writes=[(name, "wv", slot)])

        load_wv(0, wv_i)
        for cg in range(8):
            if cg + 1 < 8:
                load_wv(cg + 1, wv_i + 1)
            slot = wv_i % 2
            for tc in range(ntc):
                bk = zb % 4
                zb += 1
                for k in range(KD):
                    kb.op("pe", lambda t: t.matmul(ps[:, bk, :], lhsT=hT[:, k, tc * 128:(tc + 1) * 128],
                                                   rhs=wv[slot][:, k, :], start=(k == 0), stop=(k == KD - 1)),
                          reads=[(name, "wv", slot), hkey], writes=[("P", "ps", bk)], inc=(k == KD - 1))
                vslice = vraw[tc][:, cg * 512:(cg + 1) * 512]
                kb.op("act", lambda a_: a_.activation(out=vslice, in_=ps[:, bk, :], func=AF.Gelu_apprx_tanh),
                      reads=[("P", "ps", bk)], writes=[(name, "vr", tc)])
                kb.op("act", lambda a_: a_.activation(out=sqj[:], in_=vslice, func=AF.Square,
                                                      accum_out=ssp[:, tc, cg:cg + 1]),
                      reads=[(name, "vr", tc)], writes=[name + "_sqj", (name, "ssp", tc)])
            wv_i += 1

        for tc in range(ntc):
            kb.op("dve", lambda v: v.tensor_reduce(out=rst[:, tc:tc + 1], in_=ssp[:, tc, :], op=ALU.add,
                                                   axis=mybir.AxisListType.X),
                  reads=[(name, "ssp", tc)], writes=[(name, "rst", tc)])
            kb.op("act", lambda a_: a_.activation(out=rst[:, tc:tc + 1], in_=rst[:, tc:tc + 1], func=AF.Sqrt,
                                                  scale=1.0 / GMW, bias=EPS),
                  reads=[(name, "rst", tc)], writes=[(name, "rst", tc)])
            kb.op("dve", lambda v: v.reciprocal(out=rst[:, tc:tc + 1], in_=rst[:, tc:tc + 1]),
                  reads=[(name, "rst", tc)], writes=[(name, "rst", tc)])
            w = sp_i % 2
            sp_i += 1
            kb.op("dve", lambda v: v.tensor_scalar(out=wss[w][:], in0=wsT[:], scalar1=rst[:, tc:tc + 1], scalar2=None,
                                                   op0=ALU.mult),
                  reads=[name + "_wsT", (name, "rst", tc)], writes=[(name, "wss", w)])
            for q4 in range(8):
                bk = 4 + (q4 % 2)
                for cc in range(4):
                    c = q4 * 4 + cc
                    kb.op("pe", lambda t: t.matmul(ps[:, bk, cc * 128:(cc + 1) * 128],
                                                   lhsT=vraw[tc][:, c * 128:(c + 1) * 128],
                                                   rhs=wss[w][:, c // 2, :], start=True, stop=True),
                          reads=[(name, "vr", tc), (name, "wss", w)], writes=[("P", "ps", bk)], inc=(cc == 3))
                tm = tmp[q4 % 2]
                tk = (name, "tmp", q4 % 2)
                for cc in range(4):
                    c = q4 * 4 + cc
                    kb.op("dve", lambda v: v.scalar_tensor_tensor(
                        out=tm[:, cc, :], in0=ps[:, bk, cc * 128:(cc + 1) * 128], scalar=vg[:, c:c + 1],
                        in1=bsb[:, c // 2, :], op0=ALU.mult, op1=ALU.add),
                        reads=[("P", "ps", bk), name + "_vg", name + "_bsb"], writes=[tk])
                c0 = q4 * 4
                kb.op("dve", lambda v: v.tensor_tensor(out=uT[:, c0:c0 + 4, tc * 128:(tc + 1) * 128],
                                                       in0=uT[:, c0:c0 + 4, tc * 128:(tc + 1) * 128],
                                                       in1=tm[:], op=ALU.mult),
                      reads=[tk, ukey], writes=[ukey])
        emit_outproj(kb, opj, cst, uT, ukey, KG, wout, [(0, t0, t0, T, sset)], src, dst, ps, (6, 7))


def emit_na(kb, G, cst, src, dst, qb0, qb1, kv0, kvlen, ctx_q, special, wqkv, qkg_d, biasI, biasS, wo, oscr):
    name = "a"
    ps = G.ps
    psb = ps[:, 7, :].bitcast(BF16)
    ones, okey, idn = G.ones, G.okey, G.idn
    nkv = kvlen + NCTX
    nqb = qb1 - qb0
    nq = nqb * 128 + (NCTX if ctx_q else 0)
    nvb = nkv // 128
    qkg = kb.sbuf(name + "_qkg", [128, 2], F32)
    kb.dma("sp", DmaSem(kb, "a_d1"), qkg[:], qkg_d, writes=[name + "_qkg"])
    kb.op("dve", lambda v: v.tensor_scalar(out=qkg[:, 0:1], in0=qkg[:, 0:1], scalar1=float(128 ** -0.5), scalar2=None,
                                           op0=ALU.mult), reads=[name + "_qkg"], writes=[name + "_qkg"])
    nb = NormBufs(kb, name + "n", 64, ones, okey)
    hT = kb.sbuf(name + "_hT", [128, KD, nkv], BF16)
    hkey = name + "_hT"
    wq = [kb.sbuf("%s_wq%d" % (name, i), [128, 3, KD, 128], BF16) for i in range(2)]
    wqs = [DmaSem(kb, "a_wqs%d" % i) for i in range(2)]
    qT = kb.sbuf(name + "_qT", [128, nq], BF16)
    kT = kb.sbuf(name + "_kT", [128, nkv], BF16)
    vT = kb.sbuf(name + "_vT", [128, nkv], BF16)
    V = kb.sbuf(name + "_V", [128, nvb, 128], BF16)
    qraw = [kb.sbuf("%s_qraw%d" % (name, i), [128, 512], F32) for i in range(2)]
    sq = kb.sbuf(name + "_sqh", [128, 512], BF16)
    rs = kb.sbuf(name + "_rsh", [128, 512], F32)
    bI = [kb.sbuf("%s_bI%d" % (name, i), [128, 5, 128], F32) for i in range(2)]
    bIs = [DmaSem(kb, "a_bIs%d" % i) for i in range(2)]
    bS = [kb.sbuf("%s_bS%d" % (name, i), [128, 6, 128], F32) for i in range(2)]
    bSs = [DmaSem(kb, "a_bSs%d" % i) for i in range(2)]
    sb = [kb.sbuf("%s_sb%d" % (name, i), [128, 6, 128], F32) for i in range(2)]
    PT = [kb.sbuf("%s_PT%d" % (name, i), [128, 8, 128], BF16) for i in range(2)]
    rden = [kb.sbuf("%s_rden%d" % (name, i), [128, 128], F32) for i in range(2)]
    oTh = [kb.sbuf("%s_oTh%d" % (name, i), [128, nq], BF16) for i in range(2)]
    oTs = [DmaSem(kb, "a_oTs%d" % i) for i in range(2)]
    opj = OutProj(kb, name + "o", NH)
    P = ("P", "ps")

    pieces = [(c0, n, 0, c0 - kv0, False) for (c0, n) in _cols_split(kv0, kv0 + kvlen, nb.w)]
    pieces += [(c0, n, 1, kvlen + c0 - CTXC, False) for (c0, n) in _cols_split(CTXC, CTXC + NCTX, nb.w)]
    emit_norm(kb, nb, cst, src, G.cmask, pieces, hT, hkey, ps[:, 4, :], P + (4,))

    q_tiles = [(c0 - kv0, n, c0 - qb0 * 128) for (c0, n) in _cols_split(qb0 * 128, qb1 * 128, 512)]
    if ctx_q:
        q_tiles.append((kvlen, NCTX, nqb * 128))
    kv_tiles = [(c0, n, c0) for (c0, n) in _cols_split(0, nkv, 512)]
    zb = qr_i = sb_i = pv_i = bs_i = 0

    def load_w(h):
        slot = h % 2
        kb.dma("pool", wqs[slot], wq[slot][:], wqkv[h], writes=[(name, "wq", slot)])

    load_w(0)
    for h in range(NH):
        if h + 1 < NH:
            load_w(h + 1)
        hs = h % 2
        wkey = (name, "wq", hs)
        w = wq[hs]
        kb.dma("sp", bIs[hs], bI[hs][:], biasI[h], writes=[(name, "bI", hs)])
        ptiles = []
        for t3 in range(3):
            for (c0, n, dc0) in (q_tiles if t3 == 0 else kv_tiles):
                ptiles.append((t3, c0, n, dc0))
        pbank = {}

        def proj_mm(i):
            nonlocal zb
            (t3, c0, n, dc0) = ptiles[i]
            bk = zb % 4
            zb += 1
            pbank[i] = bk
            for k in range(KD):
                kb.op("pe", lambda t: t.matmul(ps[:, bk, 0:n], lhsT=w[:, t3, k, :], rhs=hT[:, k, c0:c0 + n],
                                               start=(k == 0), stop=(k == KD - 1)),
                      reads=[wkey, hkey], writes=[P + (bk,)], inc=(k == KD - 1))

        def proj_post(i):
            nonlocal qr_i
            (t3, c0, n, dc0) = ptiles[i]
            bk = pbank[i]
            if t3 == 2:
                kb.op("act", lambda a_: a_.activation(out=vT[:, c0:c0 + n], in_=ps[:, bk, 0:n], func=AF.Copy),
                      reads=[P + (bk,)], writes=[name + "_vT"])
                return
            r = qr_i % 2
            qr_i += 1
            kb.op("dve", lambda v: v.tensor_copy(out=qraw[r][:, 0:n], in_=ps[:, bk, 0:n]),
                  reads=[P + (bk,)], writes=[(name, "qraw", r)])
            kb.op("act", lambda a_: a_.activation(out=sq[:, 0:n], in_=qraw[r][:, 0:n], func=AF.Square),
                  reads=[(name, "qraw", r)], writes=[name + "_sqh"])
            kb.op("pe", lambda t: t.matmul(ps[:, 4, 0:n], lhsT=ones[:], rhs=sq[:, 0:n], start=True, stop=True),
                  reads=[name + "_sqh", okey], writes=[P + (4,)])
            kb.op("act", lambda a_: a_.activation(out=rs[:, 0:n], in_=ps[:, 4, 0:n], func=AF.Ln,
                                                  scale=1.0 / 128, bias=EPS),
                  reads=[P + (4,)], writes=[name + "_rsh"])
            kb.op("act", lambda a_: a_.activation(out=rs[:, 0:n], in_=rs[:, 0:n], func=AF.Exp, scale=-0.5),
                  reads=[name + "_rsh"], writes=[name + "_rsh"])
            dstt = qT if t3 == 0 else kT
            dkey = name + ("_qT" if t3 == 0 else "_kT")
            kb.op("dve", lambda v: v.scalar_tensor_tensor(out=dstt[:, dc0:dc0 + n], in0=qraw[r][:, 0:n],
                                                          scalar=qkg[:, t3:t3 + 1], in1=rs[:, 0:n],
                                                          op0=ALU.mult, op1=ALU.mult),
                  reads=[(name, "qraw", r), name + "_qkg", name + "_rsh"], writes=[dkey])

        proj_mm(0)
        for i in range(len(ptiles)):
            if i + 1 < len(ptiles):
                proj_mm(i + 1)
            proj_post(i)
        for b0 in range(0, nvb, 8):
            nbk = min(8, nvb - b0)
            for i in range(nbk):
                kb.op("pe", lambda t: t.transpose(psb[:, i * 128:(i + 1) * 128],
                                                  vT[:, (b0 + i) * 128:(b0 + i + 1) * 128], idn[:]),
                      reads=[name + "_vT", G.ikey], writes=[P + (7,)], inc=(i == nbk - 1))
            kb.op("dve", lambda v: v.tensor_copy(out=V[:, b0:b0 + nbk, :],
                                                 in_=psb[:, 0:nbk * 128].rearrange("p (b c) -> p b c", c=128)),
                  reads=[P + (7,)], writes=[name + "_V"])
        oT = oTh[hs]
        okh = (name, "oTh", hs)
        qblocks = []
        for bq in range(qb0, qb1):
            st, npair, sidx = special.get(bq, (-4, 5, -1))
            qblocks.append(("lat", bq, st, npair, sidx, (bq - qb0) * 128))
        if ctx_q:
            qblocks += [("ctx", 0, 0, 0, -1, nqb * 128), ("ctx", 0, 0, 0, -1, nqb * 128 + 128)]
        bst = {}

        def stage_a(i):
            nonlocal sb_i, bs_i
            (kind, bq, start, npair, sidx, qc0) = qblocks[i]
            s2 = sb_i % 2
            sb_i += 1
            chunks = []
            if kind == "lat":
                for p in range(npair):
                    er = 2 * bq + start + 2 * p
                    kc = er * 64 - kv0
                    assert 0 <= kc and kc + 128 <= kvlen, (bq, er, kv0, kvlen)
                    chunks.append((kc, kc // 128, p))
            for cc in range(2):
                chunks.append((kvlen + cc * 128, kvlen // 128 + cc, 6 + cc))
            if sidx >= 0:
                bsl = bs_i % 2
                bs_i += 1
                kb.dma("sp", bSs[bsl], bS[bsl][:], biasS[sidx, h], writes=[(name, "bS", bsl)])
                bias_ap, bias_key = bS[bsl], (name, "bS", bsl)
            else:
                bias_ap, bias_key = bI[hs], (name, "bI", hs)
            bkA, bkB = 2 * s2, 2 * s2 + 1
            for (kc, vb, slot) in chunks:
                bk = bkA if slot < 4 else bkB
                col = (slot % 4) * 128
                kb.op("pe", lambda t: t.matmul(ps[:, bk, col:col + 128], lhsT=kT[:, kc:kc + 128],
                                               rhs=qT[:, qc0:qc0 + 128], start=True, stop=True),
                      reads=[name + "_kT", name + "_qT"], writes=[P + (bk,)], inc=(slot == 7))
            bst[i] = (s2, chunks, bias_ap, bias_key)

        def stage_b(i):
            (kind, bq, start, npair, sidx, qc0) = qblocks[i]
            (s2, chunks, bias_ap, bias_key) = bst[i]
            bkA, bkB = 2 * s2, 2 * s2 + 1
            if kind == "lat":
                n1 = min(npair, 4)
                kb.op("dve", lambda v: v.tensor_tensor(out=sb[s2][:, 0:n1, :],
                                                       in0=ps[:, bkA, 0:n1 * 128].rearrange("p (c q) -> p c q", q=128),
                                                       in1=bias_ap[:, 0:n1, :], op=ALU.add),
                      reads=[P + (bkA,), bias_key], writes=[(name, "sb", s2)])
                n2 = npair - 4
                kb.op("dve", lambda v: v.tensor_tensor(out=sb[s2][:, 4:4 + n2, :],
                                                       in0=ps[:, bkB, 0:n2 * 128].rearrange("p (c q) -> p c q", q=128),
                                                       in1=bias_ap[:, 4:4 + n2, :], op=ALU.add),
                      reads=[P + (bkB,), bias_key], writes=[(name, "sb", s2)])
                kb.op("act", lambda a_: a_.activation(out=PT[s2][:, 0:npair, :], in_=sb[s2][:, 0:npair, :],
                                                      func=AF.Exp),
                      reads=[(name, "sb", s2)], writes=[(name, "PT", s2)])
            kb.op("act", lambda a_: a_.activation(out=PT[s2][:, 6:8, :],
                                                  in_=ps[:, bkB, 256:512].rearrange("p (c q) -> p c q", q=128),
                                                  func=AF.Exp),
                  reads=[P + (bkB,)], writes=[(name, "PT", s2)])

        def stage_c(i):
            nonlocal pv_i
            (kind, bq, start, npair, sidx, qc0) = qblocks[i]
            (s2, chunks, bias_ap, bias_key) = bst[i]
            pv = pv_i % 2
            pv_i += 1
            pvb = 5 + pv
            nch = len(chunks)
            for ii, (kc, vb, slot) in enumerate(chunks):
                kb.op("pe", lambda t: t.matmul(ps[:, pvb, 0:128], lhsT=V[:, vb, :], rhs=PT[s2][:, slot, :],
                                               start=(ii == 0), stop=(ii == nch - 1)),
                      reads=[name + "_V", (name, "PT", s2)], writes=[P + (pvb,)], inc=False)
            for ii, (kc, vb, slot) in enumerate(chunks):
                kb.op("pe", lambda t: t.matmul(ps[:, pvb, 128:256], lhsT=ones[:], rhs=PT[s2][:, slot, :],
                                               start=(ii == 0), stop=(ii == nch - 1)),
                      reads=[okey, (name, "PT", s2)], writes=[P + (pvb,)], inc=(ii == nch - 1))
            kb.op("act", lambda a_: a_.activation(out=rden[pv][:], in_=ps[:, pvb, 128:256], func=AF.Ln),
                  reads=[P + (pvb,)], writes=[(name, "rden", pv)])
            kb.op("act", lambda a_: a_.activation(out=rden[pv][:], in_=rden[pv][:], func=AF.Exp, scale=-1.0),
                  reads=[(name, "rden", pv)], writes=[(name, "rden", pv)])
            kb.op("dve", lambda v: v.tensor_tensor(out=oT[:, qc0:qc0 + 128], in0=ps[:, pvb, 0:128],
                                                   in1=rden[pv][:], op=ALU.mult),
                  reads=[P + (pvb,), (name, "rden", pv)], writes=[okh])

        nblk = len(qblocks)
        for i in range(nblk + 1):
            if i < nblk:
                stage_a(i)
                stage_b(i)
            if i >= 1:
                stage_c(i - 1)
        kb.dma("sp", oTs[hs], oscr[h, :, 0:nq], oT[:, 0:nq], reads=[okh], writes=[name + "_oscr"], partial=True)
    lds = DmaSem(kb, "a_old")
    for h in range(NH):
        kb.dma("sp", lds, hT[:, h, 0:nq], oscr[h, :, 0:nq], reads=[name + "_oscr"], writes=[hkey], partial=(h > 0))
    groups = [(c0 - qb0 * 128, c0, c0, n, 0) for (c0, n) in _cols_split(qb0 * 128, qb1 * 128, 512)]
    if ctx_q:
        groups.append((nqb * 128, CTXC, CTXC, NCTX, 1))
    emit_outproj(kb, opj, cst, hT, hkey, NH, wo, groups, src, dst, ps, (5, 6))


def emit_ada(kb, G, l, modt, modkey, wa):
    ps = G.ps
    w = [kb.sbuf("ada_w%d" % i, [128, KD, 512], BF16) for i in range(2)]
    ws = [DmaSem(kb, "ada_ws%d" % i) for i in range(2)]
    kb.dma("pool", ws[0], w[0][:], wa[l, 0], writes=["ada_w0"])
    for gi in range(ADA_NG):
        if gi + 1 < ADA_NG:
            kb.dma("pool", ws[(gi + 1) % 2], w[(gi + 1) % 2][:], wa[l, gi + 1], writes=["ada_w%d" % ((gi + 1) % 2)])
        wt = w[gi % 2]
        for c4 in range(4):
            ch = gi * 4 + c4
            bk = ch % 4
            for k in range(KD):
                kb.op("pe", lambda t: t.matmul(ps[:, bk, 0:2], lhsT=wt[:, k, c4 * 128:(c4 + 1) * 128],
                                               rhs=G.sg[:, k, :], start=(k == 0), stop=(k == KD - 1)),
                      reads=["ada_w%d" % (gi % 2), "G_sg"], writes=[("P", "ps", bk)], inc=(k == KD - 1))
            kb.op("act", lambda a_: a_.activation(out=modt[:, :, ch // 16, ch % 16], in_=ps[:, bk, 0:2],
                                                  func=AF.Identity, bias=G.bada[:, l, ch:ch + 1]),
                  reads=[("P", "ps", bk), "G_bada"], writes=[modkey])


class Globals:
    pass


R_FFN0 = (384, 3200)
R_GM1 = (384, 3200)
R_FFN1 = (446, 3138)
R_SC2 = (447, 3137)
R_FFN2 = (448, 3136)
NA_SPECIAL = {6: (-4, 6, 0), 7: (-4, 5, 1), 20: (-4, 5, 2), 21: (-6, 6, 3)}
NA0_PASSES = [(2, 14, 0, 2048, True), (14, 26, 1536, 2048, False)]
NA3_PASSES = [(5, 14, 384, 1664, False), (14, 23, 1536, 1664, False)]


def build_fused(nc, n_phases=99, dbg_out=None):
    def din(name, shape, dt=F32):
        return nc.dram_tensor(name, shape, dt, kind="ExternalInput").ap()
    xin = din("xin", [128, KD, NCA])
    cmask = din("cmask", [128, NCA])
    cond = din("cond", [128, KD, 2])
    wa = din("wa", [4, ADA_NG, 128, KD, 512])
    ba = din("ba", [128, 4, 96])
    ngm = din("ngm", [4, 128, 16])
    ngf = din("ngf", [4, 128, 16])
    na_wqkv = din("na_wqkv", [2, NH, 128, 3, KD, 128])
    na_qkg = din("na_qkg", [2, 128, 2])
    na_bI = din("na_bI", [2, NH, 128, 5, 128])
    na_bS = din("na_bS", [2, 4, NH, 128, 6, 128])
    na_wo = din("na_wo", [2, KD, 128, NH, 128])
    gm_wu = din("gm_wu", [KG, 128, 1, KD, 128])
    gm_wv = din("gm_wv", [8, 128, KD, 512])
    gm_vg = din("gm_vg", [128, KG])
    gm_wsT = din("gm_wsT", [128, 16, 128])
    gm_bsb = din("gm_bsb", [128, 16, 128])
    gm_wout = din("gm_wout", [KD, 128, KG, 128])
    sc_win = din("sc_win", [KD, 128, 3, KD, 128])
    sc_cw = din("sc_cw", [128, KD, 3])
    sc_wout = din("sc_wout", [KD, 128, KD, 128])
    f_wup = din("f_wup", [4, KF, 128, 2, KD, 128])
    f_cwb = din("f_cwb", [4, 128, 2, KF, 4])
    f_wdn = din("f_wdn", [4, KD, 128, KF, 128])
    yT = nc.dram_tensor("yT", [128, KD, NLAT], F32, kind="ExternalOutput").ap()
    kind = "ExternalOutput" if dbg_out else "Internal"
    XA = nc.dram_tensor("XA", [128, KD, NCA], F32, kind=kind).ap()
    XB = nc.dram_tensor("XB", [128, KD, NCA], F32, kind=kind).ap()
    oscr = nc.dram_tensor("oscr", [NH, 128, 12 * 128 + NCTX], BF16, kind="Internal").ap()
    with ExitStack() as es:
        kb = KB(nc, es)
        G = Globals()
        G.ps = kb.psum("ps", [128, 8, 512], F32)
        G.cmask = cmask
        G.ones = kb.sbuf("G_ones", [128, 128], BF16)
        G.okey = "G_ones"
        idf = kb.sbuf("G_idf", [128, 128], F32)
        G.idn = kb.sbuf("G_idn", [128, 128], BF16)
        G.ikey = "G_idn"
        cs = kb.sbuf("G_cs", [128, KD, 2], F32)
        G.sg = kb.sbuf("G_sg", [128, KD, 2], BF16)
        G.bada = kb.sbuf("G_bada", [128, 4, 96], F32)
        modt = [kb.sbuf("G_mod%d" % l, [128, 2, 6, 16], F32) for l in range(4)]
        kb.op("dve", lambda v: v.memset(G.ones[:], 1.0), writes=["G_ones"])
        kb.op("pool", lambda g_: g_.memset(idf[:], 1.0), writes=["G_idf"])
        kb.op("pool", lambda g_: g_.affine_select(out=idf[:], in_=idf[:], pattern=[[-1, 128]], compare_op=ALU.is_equal,
                                                 fill=0.0, base=0, channel_multiplier=1),
              reads=["G_idf"], writes=["G_idf"])
        kb.op("dve", lambda v: v.tensor_copy(out=G.idn[:], in_=idf[:]), reads=["G_idf"], writes=["G_idn"])
        kb.dma("sp", DmaSem(kb, "g_d1"), cs[:], cond, writes=["G_cs"])
        kb.dma("sp", DmaSem(kb, "g_d2"), G.bada[:], ba, writes=["G_bada"])
        kb.op("act", lambda a_: a_.activation(out=G.sg[:], in_=cs[:], func=AF.Silu), reads=["G_cs"], writes=["G_sg"])
        kb.barrier()
        ph = [0]

        def phase(fn):
            ph[0] += 1
            if ph[0] > n_phases:
                return
            with ExitStack() as pes:
                kb.cur_es = pes
                fn()
                kb.barrier()
            kb.cur_es = es

        def mixer_consts(l):
            return make_consts(kb, "cm%d" % l, modt[l], "G_mod%d" % l, ngm[l], 0)

        def ffn_consts(l):
            return make_consts(kb, "cf%d" % l, modt[l], "G_mod%d" % l, ngf[l], 1)

        phase(lambda: emit_ada(kb, G, 0, modt[0], "G_mod0", wa))
        for (qb0, qb1, kv0, kvlen, cq) in NA0_PASSES:
            phase(lambda: emit_na(kb, G, mixer_consts(0), xin, XA, qb0, qb1, kv0, kvlen, cq, NA_SPECIAL,
                                  na_wqkv[0], na_qkg[0], na_bI[0], na_bS[0], na_wo[0], oscr))
        phase(lambda: emit_ffn(kb, G, ffn_consts(0), XA, XB, 0, R_FFN0[0], R_FFN0[1], True,
                               f_wup[0], f_cwb[0], f_wdn[0]))
        phase(lambda: emit_ada(kb, G, 1, modt[1], "G_mod1", wa))
        phase(lambda: emit_gm(kb, G, mixer_consts(1), XB, XA, R_GM1[0], R_GM1[1], True,
                              gm_wu, gm_wv, gm_vg, gm_wsT, gm_bsb, gm_wout))
        phase(lambda: emit_ffn(kb, G, ffn_consts(1), XA, XB, 0, R_FFN1[0], R_FFN1[1], True,
                               f_wup[1], f_cwb[1], f_wdn[1]))
        phase(lambda: emit_ada(kb, G, 2, modt[2], "G_mod2", wa))
        phase(lambda: emit_sc(kb, G, mixer_consts(2), XB, XA, R_SC2[0], R_SC2[1], True, sc_win, sc_cw, sc_wout))
        phase(lambda: emit_ffn(kb, G, ffn_consts(2), XA, XB, 0, R_FFN2[0], R_FFN2[1], True,
                               f_wup[2], f_cwb[2], f_wdn[2]))
        phase(lambda: emit_ada(kb, G, 3, modt[3], "G_mod3", wa))
        for (qb0, qb1, kv0, kvlen, cq) in NA3_PASSES:
            phase(lambda: emit_na(kb, G, mixer_consts(3), XB, XA, qb0, qb1, kv0, kvlen, cq, NA_SPECIAL,
                                  na_wqkv[1], na_qkg[1], na_bI[1], na_bS[1], na_wo[1], oscr))
        phase(lambda: emit_ffn(kb, G, ffn_consts(3), XA, yT, OWN0, OWN0, OWN1, False,
                               f_wup[3], f_cwb[3], f_wdn[3]))
        kb.finish()
        print("fused program: ins=%d waits=%d sems=%d" % (kb.n_ins, kb.n_wait, kb.nsem))
    return nc


def fm(v):
    v = np.asarray(v, np.float32)
    k = v.shape[-1] // 128
    r = v.reshape(v.shape[:-1] + (k, 128))
    return np.ascontiguousarray(np.moveaxis(r, -1, 0))


def tile_wup(w_up):
    w = w_up.reshape(KD, 128, 2, KF, 128)
    return np.ascontiguousarray(w.transpose(3, 1, 2, 0, 4))


def tile_wdn(w_down):
    nk = w_down.shape[0] // 128
    w = w_down.reshape(nk, 128, KD, 128)
    return np.ascontiguousarray(w.transpose(2, 1, 0, 3))


def tile_win(w_in, nt):
    nj = w_in.shape[1] // (nt * 128)
    w = w_in.reshape(KD, 128, nt, nj, 128)
    return np.ascontiguousarray(w.transpose(3, 1, 2, 0, 4))


def _unused_tile_ada(w_ada_all, b_ada_all, core):
    L = w_ada_all.shape[0]
    per = L * 12288 // NCORES
    cpl = 12288 // per
    l, part = core // cpl, core % cpl
    w = w_ada_all[l][:, part * per:(part + 1) * per]
    w = w.reshape(KD, 128, ADA_G, 512)
    wt = np.ascontiguousarray(w.transpose(2, 1, 0, 3))
    b = b_ada_all[l][part * per:(part + 1) * per].reshape(ADA_CH, 128)
    return wt, np.ascontiguousarray(b.T)


def tile_cwb(conv_w, conv_b):
    a = np.concatenate([conv_w, conv_b[None]], 0)
    a = a.reshape(4, 2, KF, 128)
    return np.ascontiguousarray(a.transpose(3, 1, 2, 0))


def from_out(yT):
    return np.ascontiguousarray(yT.transpose(2, 1, 0).reshape(yT.shape[2], D))


def gm_weights(w_in, v_g, w_s, b_s, w_out):
    wu = tile_win(w_in[:, :GMW], 1)
    wv = w_in[:, GMW:].reshape(KD, 128, 8, 512).transpose(2, 1, 0, 3)
    return {
        "wu": wu,
        "wv": np.ascontiguousarray(wv),
        "vg": fm(v_g),
        "wsT": np.ascontiguousarray(w_s.transpose(2, 0, 1)),
        "bsb": np.ascontiguousarray(np.broadcast_to(b_s[None], (128, 16, 128))),
        "wout": tile_wdn(w_out),
    }


def _na_block_bias(rpb, r0, start, npair, rows=128):
    qq = np.arange(128)
    q_row, q_col = r0 + qq // 64, qq % 64
    kk = np.arange(128)
    r_start = np.clip(q_row - 4, 0, rows - 8)
    c_start = np.clip(q_col - 8, 0, 64 - 16)
    out = np.full((NH, 128, 6, 128), -30000.0, np.float32)
    for p in range(npair):
        k_row = (r0 + start + 2 * p + kk // 64)[:, None]
        k_col = (kk % 64)[:, None]
        ok = ((k_row >= 0) & (k_row < rows) & (k_row >= r_start[None]) & (k_row < r_start[None] + 8)
              & (k_col >= c_start[None]) & (k_col < c_start[None] + 16))
        dr = np.clip(k_row - q_row[None], -7, 7) + 7
        dc = np.clip(k_col - q_col[None], -15, 15) + 15
        out[:, :, p, :] = np.where(ok[None], rpb[:, dr, dc], np.float32(-30000.0))
    return out


def na_bias(rpb, core):
    R0 = (core % 4) * 32
    bI = _na_block_bias(rpb, 64, -4, 5)[:, :, 0:5, :]
    bS = np.stack([
        _na_block_bias(rpb, R0 + 0, -4, 6),
        _na_block_bias(rpb, R0 + 2, -4, 5),
        _na_block_bias(rpb, R0 + 28, -4, 5),
        _na_block_bias(rpb, R0 + 30, -6, 6),
    ], 0)
    return np.ascontiguousarray(bI), np.ascontiguousarray(bS)


def ext_cols(xfull, ctxfull, core):
    b, q = core // 4, core % 4
    t0 = q * NLAT - OWN0
    ext = np.zeros((NCA, D), np.float32)
    lo, hi = max(t0, 0), min(t0 + NE, xfull.shape[1])
    ext[lo - t0:hi - t0] = xfull[b, lo:hi]
    ext[CTXC:] = ctxfull[b]
    return np.ascontiguousarray(ext.reshape(NCA, KD, 128).transpose(2, 1, 0))


def cmask_for(core, seq):
    q = core % 4
    t0 = q * NLAT - OWN0
    g = t0 + np.arange(NE)
    m = np.ones((NCA,), np.float32)
    m[:NE] = ((g >= 0) & (g < seq)).astype(np.float32)
    return np.ascontiguousarray(np.broadcast_to(m[None], (128, NCA)))


_PROG = {}


def kernel(x, c, ctx, c_ctx, norm_mix_g, norm_ffn_g, w_ada, b_ada,
           na_w_qkv, na_q_g, na_k_g, na_rpb, na_w_o,
           gm_w_in, gm_v_g, gm_w_s, gm_b_s, gm_w_out,
           sc_w_in, sc_conv_w, sc_w_out,
           ffn_w_up, ffn_conv_w, ffn_conv_b, ffn_w_down):
    f32 = lambda a: np.ascontiguousarray(np.asarray(a, np.float32))
    x, c, ctx, c_ctx = f32(x), f32(c), f32(ctx), f32(c_ctx)
    w_ada, b_ada = f32(w_ada), f32(b_ada)
    if "nc" not in _PROG:
        nc = bass.Bass("TRN2", target_bir_lowering=False)
        build_fused(nc)
        _PROG["nc"] = nc
    nc = _PROG["nc"]
    shared = {
        "wa": np.ascontiguousarray(w_ada.reshape(4, KD, 128, ADA_NG, 512).transpose(0, 3, 2, 1, 4)),
        "ba": fm(b_ada),
        "ngm": np.ascontiguousarray(fm(f32(norm_mix_g)).transpose(1, 0, 2)),
        "ngf": np.ascontiguousarray(fm(f32(norm_ffn_g)).transpose(1, 0, 2)),
        "na_wqkv": np.stack([tile_win(f32(na_w_qkv[j]), 3) for j in range(2)], 0),
        "na_qkg": np.ascontiguousarray(np.stack([f32(na_q_g), f32(na_k_g)], -1)),
        "na_wo": np.stack([tile_wdn(f32(na_w_o[j])) for j in range(2)], 0),
        "sc_win": tile_win(f32(sc_w_in[0]), 3),
        "sc_cw": np.ascontiguousarray(fm(f32(sc_conv_w[0])).transpose(0, 2, 1)),
        "sc_wout": tile_wdn(f32(sc_w_out[0])),
        "f_wup": np.stack([tile_wup(f32(ffn_w_up[i])) for i in range(4)], 0),
        "f_cwb": np.stack([tile_cwb(f32(ffn_conv_w[i]), f32(ffn_conv_b[i])) for i in range(4)], 0),
        "f_wdn": np.stack([tile_wdn(f32(ffn_w_down[i])) for i in range(4)], 0),
    }
    gw = gm_weights(f32(gm_w_in[0]), f32(gm_v_g[0]), f32(gm_w_s[0]), f32(gm_b_s[0]), f32(gm_w_out[0]))
    for k, v in gw.items():
        shared["gm_" + k] = v
    rpb = f32(na_rpb)
    in_maps = []
    for core in range(NCORES):
        b = core // 4
        cond = np.stack([c[b], c_ctx], 0)
        bias = [na_bias(rpb[j], core) for j in range(2)]
        d = dict(shared)
        d["xin"] = ext_cols(x, ctx, core)
        d["cmask"] = cmask_for(core, x.shape[1])
        d["cond"] = np.ascontiguousarray(fm(cond).transpose(0, 2, 1))
        d["na_bI"] = np.stack([bias[0][0], bias[1][0]], 0)
        d["na_bS"] = np.stack([bias[0][1], bias[1][1]], 0)
        in_maps.append(d)
    res = run_bass_kernel_spmd(nc, in_maps, core_ids=list(range(NCORES)))
    out = np.empty_like(x)
    for core in range(NCORES):
        b, q = core // 4, core % 4
        out[b, q * NLAT:(q + 1) * NLAT] = from_out(np.asarray(res.results[core]["yT"]))
    return out
```
